# Optimizing a Trainium2 kernel written in Bass

```python
import jax, jax.numpy as jnp
from jax import lax
import numpy as np

D_MODEL = 2048
BATCH = 1
SEQ = 8192
DEPTH = 1
DEC_BATCH = 16
DEC_SEQ = 64
PAST_LEN = 2048

CHUNK = 64
POOL_WINDOWS = (2, 4, 8, 16)
POOL_GROUPS = 4
POOL_WIDTH = D_MODEL // 2
POOL_GROUP_WIDTH = POOL_WIDTH // POOL_GROUPS
POOL_LAG = 15
N_RET_HEADS = 8
RET_DK = D_MODEL // 2 // N_RET_HEADS
RET_DV = D_MODEL // N_RET_HEADS
RET_QK_WIDTH = N_RET_HEADS * RET_DK
RET_V_WIDTH = N_RET_HEADS * RET_DV
N_BRANCHES = 2
IN_WIDTH = POOL_WIDTH + 2 * RET_QK_WIDTH + 2 * RET_V_WIDTH + N_BRANCHES * D_MODEL
D_FF = 4 * D_MODEL
ROPE_BASE = 10000.0
EPS = 1e-6

kernel_name = "pool_retention_hybrid_stream_step"


def rms_norm(x, g):
    xf = x.astype(jnp.float32)
    y = xf * lax.rsqrt(jnp.mean(xf * xf, axis=-1, keepdims=True) + EPS)
    return (y * g.astype(jnp.float32)).astype(x.dtype)


def split_in(z):
    sizes = (POOL_WIDTH, RET_QK_WIDTH, RET_QK_WIDTH, RET_V_WIDTH, RET_V_WIDTH, D_MODEL)
    offs = [int(o) for o in np.cumsum(sizes)]
    return jnp.split(z, offs, axis=-1)


def pool_mix(u, left, pos0, w_pool, pool_scale):
    B, T, _ = u.shape
    ext = jnp.concatenate([left, u], axis=1).astype(jnp.float32)
    cs = jnp.concatenate([jnp.zeros((B, 1, POOL_WIDTH), jnp.float32), jnp.cumsum(ext, axis=1)], axis=1)
    end = cs[:, POOL_LAG + 1:]
    pos = pos0 + jnp.arange(T)
    means = []
    for g, w in enumerate(POOL_WINDOWS):
        sl = slice(g * POOL_GROUP_WIDTH, (g + 1) * POOL_GROUP_WIDTH)
        start = cs[:, POOL_LAG + 1 - w: POOL_LAG + 1 - w + T, sl]
        cnt = jnp.minimum(pos + 1, w).astype(jnp.float32)[None, :, None]
        means.append((end[..., sl] - start) / cnt)
    mean = jnp.stack(means, axis=2)
    diff = mean - u.astype(jnp.float32).reshape(B, T, POOL_GROUPS, POOL_GROUP_WIDTH)
    y = jnp.einsum('btgc,gcd->btgd', diff.astype(u.dtype), w_pool).reshape(B, T, POOL_WIDTH)
    new_left = ext[:, -POOL_LAG:].astype(u.dtype)
    return y * pool_scale, new_left


def rotate(x, pos):
    inv = 1.0 / (ROPE_BASE ** jnp.linspace(0.0, 1.0, RET_DK // 2, dtype=jnp.float32))
    ang = pos.astype(jnp.float32)[:, None] * inv[None, :]
    cos = jnp.cos(ang)[None, :, None, :]
    sin = jnp.sin(ang)[None, :, None, :]
    x1, x2 = x[..., 0::2], x[..., 1::2]
    return jnp.stack([x1 * cos - x2 * sin, x1 * sin + x2 * cos], axis=-1).reshape(x.shape)


def retention_chunk(S, q, k, v, log_gamma):
    T = q.shape[1]
    idx = jnp.arange(T, dtype=jnp.float32)
    rel = idx[:, None] - idx[None, :]
    dmask = jnp.where(rel[None] >= 0, jnp.exp(jnp.maximum(rel, 0.0)[None] * log_gamma[:, None, None]), 0.0)
    scores = jnp.einsum('bihd,bjhd->bhij', q, k) * dmask[None]
    inner = jnp.einsum('bhij,bjhe->bihe', scores, v)
    cross = jnp.einsum('bihd,bhde->bihe', q, S) * jnp.exp((idx + 1.0)[:, None] * log_gamma[None, :])[None, :, :, None]
    kdec = k * jnp.exp((T - 1.0 - idx)[:, None] * log_gamma[None, :])[None, :, :, None]
    S_new = jnp.exp(T * log_gamma)[None, :, None, None] * S + jnp.einsum('bjhd,bjhe->bhde', kdec, v)
    return S_new, inner + cross


def retention(q, k, v, S0, log_gamma):
    B, T = q.shape[0], q.shape[1]
    if T <= CHUNK:
        return retention_chunk(S0, q, k, v, log_gamma)
    nc = T // CHUNK
    def blocks(a):
        return jnp.moveaxis(a.reshape(B, nc, CHUNK, a.shape[2], a.shape[3]), 1, 0)
    S_fin, o = lax.scan(lambda S, xs: retention_chunk(S, xs[0], xs[1], xs[2], log_gamma),
                        S0, (blocks(q), blocks(k), blocks(v)))
    o = jnp.moveaxis(o, 0, 1).reshape(B, T, N_RET_HEADS, RET_DV)
    return S_fin, o


def trunk_layer(x, pool_left, ret_state, pos0, g_pre_mix, w_in, w_pool, pool_scale, w_pool_out,
                w_ret_out, w_o, g_post_mix, g_pre_ffn, w_up, w_down, g_post_ffn):
    B, T, _ = x.shape
    h = rms_norm(x, g_pre_mix)
    u, q, k, v, g_r, a_pool, a_ret = split_in(h @ w_in)
    pool_y, new_left = pool_mix(u, pool_left, pos0, w_pool, pool_scale)
    pool_branch = pool_y @ w_pool_out
    pos = pos0 + jnp.arange(T)
    qh = rotate(q.reshape(B, T, N_RET_HEADS, RET_DK).astype(jnp.float32), pos)
    kh = rotate(k.reshape(B, T, N_RET_HEADS, RET_DK).astype(jnp.float32), pos) * (RET_DK ** -0.5)
    vh = v.reshape(B, T, N_RET_HEADS, RET_DV).astype(jnp.float32)
    log_gamma = jnp.log1p(-jnp.exp2(-5.0 - jnp.arange(N_RET_HEADS, dtype=jnp.float32)))
    S_new, o = retention(qh, kh, vh, ret_state.astype(jnp.float32), log_gamma)
    o = o * lax.rsqrt(jnp.mean(o * o, axis=-1, keepdims=True) + EPS)
    ret_branch = (o.reshape(B, T, RET_V_WIDTH).astype(x.dtype) * jax.nn.silu(g_r)) @ w_ret_out
    merged = jax.nn.sigmoid(a_pool) * pool_branch + jax.nn.sigmoid(a_ret) * ret_branch
    x1 = x + rms_norm(merged @ w_o, g_post_mix)
    h2 = rms_norm(x1, g_pre_ffn)
    f = jnp.square(jax.nn.relu(h2 @ w_up)) @ w_down
    y = x1 + rms_norm(f, g_post_ffn)
    return y, new_left, S_new


def setup_inputs(seed: int = 0) -> dict:
    key = jax.random.key(seed)
    ks = jax.random.split(key, 20)
    f32 = jnp.float32
    def nrm(k, shape, scale):
        return jax.random.normal(k, shape, f32) * scale
    def gain(k):
        return 1.0 + 0.05 * jax.random.normal(k, (DEPTH, D_MODEL), f32)
    return {
        "x_prompt": nrm(ks[0], (BATCH, SEQ, D_MODEL), 1.0),
        "x_sample": nrm(ks[1], (DEC_BATCH, DEC_SEQ, D_MODEL), 1.0),
        "cache_pool": nrm(ks[2], (DEPTH, DEC_BATCH, POOL_LAG, POOL_WIDTH), 1.0),
        "state_retention": nrm(ks[3], (DEPTH, DEC_BATCH, N_RET_HEADS, RET_DK, RET_DV), 1.0),
        "g_pre_mix": gain(ks[4]),
        "w_in": nrm(ks[5], (DEPTH, D_MODEL, IN_WIDTH), D_MODEL ** -0.5),
        "w_pool": nrm(ks[6], (DEPTH, POOL_GROUPS, POOL_GROUP_WIDTH, POOL_GROUP_WIDTH), POOL_GROUP_WIDTH ** -0.5),
        "pool_scale": 1.0 + 0.05 * jax.random.normal(ks[7], (DEPTH, POOL_WIDTH), f32),
        "w_pool_out": nrm(ks[8], (DEPTH, POOL_WIDTH, D_MODEL), POOL_WIDTH ** -0.5),
        "w_ret_out": nrm(ks[9], (DEPTH, RET_V_WIDTH, D_MODEL), RET_V_WIDTH ** -0.5),
        "w_o": nrm(ks[10], (DEPTH, D_MODEL, D_MODEL), D_MODEL ** -0.5),
        "g_post_mix": gain(ks[11]),
        "g_pre_ffn": gain(ks[12]),
        "w_up": nrm(ks[13], (DEPTH, D_MODEL, D_FF), D_MODEL ** -0.5),
        "w_down": nrm(ks[14], (DEPTH, D_FF, D_MODEL), D_FF ** -0.5),
        "g_post_ffn": gain(ks[15]),
    }


def reference(x_prompt, x_sample, cache_pool, state_retention, g_pre_mix, w_in, w_pool, pool_scale,
              w_pool_out, w_ret_out, w_o, g_post_mix, g_pre_ffn, w_up, w_down, g_post_ffn):
    xp, xs = x_prompt, x_sample
    pool_p, ret_p, pool_s, ret_s = [], [], [], []
    for l in range(DEPTH):
        params = (g_pre_mix[l], w_in[l], w_pool[l], pool_scale[l], w_pool_out[l], w_ret_out[l], w_o[l],
                  g_post_mix[l], g_pre_ffn[l], w_up[l], w_down[l], g_post_ffn[l])
        left0 = jnp.zeros((xp.shape[0], POOL_LAG, POOL_WIDTH), xp.dtype)
        S0 = jnp.zeros((xp.shape[0], N_RET_HEADS, RET_DK, RET_DV), jnp.float32)
        xp, lp, sp = trunk_layer(xp, left0, S0, 0, *params)
        xs, ls, ss = trunk_layer(xs, cache_pool[l], state_retention[l], PAST_LEN, *params)
        pool_p.append(lp)
        ret_p.append(sp.astype(x_prompt.dtype))
        pool_s.append(ls.astype(cache_pool.dtype))
        ret_s.append(ss.astype(state_retention.dtype))
    return (xp, xs, jnp.stack(pool_p), jnp.stack(ret_p), jnp.stack(pool_s), jnp.stack(ret_s))
```

```python
import bisect
from contextlib import ExitStack

import numpy as np
import concourse.bass as bass
import concourse.mybir as mybir
from concourse.bass_utils import run_bass_kernel_spmd

F32 = mybir.dt.float32
BF16 = mybir.dt.bfloat16
ALU = mybir.AluOpType
AF = mybir.ActivationFunctionType
AX = mybir.AxisListType

NCORES = 8
D = 2048
KC = D // 128
TP = 1024
NT = 9
T = NT * 128
SEQ = 8192
DEC_B, DEC_T = 16, 64
PAST = 2048
PW = 1024
LAG = 15
NH, DK, DV = 8, 128, 256
IN_W = 11264
DFF = 8192
EPS = 1e-6
OFF_U, OFF_Q, OFF_K, OFF_V, OFF_G, OFF_AP, OFF_AR = 0, 1024, 2048, 3072, 5120, 7168, 9216
TBLK = [(0, 512), (512, 512), (1024, 128)]
FG = 1024
NPRE = 56
NSLOT = 2
LOG_GAMMA = [float(np.log1p(-np.exp2(np.float32(-5.0 - h)))) for h in range(NH)]
G128 = [float(np.exp(128.0 * lg)) for lg in LOG_GAMMA]
G64 = [float(np.exp(64.0 * lg)) for lg in LOG_GAMMA]

ENGS = ("pe", "act", "dve", "pool", "sp")


class Sched:
    def __init__(self, nc, es):
        self.nc = nc
        self.es = es
        self.items = {e: [] for e in ENGS}
        self.npos = {e: 0 for e in ENGS}
        self.counted = {e: [] for e in ENGS}
        self.known = {e: {} for e in ENGS}
        self.lw = {}
        self.rd = {}
        self.dma_val = {}
        self.dma_sem = {}
        self.esem = {e: es.enter_context(nc.semaphore("c_" + e)) for e in ENGS}
        self.nwaits = 0
        self.nops = 0
        self.simsem = {}

    def _deps(self, reads, writes):
        deps = []
        for r in reads:
            t = self.lw.get(r)
            if t is not None:
                deps.append(t)
        for w in writes:
            t = self.lw.get(w)
            if t is not None:
                deps.append(t)
            deps.extend(self.rd.get(w, ()))
        return deps

    def _commit(self, tok, reads, writes):
        for r in reads:
            self.rd.setdefault(r, []).append(tok)
        for w in writes:
            self.lw[w] = tok
            self.rd[w] = []

    def op(self, eng, fn, reads=(), writes=(), counted=True):
        reads, writes = list(reads), list(writes)
        deps = self._deps(reads, writes)
        pos = self.npos[eng]
        self.npos[eng] += 1
        if counted:
            self.counted[eng].append(pos)
        it = dict(kind="op", fn=fn, deps=deps, counted=counted, pos=pos)
        self.items[eng].append(it)
        self._commit(("e", eng, pos), reads, writes)
        self.nops += 1
        return it

    def dma(self, queue, chan, fn, reads=(), writes=(), inc=16):
        reads, writes = list(reads), list(writes)
        deps = self._deps(reads, writes)
        if chan not in self.dma_sem:
            self.dma_sem[chan] = self.es.enter_context(self.nc.semaphore("d_" + chan))
            self.dma_val[chan] = 0
        self.dma_val[chan] += inc
        tok = ("d", chan, self.dma_val[chan])
        it = dict(kind="dma", fn=fn, deps=deps, chan=chan, inc=inc)
        self.items[queue].append(it)
        self._commit(tok, reads, writes)
        return tok

    def wait_tok(self, eng, toks):
        self.items[eng].append(dict(kind="wait", deps=list(toks)))

    def _resolve(self, tok):
        if tok[0] == "d":
            return ("d", tok[1]), self.dma_sem[tok[1]], tok[2]
        _, eng, pos = tok
        lst = self.counted[eng]
        i = bisect.bisect_left(lst, pos)
        assert i < len(lst), ("dependency on trailing uncounted op", eng, pos)
        return ("e", eng), self.esem[eng], i + 1

    def flush(self):
        for e in ENGS:
            ops = [it for it in self.items[e] if it["kind"] == "op"]
            if ops and not ops[-1]["counted"]:
                ops[-1]["counted"] = True
                bisect.insort(self.counted[e], ops[-1]["pos"])
        nc = self.nc
        simlog = {e: [] for e in ENGS}
        with nc.Block() as block:
            def emit(ename, eng):
                known = self.known[ename]
                for it in self.items[ename]:
                    for tok in it["deps"]:
                        if tok[0] == "e" and tok[1] == "pe" and ename == "pe":
                            continue
                        key, sem, val = self._resolve(tok)
                        if known.get(key, 0) >= val:
                            continue
                        eng.wait_ge(sem, val)
                        known[key] = val
                        self.nwaits += 1
                        simlog[ename].append(("wait", key, val))
                    if it["kind"] == "wait":
                        continue
                    ins = it["fn"](eng)
                    if it["kind"] == "dma":
                        ins.then_inc(self.dma_sem[it["chan"]], it["inc"])
                        simlog[ename].append(("inc", ("d", it["chan"]), it["inc"]))
                    elif it["counted"]:
                        ins.then_inc(self.esem[ename], 1)
                        simlog[ename].append(("inc", ("e", ename), 1))

            @block.tensor
            def _(eng):
                emit("pe", eng)

            @block.scalar
            def _(eng):
                emit("act", eng)

            @block.vector
            def _(eng):
                emit("dve", eng)

            @block.gpsimd
            def _(eng):
                emit("pool", eng)

            @block.sync
            def _(eng):
                emit("sp", eng)
        self._simulate(simlog)
        for e in ENGS:
            for e2 in ENGS:
                self.known[e][("e", e2)] = len(self.counted[e2])
        self.items = {e: [] for e in ENGS}
        self.lw = {k: v for k, v in self.lw.items() if v[0] == "d"}
        self.rd = {k: [t for t in v if t[0] == "d"] for k, v in self.rd.items()}
        self.rd = {k: v for k, v in self.rd.items() if v}


def _sched_simulate(self, simlog):
    sem = dict(self.simsem)
    pc = {e: 0 for e in ENGS}
    progress = True
    while progress:
        progress = False
        for e in ENGS:
            lst = simlog[e]
            while pc[e] < len(lst):
                kind, key, val = lst[pc[e]]
                if kind == "wait":
                    if sem.get(key, 0) < val:
                        break
                else:
                    sem[key] = sem.get(key, 0) + val
                pc[e] += 1
                progress = True
    stuck = {e: (pc[e], len(simlog[e]), simlog[e][pc[e]]) for e in ENGS if pc[e] < len(simlog[e])}
    assert not stuck, ("DEADLOCK in phase", stuck, {k_: sem.get(k_) for k_ in [v[2][1] for v in stuck.values()]})
    self.simsem = sem


Sched._simulate = _sched_simulate


class Mem:
    BASE = 16512
    LIMIT = 229376

    def __init__(self, nc):
        self.nc = nc
        self.live = {}
        self.pending = []
        self.n = 0
        self.peak = 0

    def alloc(self, name, shape, dtype):
        esz = 2 if dtype == BF16 else 4
        size = int(np.prod(shape[1:])) * esz
        size = (size + 31) // 32 * 32
        segs = sorted(self.live.values())
        off = self.BASE
        for (o, s) in segs:
            if off + size <= o:
                break
            off = max(off, o + s)
        assert off + size <= self.LIMIT, ("SBUF OOM", name, size, off, sorted(self.live.items(), key=lambda kv: kv[1]))
        self.n += 1
        uname = "%s_%d" % (name, self.n)
        self.live[uname] = (off, size)
        self.peak = max(self.peak, off + size)
        t = self.nc.alloc_sbuf_tensor_at(uname, list(shape), dtype, offset=off)
        return t, uname

    def free(self, uname):
        self.pending.append(uname)

    def commit(self):
        for u in self.pending:
            self.live.pop(u, None)
        self.pending = []


class K:
    pass


def _mk(fn, *a, **kw):
    return lambda eng: fn(eng, *a, **kw)


def build_program(stop_after=99, dbg=()):
    nc = bass.Bass("TRN2", target_bir_lowering=False)
    es = ExitStack()
    S = Sched(nc, es)
    M = Mem(nc)
    k = K()
    k.nc, k.S, k.M, k.es = nc, S, M, es
    k.dbg = set(dbg)
    k.dbg_outs = []

    def din(name, shape):
        return nc.dram_tensor(name, list(shape), F32, kind="ExternalInput").ap()

    def dout(name, shape):
        return nc.dram_tensor(name, list(shape), F32, kind="ExternalOutput").ap()

    k.x = din("x", [T, D])
    k.xh = din("xh", [16, D])
    k.xprev = din("xprev", [NPRE * 128, D])
    k.t_cosp = din("t_cosp", [NPRE * 128, 64])
    k.t_sinp = din("t_sinp", [NPRE * 128, 64])
    k.cpool = din("cpool", [2, LAG, PW])
    k.sret = din("sret", [2, NH, DK, DV])
    k.w_in = din("w_in", [D, IN_W])
    k.w_pool = din("w_pool", [4, 256, 256])
    k.w_pool_out = din("w_pool_out", [PW, D])
    k.w_ret_out = din("w_ret_out", [D, D])
    k.w_o = din("w_o", [D, D])
    k.w_up = din("w_up", [D, DFF])
    k.w_down = din("w_down", [DFF, D])
    k.g_pre_mix = din("g_pre_mix", [128, KC])
    k.g_pre_ffn = din("g_pre_ffn", [128, KC])
    k.pscale = din("pscale", [128, 8])
    k.g_post_mix = din("g_post_mix", [128, D])
    k.g_post_ffn = din("g_post_ffn", [128, D])
    k.t_cos = din("t_cos", [T, 64])
    k.t_sin = din("t_sin", [T, 64])
    k.t_dec = din("t_dec", [128, 32])
    k.t_mask = din("t_mask", [128, 256])
    k.t_coef = din("t_coef", [128, 64])
    k.t_invc = din("t_invc", [128, 64])
    k.t_ident = din("t_ident", [128, 128])

    k.y = dout("y", [T, D])
    k.pool_p = dout("pool_p", [LAG, PW])
    k.ret_p = dout("ret_p", [NH, DK, DV])
    k.pool_s = dout("pool_s", [2, LAG, PW])
    k.ret_s = dout("ret_s", [2, NH, DK, DV])

    k.x1_d = nc.dram_tensor("x1_scratch", [T, D], F32).ap()

    k.ps = [nc.alloc_psum_tensor("ps%d" % i, [128, 512], F32) for i in range(8)]
    k.wslot = [M.alloc("wslot%d" % i, [128, KC, 512], BF16)[0] for i in range(NSLOT)]
    k.ident_f, _ = M.alloc("ident_f", [128, 128], F32)
    k.ident_b, _ = M.alloc("ident_b", [128, 128], BF16)
    k.eps_t, _ = M.alloc("eps_t", [128, 1], F32)
    k.gpm, _ = M.alloc("gpm", [128, KC], F32)
    k.gpf, _ = M.alloc("gpf", [128, KC], F32)
    k.psc, _ = M.alloc("psc", [128, 8], F32)
    k.dec, _ = M.alloc("dec", [128, 32], F32)
    k.mask, _ = M.alloc("mask", [128, 256], F32)
    k.coef, _ = M.alloc("coef", [128, 64], F32)
    k.invc, _ = M.alloc("invc", [128, 64], F32)

    k.phase_toks = []

    def ld(chan, dst, src, writes, queue="sp"):
        tok = S.dma(queue, chan, lambda e: e.dma_start(out=dst, in_=src), writes=writes)
        k.phase_toks.append(tok)
        return tok
    k.ld = ld

    def st(chan, dst, src, reads, queue="sp", writes=()):
        tok = S.dma(queue, chan, lambda e: e.dma_start(out=dst, in_=src), reads=reads, writes=writes)
        k.phase_toks.append(tok)
        return tok
    k.st = st

    def end_phase():
        S.wait_tok("sp", k.phase_toks)
        k.phase_toks = []
        S.flush()
        M.commit()
    k.end_phase = end_phase

    def dump(name, sb_ap, shape, reads, dtype=F32):
        if name not in k.dbg:
            return
        d = nc.dram_tensor("dbg_" + name, list(shape), dtype, kind="ExternalOutput").ap()
        k.dbg_outs.append("dbg_" + name)
        st("dbg_" + name, d, sb_ap, reads)
    k.dump = dump

    ld("c_identf", k.ident_f[:], k.t_ident, [("ident_f",)])
    ld("c_identb", k.ident_b[:], k.t_ident, [("ident_b",)], queue="pool")
    ld("c_gpm", k.gpm[:], k.g_pre_mix, [("gpm",)])
    ld("c_gpf", k.gpf[:], k.g_pre_ffn, [("gpf",)])
    ld("c_psc", k.psc[:], k.pscale, [("psc",)])
    ld("c_dec", k.dec[:], k.t_dec, [("dec",)])
    ld("c_mask", k.mask[:], k.t_mask, [("mask",)])
    ld("c_coef", k.coef[:], k.t_coef, [("coef",)])
    ld("c_invc", k.invc[:], k.t_invc, [("invc",)])
    S.op("dve", lambda e: e.memset(k.eps_t[:], EPS), writes=[("eps",)])

    blocks = []
    def wblk(ap2d, r0, c0, nk=KC):
        blocks.append((ap2d, r0, c0, nk))
    for c0 in (OFF_Q, OFF_Q + 512, OFF_K, OFF_K + 512):
        wblk(k.w_in, 0, c0)
    for j in range(4):
        wblk(k.w_in, 0, OFF_V + 512 * j)
    for j in range(4):
        wblk(k.w_in, 0, OFF_G + 512 * j)
    for j in range(2):
        wblk(k.w_in, 0, OFF_U + 512 * j)
    for j in range(4):
        wblk(k.w_in, 0, OFF_AP + 512 * j)
        wblk(k.w_pool_out, 0, 512 * j, 8)
        wblk(k.w_in, 0, OFF_AR + 512 * j)
        wblk(k.w_ret_out, 0, 512 * j)
    k.n_pre_wo = len(blocks)
    for j in range(4):
        wblk(k.w_o, 0, 512 * j)
    for g in range(DFF // FG):
        for j in range(FG // 512):
            wblk(k.w_up, 0, g * FG + 512 * j)
        for j in range(4):
            wblk(k.w_down, g * FG, 512 * j, FG // 128)
    k.blocks = blocks
    k.wcur = 0
    k.wloaded = 0

    def wload(j, slot_tile, key, chan):
        ap2d, r0, c0, nk = k.blocks[j]
        src = ap2d[r0:r0 + nk * 128, c0:c0 + 512].rearrange("(kc p) n -> p kc n", p=128)
        S.dma("pool", chan, lambda e: e.dma_start(out=slot_tile[:, 0:nk, :], in_=src), writes=[key])

    def wnext():
        j = k.wcur
        k.wcur += 1
        while k.wloaded < min(j + NSLOT, len(k.blocks)):
            i = k.wloaded
            wload(i, k.wslot[i % NSLOT], ("w", i % NSLOT), "w%d" % (i % NSLOT))
            k.wloaded += 1
        return k.wslot[j % NSLOT], ("w", j % NSLOT), k.blocks[j]
    k.wnext = wnext

    phases = [phase0_prepare, phase1, phase0, phase3, phase4, phase2, phase5, phase6, phase7]
    for i, ph in enumerate(phases):
        if i >= stop_after:
            break
        ph(k)
    if k.phase_toks or any(S.items[e] for e in ENGS):
        end_phase()
    es.close()
    return nc, k


def norm_to_featmajor(k, name, x_src_fn, ntile, rows_fn, g_tile, gkey, hT, hT_key_fn, hT_cols_fn,
                      x_keep=None):
    S, M = k.S, k.M
    xt = [M.alloc(name + "_xt%d" % i, [128, D], F32) for i in range(2)]
    xn = [M.alloc(name + "_xn%d" % i, [128, D], BF16) for i in range(2)]
    junk = M.alloc(name + "_junk", [128, D], BF16)
    st_ = M.alloc(name + "_stat", [128, 2 * ntile + 2], F32)
    stat = st_[0]
    def stage_A(i):
        rows = rows_fn(i)
        b = i % 2
        if x_keep is None:
            xa = xt[b][0][0:rows, :]
            xkey = (name + "_xt", b)
            k.ld(name + "_x%d" % b, xa, x_src_fn(i), [xkey])
        else:
            xa, xkey = x_keep(i)
        ss = stat[0:rows, 2 * i:2 * i + 1]
        rs = stat[0:rows, 2 * i + 1:2 * i + 2]
        skey = (name + "_stat", i)
        S.op("act", _mk(lambda e, o, a, s: e.activation(out=o, in_=a, func=AF.Square, accum_out=s),
                        junk[0][0:rows, :], xa, ss),
             reads=[xkey], writes=[(name + "_junk",), skey])
        S.op("act", _mk(lambda e, s, ep: e.activation(out=s, in_=s, func=AF.Sqrt, bias=ep, scale=1.0 / D),
                        ss, k.eps_t[0:rows, :]),
             reads=[skey, ("eps",)], writes=[skey])
        S.op("dve", _mk(lambda e, o, a: e.reciprocal(o, a), rs, ss), reads=[skey], writes=[(name + "_rs", i)])
        xna = xn[b][0][0:rows, :]
        xnkey = (name + "_xn", b)
        S.op("dve", _mk(lambda e, o, a, s: e.tensor_scalar(o, a, s, None, ALU.mult), xna, xa, rs),
             reads=[xkey, (name + "_rs", i)], writes=[xnkey])

    def stage_B(i):
        rows = rows_fn(i)
        b = i % 2
        xnkey = (name + "_xn", b)
        c0, ncol = hT_cols_fn(i)
        for g4 in range(KC // 4):
            bank = (g4 % 2) + 2 * (i % 2)
            bkey = ("ps", bank)
            pb = k.ps[bank][:, :].bitcast(BF16)
            for j in range(4):
                kc = g4 * 4 + j
                S.op("pe", _mk(lambda e, o, a, idn: e.transpose(o, a, idn),
                               pb[:, j * 128:j * 128 + rows], xn[b][0][0:rows, kc * 128:(kc + 1) * 128],
                               k.ident_b[0:rows, 0:rows]),
                     reads=[xnkey, ("ident_b",)], writes=[bkey], counted=(j == 3))
            src = pb[:, 0:512].rearrange("p (j t) -> p j t", j=4)[:, :, 0:rows]
            dst = hT[:, g4 * 4:g4 * 4 + 4, c0:c0 + ncol]
            gb = g_tile[:, g4 * 4:g4 * 4 + 4].unsqueeze(2).to_broadcast([128, 4, rows])
            S.op("dve", _mk(lambda e, o, a, g: e.tensor_tensor(o, a, g, ALU.mult), dst, src, gb),
                 reads=[bkey, gkey], writes=[hT_key_fn(i, g4)])

    stage_A(0)
    for i in range(ntile):
        if i + 1 < ntile:
            stage_A(i + 1)
        stage_B(i)
    for t_, u in xt + xn + [junk, st_]:
        M.free(u)


def phase1(k):
    S, M = k.S, k.M
    hT, k.hT_u = M.alloc("hT", [128, KC, T], BF16)
    hTh, k.hTh_u = M.alloc("hTh", [128, KC, 16], BF16)
    k.hT, k.hTh = hT, hTh

    def rows_fn(i):
        return 16 if i == 0 else 128

    def src_fn(i):
        return k.xh if i == 0 else k.x[(i - 1) * 128:i * 128, :]

    def key_fn(i, g4):
        return ("hTh", g4) if i == 0 else ("hT", i - 1, g4)

    def run(i_list):
        pass
    class _HT:
        def __getitem__(self_, idx):
            raise NotImplementedError
    norm_to_featmajor(k, "p1h", lambda i: k.xh, 1, lambda i: 16, k.gpm, ("gpm",), hTh,
                      lambda i, g4: ("hTh", g4), lambda i: (0, 16))
    norm_to_featmajor(k, "p1", lambda i: k.x[i * 128:(i + 1) * 128, :], NT, lambda i: 128, k.gpm, ("gpm",), hT,
                      lambda i, g4: ("hT", i, g4), lambda i: (i * 128, 128))
    k.dump("hT", hT[:], [128, KC, T], [("hT", i, g4) for i in range(NT) for g4 in range(4)], BF16)
    k.end_phase()


def hT_keys(t0, ntok, kc):
    return [("hT", t, kc // 4) for t in range(t0 // 128, (t0 + ntok + 127) // 128)]


def phase3(k):
    S, M = k.S, k.M
    hT = k.hT
    qT, k.qT_u = M.alloc("qT", [128, NH, T], BF16)
    kt, k.kt_u = M.alloc("kt", [128, NT, NH * DK], BF16)
    vg, k.vg_u = M.alloc("vg", [128, NH, NT * DV], BF16)
    k.qT, k.kt, k.vg = qT, kt, vg
    cs_c, u_c = M.alloc("cs_c", [128, NT, 64], F32)
    cs_s, u_s = M.alloc("cs_s", [128, NT, 64], F32)
    k.ld("cs_c", cs_c[:], k.t_cos.rearrange("(t p) f -> p t f", p=128), [("cs_c",)])
    k.ld("cs_s", cs_s[:], k.t_sin.rearrange("(t p) f -> p t f", p=128), [("cs_s",)])
    tmp = [[M.alloc("rt%d_%d" % (s, i), [128, 4, 64], F32) for i in range(6)] for s in range(2)]
    qtok = [M.alloc("qtok%d" % s, [128, 512], BF16) for s in range(3)]
    Rl, u_Rl = M.alloc("Rl", [128, NH, DV], F32)
    it = 0
    pend = []
    for blk in range(4):
        wt, wkey, _ = k.wnext()
        is_q = blk < 2
        hb = blk % 2
        for tt in range(NT):
            var = 0 if tt < 8 else 1
            s = it % 2
            bank = it % 4
            it += 1
            ps = k.ps[bank]
            pkey = ("ps", bank)
            for kc in range(KC):
                S.op("pe", _mk(lambda e, o, l, r, st, sp: e.matmul(o, lhsT=l, rhs=r, start=st, stop=sp),
                               ps[:, :], hT[:, kc, tt * 128:(tt + 1) * 128], wt[:, kc, :], kc == 0, kc == KC - 1),
                     reads=[("hT", tt, kc // 4), wkey], writes=[pkey], counted=(kc == KC - 1))
            pv = ps[:, :].rearrange("p (h f two) -> p h f two", h=4, two=2)
            xe, xo = pv[:, :, :, 0], pv[:, :, :, 1]
            cb = cs_c[:, tt, :].unsqueeze(1).to_broadcast([128, 4, 64])
            sb = cs_s[:, tt, :].unsqueeze(1).to_broadcast([128, 4, 64])
            t1, t2, t3, t4, re, ro = [tmp[s][i][0] for i in range(6)]
            tk = [("rt", s, i) for i in range(6)]
            for (o, a, b_, ky) in ((t1, xe, cb, tk[0]), (t2, xo, sb, tk[1]), (t3, xe, sb, tk[2]), (t4, xo, cb, tk[3])):
                S.op("dve", _mk(lambda e, o, a, b_: e.tensor_tensor(o[:], a, b_, ALU.mult), o, a, b_),
                     reads=[pkey, ("cs_c",), ("cs_s",)], writes=[ky])
            S.op("pool", _mk(lambda e, o, a, b_: e.tensor_tensor(o[:], a[:], b_[:], ALU.subtract), re, t1, t2),
                 reads=[tk[0], tk[1]], writes=[tk[4]])
            S.op("pool", _mk(lambda e, o, a, b_: e.tensor_tensor(o[:], a[:], b_[:], ALU.add), ro, t3, t4),
                 reads=[tk[2], tk[3]], writes=[tk[5]])
            di = (0 if is_q else 16) + var * 8 + hb * 4
            db = k.dec[:, di:di + 4].unsqueeze(2).to_broadcast([128, 4, 64])
            qs = it % 3
            if is_q:
                dst2d = qtok[qs][0][:, :]
                dkey = ("qtok", qs)
            else:
                dst2d = kt[:, tt, hb * 512:(hb + 1) * 512]
                dkey = ("kt", tt, hb)
            dv_ = dst2d.rearrange("p (h f two) -> p h f two", h=4, two=2)
            S.op("pool", _mk(lambda e, o, a, b_: e.tensor_tensor(o, a[:], b_, ALU.mult), dv_[:, :, :, 0], re, db),
                 reads=[tk[4], ("dec",)], writes=[dkey])
            S.op("pool", _mk(lambda e, o, a, b_: e.tensor_tensor(o, a[:], b_, ALU.mult), dv_[:, :, :, 1], ro, db),
                 reads=[tk[5], ("dec",)], writes=[dkey])
            if is_q:
                pend.append((qs, hb, tt, dkey, 4 + (it % 2)))
            while pend and (not is_q or len(pend) > 2 or tt == NT - 1):
                s_, hb_, tt_, dkey_, tb = pend.pop(0)
                pb = k.ps[tb][:, :].bitcast(BF16)
                tkey = ("ps", tb)
                for j in range(4):
                    S.op("pe", _mk(lambda e, o, a, idn: e.transpose(o, a, idn),
                                   pb[:, j * 128:(j + 1) * 128], qtok[s_][0][:, j * 128:(j + 1) * 128], k.ident_b[:, :]),
                         reads=[dkey_, ("ident_b",)], writes=[tkey], counted=(j == 3))
                S.op("act", _mk(lambda e, o, a: e.copy(o, a),
                                qT[:, hb_ * 4:hb_ * 4 + 4, tt_ * 128:(tt_ + 1) * 128],
                                pb[:, 0:512].rearrange("p (j t) -> p j t", j=4)),
                     reads=[tkey], writes=[("qT", hb_ * 4 + j, tt_) for j in range(4)])
    for blk in range(4):
        wt, wkey, _ = k.wnext()
        for tt in range(NT):
            bank = it % 4
            it += 1
            ps = k.ps[bank]
            pkey = ("ps", bank)
            for kc in range(KC):
                S.op("pe", _mk(lambda e, o, l, r, st, sp: e.matmul(o, lhsT=l, rhs=r, start=st, stop=sp),
                               ps[:, :], hT[:, kc, tt * 128:(tt + 1) * 128], wt[:, kc, :], kc == 0, kc == KC - 1),
                     reads=[("hT", tt, kc // 4), wkey], writes=[pkey], counted=(kc == KC - 1))
            S.op("act", _mk(lambda e, o, a: e.copy(o, a),
                            vg[:, 2 * blk:2 * blk + 2, tt * DV:(tt + 1) * DV],
                            ps[:, :].rearrange("p (h e) -> p h e", h=2)),
                 reads=[pkey], writes=[("vg", 2 * blk, tt), ("vg", 2 * blk + 1, tt)])
    k.dump("qT", qT[:], [128, NH, T], [("qT", h, tt) for h in range(NH) for tt in range(NT)], BF16)
    k.dump("kt", kt[:], [128, NT, NH * DK], [("kt", tt, hb) for tt in range(NT) for hb in range(2)], BF16)
    k.dump("vg", vg[:], [128, NH, NT * DV], [("vg", h, tt) for h in range(NH) for tt in range(NT)], BF16)
    for tt in (range(8) if "Sloc" in k.dbg else ()):
        for h in range(NH):
            bank = 4 + (h // 2) % 2 + 2 * (tt % 2)
            half = h % 2
            pkey = ("ps", bank, half)
            pd = k.ps[bank][:, half * 256:(half + 1) * 256]
            S.op("pe", _mk(lambda e, o, l, r: e.matmul(o, lhsT=l, rhs=r, start=True, stop=True),
                           pd, kt[:, tt, h * DK:(h + 1) * DK], vg[:, h, tt * DV:(tt + 1) * DV]),
                 reads=[("kt", tt, h // 4), ("vg", h, tt)], writes=[pkey])
            if tt == 0:
                S.op("dve", _mk(lambda e, o, a: e.tensor_copy(o, a), Rl[:, h, :], pd),
                     reads=[pkey], writes=[("Rl", h)])
            else:
                S.op("dve", _mk(lambda e, o, a, g: e.scalar_tensor_tensor(o, o, g, a, ALU.mult, ALU.add),
                                Rl[:, h, :], pd, G128[h]),
                     reads=[pkey, ("Rl", h)], writes=[("Rl", h)])
    for h in (range(NH) if "Sloc" in k.dbg else ()):
        S.op("act", _mk(lambda e, o, g: e.mul(o, o, g), Rl[:, h, :], G128[h]),
             reads=[("Rl", h)], writes=[("Rl", h)])
    k.dump("Sloc", Rl[:], [128, NH, DV], [("Rl", h) for h in range(NH)])
    for lst in tmp:
        for t_, u in lst:
            M.free(u)
    for t_, u in qtok:
        M.free(u)
    M.free(u_c); M.free(u_s); M.free(u_Rl)
    k.end_phase()


def _tables(c):
    inv = (1.0 / (10000.0 ** np.linspace(0.0, 1.0, DK // 2, dtype=np.float32))).astype(np.float32)
    pos = np.concatenate([c * TP + np.arange(TP), PAST + np.arange(DEC_T), PAST + np.arange(DEC_T)])
    ang = pos.astype(np.float32)[:, None] * inv[None, :]
    t_cos = np.cos(ang).astype(np.float32)
    t_sin = np.sin(ang).astype(np.float32)
    lg = np.array(LOG_GAMMA, dtype=np.float64)
    p = np.arange(128)
    dec = np.zeros((128, 2, 2, NH), dtype=np.float64)
    for var, L in enumerate((128, 64)):
        e = (p % L + 1).astype(np.float64)[:, None]
        dec[:, 0, var, :] = np.exp(e * lg[None, :])
        dec[:, 1, var, :] = np.exp(-e * lg[None, :]) * (DK ** -0.5)
    t_dec = dec.reshape(128, 32).astype(np.float32)
    j = np.arange(128)[:, None]
    i = np.arange(128)[None, :]
    maskP = (i >= j).astype(np.float32)
    maskS = ((i >= j) & ((i // 64) == (j // 64))).astype(np.float32)
    t_mask = np.concatenate([maskP, maskS], axis=1)
    coef = np.zeros((NCORES, NH), dtype=np.float64)
    for r in range(NCORES):
        if r < c:
            coef[r] = np.exp(1024.0 * (c - 1 - r) * lg) / np.exp(128.0 * lg)
    t_coef = np.broadcast_to(coef.reshape(1, 64), (128, 64)).astype(np.float32)
    invc = np.zeros((4, 16), dtype=np.float64)
    for g, w in enumerate((2, 4, 8, 16)):
        invc[g] = 1.0 / np.minimum(c * TP + np.arange(16) + 1, w)
    t_invc = np.broadcast_to(invc.reshape(1, 64), (128, 64)).astype(np.float32)
    ppos = (np.arange(NPRE * 128) - (NCORES - 1 - c) * TP).astype(np.float32)
    pang = ppos[:, None] * inv[None, :]
    t_cosp = np.cos(pang).astype(np.float32)
    t_sinp = np.sin(pang).astype(np.float32)
    return dict(t_cosp=t_cosp, t_sinp=t_sinp, t_cos=t_cos, t_sin=t_sin, t_dec=t_dec, t_mask=t_mask,
                t_coef=np.ascontiguousarray(t_coef), t_invc=np.ascontiguousarray(t_invc),
                t_ident=np.eye(128, dtype=np.float32))


def prep_inputs(x_prompt, x_sample, cache_pool, state_retention, g_pre_mix, w_in, w_pool, pool_scale,
                w_pool_out, w_ret_out, w_o, g_post_mix, g_pre_ffn, w_up, w_down, g_post_ffn):
    f = lambda a: np.ascontiguousarray(np.asarray(a, dtype=np.float32))
    shared = dict(
        w_in=f(w_in[0]), w_pool=f(w_pool[0]), w_pool_out=f(w_pool_out[0]), w_ret_out=f(w_ret_out[0]),
        w_o=f(w_o[0]), w_up=f(w_up[0]), w_down=f(w_down[0]),
        g_pre_mix=f(np.asarray(g_pre_mix[0]).reshape(KC, 128).T),
        g_pre_ffn=f(np.asarray(g_pre_ffn[0]).reshape(KC, 128).T),
        pscale=f(np.asarray(pool_scale[0]).reshape(8, 128).T),
        g_post_mix=f(np.broadcast_to(np.asarray(g_post_mix[0])[None, :], (128, D))),
        g_post_ffn=f(np.broadcast_to(np.asarray(g_post_ffn[0])[None, :], (128, D))),
    )
    xp = np.asarray(x_prompt)[0]
    xs = np.asarray(x_sample)
    maps = []
    for c in range(NCORES):
        m = dict(shared)
        m["x"] = f(np.concatenate([xp[c * TP:(c + 1) * TP], xs[2 * c], xs[2 * c + 1]], axis=0))
        m["xh"] = f(xp[c * TP - 16:c * TP]) if c > 0 else np.zeros((16, D), np.float32)
        m["cpool"] = f(np.asarray(cache_pool)[0, 2 * c:2 * c + 2])
        m["sret"] = f(np.asarray(state_retention)[0, 2 * c:2 * c + 2])
        xpv = np.zeros((NPRE * 128, D), np.float32)
        if c > 0:
            xpv[(NCORES - 1 - c) * TP:] = xp[:c * TP]
        m["xprev"] = xpv
        m.update(_tables(c))
        maps.append(m)
    return maps


_PROG = {}


def kernel(**inputs):
    if "nc" not in _PROG:
        _PROG["nc"], _PROG["k"] = build_program()
    nc = _PROG["nc"]
    maps = prep_inputs(**inputs)
    res = run_bass_kernel_spmd(nc, maps, core_ids=list(range(NCORES)))
    R = res.results
    y = [np.asarray(r["y"]) for r in R]
    y_prompt = np.concatenate([a[:TP] for a in y], axis=0)[None]
    y_sample = np.concatenate([a[TP:].reshape(2, DEC_T, D) for a in y], axis=0)
    pool_prompt = np.asarray(R[NCORES - 1]["pool_p"])[None, None]
    ret_prompt = np.asarray(R[NCORES - 1]["ret_p"])[None, None]
    pool_sample = np.concatenate([np.asarray(r["pool_s"]) for r in R], axis=0)[None]
    ret_sample = np.concatenate([np.asarray(r["ret_s"]) for r in R], axis=0)[None]
    return tuple(np.ascontiguousarray(a.astype(np.float32)) for a in
                 (y_prompt, y_sample, pool_prompt, ret_prompt, pool_sample, ret_sample))


def _mm(S, out, lhsT, rhs, start, stop, reads, writes, counted=True):
    S.op("pe", _mk(lambda e, o, l, r, st, sp: e.matmul(o, lhsT=l, rhs=r, start=st, stop=sp),
                   out, lhsT, rhs, start, stop), reads=reads, writes=writes, counted=counted)


def _tr(S, out, in_, ident, reads, writes, counted=True):
    S.op("pe", _mk(lambda e, o, a, idn: e.transpose(o, a, idn), out, in_, ident),
         reads=reads, writes=writes, counted=counted)


def phase0_prepare(k):
    S, M = k.S, k.M
    Rp, k.Rp_u = M.alloc("Rp", [128, NH, DV], F32)
    k.Rp = Rp
    wk = [k.wslot[0], k.wslot[1]]
    wkk = [("w", 0), ("w", 1)]
    wv = [M.alloc("wv%d" % j, [128, KC, 512], BF16) for j in range(4)]
    k.p0w = (wk, wkk, wv)
    for kind, j in (("k", 1), ("v", 2), ("v", 3), ("k", 0), ("v", 1), ("v", 0)):
        if kind == "k":
            src = k.w_in[:, OFF_K + 512 * j:OFF_K + 512 * (j + 1)].rearrange("(kc p) n -> p kc n", p=128)
            S.dma("pool", "w%d" % j, _mk(lambda e, o, s: e.dma_start(out=o, in_=s), wk[j][:], src), writes=[wkk[j]])
        else:
            src = k.w_in[:, OFF_V + 512 * j:OFF_V + 512 * (j + 1)].rearrange("(kc p) n -> p kc n", p=128)
            tok = S.dma("pool", "wv%d" % j, _mk(lambda e, o, s: e.dma_start(out=o, in_=s), wv[j][0][:], src),
                        writes=[("wv", j)])
            k.p0_toks = getattr(k, "p0_toks", []) + [tok]


def phase0(k):
    S, M = k.S, k.M
    Rp = k.Rp
    wk, wkk, wv = k.p0w
    k.phase_toks.extend(k.p0_toks)
    xt = [M.alloc("p0x%d" % i, [128, D], F32) for i in range(2)]
    xn = [M.alloc("p0xn%d" % i, [128, D], BF16) for i in range(2)]
    hTt = [M.alloc("p0h%d" % i, [128, KC, 128], BF16) for i in range(2)]
    cs = [M.alloc("p0cs%d" % i, [128, 2, 64], F32) for i in range(2)]
    tmp = [[M.alloc("p0rt%d_%d" % (s, i), [128, 4, 64], F32) for i in range(6)] for s in range(2)]
    ktl = [M.alloc("p0k%d" % i, [128, NH * DK], BF16) for i in range(2)]
    vtl = [M.alloc("p0v%d" % i, [128, NH * DV], BF16) for i in range(2)]
    stat_, stat_u = M.alloc("p0stat", [128, 2 * NPRE], F32)
    state = {"pj": 0}

    def hmin_of(pt):
        s = 7 - pt // 8
        return {1: 0, 2: 1, 3: 2, 4: 3, 5: 3, 6: 3, 7: 4}[s]
    started = set()

    def stage_A(pt):
        b = pt % 2
        xa = xt[b][0]
        xkey = ("p0x", b)
        k.ld("p0x%d" % b, xa[:], k.xprev[pt * 128:(pt + 1) * 128, :], [xkey])
        ckey = ("p0cs", b)
        k.ld("p0cs%d" % b, cs[b][0][:, 0, :], k.t_cosp[pt * 128:(pt + 1) * 128, :], [ckey])
        k.ld("p0cs%d" % b, cs[b][0][:, 1, :], k.t_sinp[pt * 128:(pt + 1) * 128, :], [ckey])
        ss = stat_[:, 2 * pt:2 * pt + 1]
        rs = stat_[:, 2 * pt + 1:2 * pt + 2]
        xna = xn[b][0]
        xnkey = ("p0xn", b)
        S.op("act", _mk(lambda e, o, a, s: e.activation(out=o, in_=a, func=AF.Square, accum_out=s), xna[:], xa[:], ss),
             reads=[xkey], writes=[xnkey, ("p0ss", pt)])
        S.op("act", _mk(lambda e, s, ep: e.activation(out=s, in_=s, func=AF.Sqrt, bias=ep, scale=1.0 / D), ss, k.eps_t[:, :]),
             reads=[("p0ss", pt), ("eps",)], writes=[("p0ss", pt)])
        S.op("dve", _mk(lambda e, o, a: e.reciprocal(o, a), rs, ss), reads=[("p0ss", pt)], writes=[("p0rs", pt)])
        S.op("dve", _mk(lambda e, o, a, s: e.tensor_scalar(o, a, s, None, ALU.mult), xna[:], xa[:], rs),
             reads=[xkey, ("p0rs", pt)], writes=[xnkey])

    def stage_B(pt):
        b = pt % 2
        xna = xn[b][0]
        xnkey = ("p0xn", b)
        hk = ("p0h", b)
        for g4 in range(4):
            bank = g4 % 2
            bkey = ("ps", bank)
            pb = k.ps[bank][:, :].bitcast(BF16)
            for j in range(4):
                kc = g4 * 4 + j
                _tr(S, pb[:, j * 128:(j + 1) * 128], xna[:, kc * 128:(kc + 1) * 128], k.ident_b[:, :],
                    [xnkey, ("ident_b",)], [bkey], counted=(j == 3))
            srcp = pb[:, 0:512].rearrange("p (j t) -> p j t", j=4)
            gb = k.gpm[:, g4 * 4:g4 * 4 + 4].unsqueeze(2).to_broadcast([128, 4, 128])
            S.op("dve", _mk(lambda e, o, a, g: e.tensor_tensor(o, a, g, ALU.mult), hTt[b][0][:, g4 * 4:g4 * 4 + 4, :], srcp, gb),
                 reads=[bkey, ("gpm",)], writes=[(hk, g4)])

    def stage_C(pt):
        b = pt % 2
        hk = ("p0h", b)
        ckey = ("p0cs", b)
        kkey = ("p0k", b)
        vkey = ("p0v", b)
        hmin = hmin_of(pt)
        for blk in range(2):
            h0 = max(hmin - 4 * blk, 0)
            if h0 >= 4:
                continue
            nh = 4 - h0
            ncol = nh * 128
            bank = 2 + state["pj"] % 4
            state["pj"] += 1
            ps = k.ps[bank]
            pkey = ("ps", bank)
            for kc in range(KC):
                _mm(S, ps[:, 0:ncol], hTt[b][0][:, kc, :], wk[blk][:, kc, h0 * 128:512], kc == 0, kc == KC - 1,
                    [(hk, kc // 4), wkk[blk]], [pkey], counted=(kc == KC - 1))
            s = state["pj"] % 2
            pv = ps[:, 0:ncol].rearrange("p (h f two) -> p h f two", h=nh, two=2)
            xe, xo = pv[:, :, :, 0], pv[:, :, :, 1]
            cb = cs[b][0][:, 0, :].unsqueeze(1).to_broadcast([128, nh, 64])
            sb = cs[b][0][:, 1, :].unsqueeze(1).to_broadcast([128, nh, 64])
            t1, t2, t3, t4, re, ro = [tmp[s][i][0][:, 0:nh, :] for i in range(6)]
            tk = [("p0rt", s, i) for i in range(6)]
            for (o, a, b_, ky) in ((t1, xe, cb, tk[0]), (t2, xo, sb, tk[1]), (t3, xe, sb, tk[2]), (t4, xo, cb, tk[3])):
                S.op("dve", _mk(lambda e, o, a, b_: e.tensor_tensor(o, a, b_, ALU.mult), o, a, b_),
                     reads=[pkey, ckey], writes=[ky])
            S.op("pool", _mk(lambda e, o, a, b_: e.tensor_tensor(o, a, b_, ALU.subtract), re, t1, t2),
                 reads=[tk[0], tk[1]], writes=[tk[4]])
            S.op("pool", _mk(lambda e, o, a, b_: e.tensor_tensor(o, a, b_, ALU.add), ro, t3, t4),
                 reads=[tk[2], tk[3]], writes=[tk[5]])
            di = 16 + blk * 4 + h0
            db = k.dec[:, di:di + nh].unsqueeze(2).to_broadcast([128, nh, 64])
            dv_ = ktl[b][0][:, blk * 512 + h0 * 128:(blk + 1) * 512].rearrange("p (h f two) -> p h f two", h=nh, two=2)
            S.op("pool", _mk(lambda e, o, a, b_: e.tensor_tensor(o, a, b_, ALU.mult), dv_[:, :, :, 0], re, db),
                 reads=[tk[4], ("dec",)], writes=[(kkey, blk)])
            S.op("pool", _mk(lambda e, o, a, b_: e.tensor_tensor(o, a, b_, ALU.mult), dv_[:, :, :, 1], ro, db),
                 reads=[tk[5], ("dec",)], writes=[(kkey, blk)])

    def stage_Cv(pt):
        b = pt % 2
        hk = ("p0h", b)
        vkey = ("p0v", b)
        hmin = hmin_of(pt)
        for blk in range(4):
            h0 = max(hmin - 2 * blk, 0)
            if h0 >= 2:
                continue
            c0 = h0 * 256
            ncol = 512 - c0
            bank = 2 + state["pj"] % 4
            state["pj"] += 1
            ps = k.ps[bank]
            pkey = ("ps", bank)
            for kc in range(KC):
                _mm(S, ps[:, 0:ncol], hTt[b][0][:, kc, :], wv[blk][0][:, kc, c0:512], kc == 0, kc == KC - 1,
                    [(hk, kc // 4), ("wv", blk)], [pkey], counted=(kc == KC - 1))
            S.op("act", _mk(lambda e, o, a: e.copy(o, a), vtl[b][0][:, blk * 512 + c0:(blk + 1) * 512], ps[:, 0:ncol]),
                 reads=[pkey], writes=[(vkey, blk)])

    def stage_D(pt):
        b = pt % 2
        kkey = ("p0k", b)
        vkey = ("p0v", b)
        for h in range(hmin_of(pt), NH):
            bank = 6 + (h // 2) % 2
            half = h % 2
            pkey = ("ps", bank, half)
            pd = k.ps[bank][:, half * 256:(half + 1) * 256]
            _mm(S, pd, ktl[b][0][:, h * DK:(h + 1) * DK], vtl[b][0][:, h * DV:(h + 1) * DV], True, True,
                [(kkey, h // 4), (vkey, h // 2)], [pkey])
            if h not in started:
                started.add(h)
                S.op("dve", _mk(lambda e, o, a: e.tensor_copy(o, a), Rp[:, h, :], pd), reads=[pkey], writes=[("Rp", h)])
            else:
                S.op("dve", _mk(lambda e, o, a, g: e.scalar_tensor_tensor(o, o, g, a, ALU.mult, ALU.add), Rp[:, h, :], pd, G128[h]),
                     reads=[pkey, ("Rp", h)], writes=[("Rp", h)])

    stage_A(0)
    for pt in range(NPRE):
        stage_B(pt)
        stage_C(pt)
        if pt > 0:
            stage_D(pt - 1)
        if pt + 1 < NPRE:
            stage_A(pt + 1)
        stage_Cv(pt)
    stage_D(NPRE - 1)
    k.dump("Rp", Rp[:], [128, NH, DV], [("Rp", h) for h in range(NH)])
    for lst in (wv, xt, xn, hTt, cs, ktl, vtl):
        for t_, u in lst:
            M.free(u)
    for lst in tmp:
        for t_, u in lst:
            M.free(u)
    M.free(stat_u)
    k.end_phase()


def phase4(k):
    S, M = k.S, k.M
    hT, qT, kt, vg, Rp = k.hT, k.qT, k.kt, k.vg, k.Rp
    sg, sg_u = M.alloc("sg", [128, NT, 512], BF16)
    ktT = [M.alloc("ktT%d" % i, [128, T], BF16) for i in range(2)]
    go = [M.alloc("go%d" % i, [128, NT, DV], BF16) for i in range(2)]
    Sbf, Sbf_u = M.alloc("Sbf", [128, 2, DV], BF16)
    S0, S0_u = M.alloc("S0", [128, 2, 2, DV], F32)
    S0b, S0b_u = M.alloc("S0b", [128, 2, 2, DV], BF16)
    sT = [M.alloc("sT%d" % i, [128, 128], BF16) for i in range(4)]
    qA = [M.alloc("qA%d" % i, [128, 128], BF16) for i in range(2)]
    qB = [M.alloc("qB%d" % i, [128, 128], BF16) for i in range(2)]
    junk, junk_u = M.alloc("p4junk", [128, DV], BF16)
    st_, st_u = M.alloc("p4stat", [128, 2 * NT * NH], F32)
    So = [M.alloc("So%d" % i, [128, DV], F32) for i in range(4)]
    for i in range(2):
        S.op("pool", _mk(lambda e, o: e.memset(o, 0.0), qA[i][0][:, :]), writes=[("qA", i)])
        S.op("pool", _mk(lambda e, o: e.memset(o, 0.0), qB[i][0][:, :]), writes=[("qB", i)])
    nsT = 0
    nSo = 0
    nps = 0
    sgb = [(sg, sg_u), M.alloc("sg2", [128, NT, 512], BF16)]

    def gr_proj(hp_):
        wt, wkey, _ = k.wnext()
        sgt = sgb[hp_ % 2][0]
        for tt in range(NT):
            bank = tt % 2
            pkey = ("ps", bank)
            for kc in range(KC):
                _mm(S, k.ps[bank][:, :], hT[:, kc, tt * 128:(tt + 1) * 128], wt[:, kc, :], kc == 0, kc == KC - 1,
                    [("hT", tt, kc // 4), wkey], [pkey], counted=(kc == KC - 1))
            S.op("act", _mk(lambda e, o, a: e.activation(out=o, in_=a, func=AF.Silu), sgt[:, tt, :], k.ps[bank][:, :]),
                 reads=[pkey], writes=[("sg", hp_ % 2, tt)])

    gr_proj(0)
    for hp in range(4):
        sg = sgb[hp % 2][0]
        for hl in range(2):
            h = 2 * hp + hl
            for (t0, n) in ((0, 4), (4, 4), (8, 1)):
                bank = 2 + nps % 2
                nps += 1
                bkey = ("ps", bank)
                pb = k.ps[bank][:, :].bitcast(BF16)
                for j in range(n):
                    _tr(S, pb[:, j * 128:(j + 1) * 128], kt[:, t0 + j, h * DK:(h + 1) * DK], k.ident_b[:, :],
                        [("kt", t0 + j, h // 4), ("ident_b",)], [bkey], counted=(j == n - 1))
                S.op("act", _mk(lambda e, o, a: e.copy(o, a), ktT[hl][0][:, t0 * 128:(t0 + n) * 128], pb[:, 0:n * 128]),
                     reads=[bkey], writes=[("ktT", hl)])
            S.op("act", _mk(lambda e, o, a, g: e.mul(o, a, g), Sbf[:, hl, :], Rp[:, h, :], G128[h]),
                 reads=[("Rp", h)], writes=[("Sbf", hl)])
            for s_ in range(2):
                k.ld("S0_%d_%d" % (s_, hl), S0[:, s_, hl, :], k.sret[s_, h], [("S0", s_, hl)])
                S.op("act", _mk(lambda e, o, a: e.copy(o, a), S0b[:, s_, hl, :], S0[:, s_, hl, :]),
                     reads=[("S0", s_, hl)], writes=[("S0b", s_, hl)])
                S.op("act", _mk(lambda e, o, g: e.mul(o, o, g), S0[:, s_, hl, :], G64[h]),
                     reads=[("S0", s_, hl), ("S0b", s_, hl)], writes=[("S0", s_, hl)])
            S.op("pool", _mk(lambda e, o, a: e.tensor_copy(o, a), qA[hl][0][:, 0:64], qT[:, h, 1024:1088]),
                 reads=[("qT", h, 8)], writes=[("qA", hl)])
            S.op("pool", _mk(lambda e, o, a: e.tensor_copy(o, a), qB[hl][0][:, 64:128], qT[:, h, 1088:1152]),
                 reads=[("qT", h, 8)], writes=[("qB", hl)])
        ctx = {}

        def st_A(tt, hl):
            h = 2 * hp + hl
            hc = slice(h * DK, (h + 1) * DK)
            tsl = slice(tt * 128, (tt + 1) * 128)
            vsl = slice(tt * DV, (tt + 1) * DV)
            c = ctx[(tt, hl)] = {}
            if tt < 8:
                c["dkey"] = ("ps", 7, hl)
                c["pd"] = k.ps[7][:, hl * 256:(hl + 1) * 256]
                _mm(S, c["pd"], kt[:, tt, hc], vg[:, h, vsl], True, True, [("kt", tt, h // 4), ("vg", h, tt)], [c["dkey"]])
            else:
                c["dkeyA"] = ("ps", 7, hl)
                c["dkeyB"] = ("ps", 3)
                c["pdA"] = k.ps[7][:, hl * 256:(hl + 1) * 256]
                c["pdB"] = k.ps[3][:, hl * 256:(hl + 1) * 256]
                _mm(S, c["pdA"], kt[0:64, tt, hc], vg[0:64, h, vsl], True, True, [("kt", tt, h // 4), ("vg", h, tt)], [c["dkeyA"]])
                _mm(S, c["pdB"], kt[64:128, tt, hc], vg[64:128, h, vsl], True, True, [("kt", tt, h // 4), ("vg", h, tt)], [c["dkeyB"]])
            q4 = (2 * tt + hl) % 4
            skey = ("ps", 4, q4)
            pS = k.ps[4][:, q4 * 128:(q4 + 1) * 128]
            _mm(S, pS, ktT[hl][0][:, tsl], qT[:, h, tsl], True, True, [("ktT", hl), ("qT", h, tt)], [skey])
            si = (2 * tt + hl) % 4
            c["si"] = si
            mk = k.mask[:, 0:128] if tt < 8 else k.mask[:, 128:256]
            S.op("dve", _mk(lambda e, o, a, m: e.tensor_tensor(o, a, m, ALU.mult), sT[si][0][:, :], pS, mk),
                 reads=[skey, ("mask",)], writes=[("sT", si)])

        def st_B(tt, hl):
            nonlocal nSo
            h = 2 * hp + hl
            tsl = slice(tt * 128, (tt + 1) * 128)
            vsl = slice(tt * DV, (tt + 1) * DV)
            c = ctx[(tt, hl)]
            si = c["si"]
            ob = 5 if hl == 0 else 6
            oh = tt % 2
            okey = ("ps", ob, oh)
            pO = k.ps[ob][:, oh * 256:(oh + 1) * 256]
            c["okey"], c["pO"] = okey, pO
            _mm(S, pO, sT[si][0][:, :], vg[:, h, vsl], True, False, [("sT", si), ("vg", h, tt)], [okey], counted=False)
            if tt < 8:
                _mm(S, pO, qT[:, h, tsl], Sbf[:, hl, :], False, True, [("qT", h, tt), ("Sbf", hl)], [okey])
            else:
                _mm(S, pO, qA[hl][0][:, :], S0b[:, 0, hl, :], False, False, [("qA", hl), ("S0b", 0, hl)], [okey], counted=False)
                _mm(S, pO, qB[hl][0][:, :], S0b[:, 1, hl, :], False, True, [("qB", hl), ("S0b", 1, hl)], [okey])
            if tt < 8:
                S.op("dve", _mk(lambda e, o, a, g: e.scalar_tensor_tensor(o, o, g, a, ALU.mult, ALU.add), Rp[:, h, :], c["pd"], G128[h]),
                     reads=[c["dkey"], ("Rp", h)], writes=[("Rp", h)])
                if tt < 7:
                    S.op("act", _mk(lambda e, o, a, g: e.mul(o, a, g), Sbf[:, hl, :], Rp[:, h, :], G128[h]),
                         reads=[("Rp", h)], writes=[("Sbf", hl)])
                else:
                    so = So[nSo % 4]
                    sok = ("So", nSo % 4)
                    nSo += 1
                    S.op("act", _mk(lambda e, o, a, g: e.mul(o, a, g), so[0][:, :], Rp[:, h, :], G128[h]),
                         reads=[("Rp", h)], writes=[sok])
                    k.st("So%d" % (int(sok[1])), k.ret_p[h], so[0][:, :], [sok])
            else:
                for s_, (pdx, dk_) in enumerate(((c["pdA"], c["dkeyA"]), (c["pdB"], c["dkeyB"]))):
                    so = So[nSo % 4]
                    sok = ("So", nSo % 4)
                    nSo += 1
                    S.op("dve", _mk(lambda e, o, a, g, b_: e.scalar_tensor_tensor(o, a, g, b_, ALU.mult, ALU.add),
                                    so[0][:, :], pdx, G64[h], S0[:, s_, hl, :]),
                         reads=[dk_, ("S0", s_, hl)], writes=[sok])
                    k.st("So%d" % (int(sok[1])), k.ret_s[s_, h], so[0][:, :], [sok])

        def st_C(tt, hl):
            h = 2 * hp + hl
            c = ctx.pop((tt, hl))
            okey, pO = c["okey"], c["pO"]
            ssa = st_[:, h * NT + tt:h * NT + tt + 1]
            S.op("act", _mk(lambda e, o, a, s: e.activation(out=o, in_=a, func=AF.Square, accum_out=s), junk[:, :], pO, ssa),
                 reads=[okey], writes=[("p4junk",), ("p4ss", tt, h)])
            S.op("dve", _mk(lambda e, o, a, g: e.tensor_tensor(o, a, g, ALU.mult),
                            go[hl][0][:, tt, :], pO, sg[:, tt, hl * DV:(hl + 1) * DV]),
                 reads=[okey, ("p4ss", tt, h), ("sg", hp % 2, tt)], writes=[("go", hl, tt)])

        for tt in range(NT):
            for hl in range(2):
                st_A(tt, hl)
            for hl in range(2):
                st_B(tt, hl)
            for hl in range(2):
                st_C(tt, hl)
        if hp + 1 < 4:
            gr_proj(hp + 1)
        for hl in range(2):
            h = 2 * hp + hl
            ssv = st_[:, h * NT:(h + 1) * NT]
            rsv = st_[:, NH * NT + h * NT:NH * NT + (h + 1) * NT]
            S.op("act", _mk(lambda e, s, ep: e.activation(out=s, in_=s, func=AF.Sqrt, bias=ep, scale=1.0 / DV), ssv, k.eps_t[:, :]),
                 reads=[("p4ss", tt, h) for tt in range(NT)] + [("eps",)], writes=[("p4ssv", h)])
            S.op("dve", _mk(lambda e, o, a: e.reciprocal(o, a), rsv, ssv), reads=[("p4ssv", h)], writes=[("p4rs", h)])
            S.op("dve", _mk(lambda e, o, r: e.tensor_tensor(o, o, r, ALU.mult), go[hl][0][:, :, :],
                            rsv.unsqueeze(2).to_broadcast([128, NT, DV])),
                 reads=[("go", hl, tt) for tt in range(NT)] + [("p4rs", h)], writes=[("go", hl, tt) for tt in range(NT)])
        for hl in range(2):
            h = 2 * hp + hl
            for c2 in range(2):
                for (t0, n) in ((0, 4), (4, 4), (8, 1)):
                    bank = 2 + nps % 2
                    nps += 1
                    bkey = ("ps", bank)
                    pb = k.ps[bank][:, :].bitcast(BF16)
                    for j in range(n):
                        _tr(S, pb[:, j * 128:(j + 1) * 128], go[hl][0][:, t0 + j, c2 * 128:(c2 + 1) * 128], k.ident_b[:, :],
                            [("go", hl, t0 + j), ("ident_b",)], [bkey], counted=(j == n - 1))
                    S.op("act", _mk(lambda e, o, a: e.copy(o, a),
                                    vg[:, h, c2 * T + t0 * 128:c2 * T + (t0 + n) * 128], pb[:, 0:n * 128]),
                         reads=[bkey], writes=[("vg", h, t) for t in range(NT)] + [("goT", h)])
    k.dump("goT", vg[:], [128, NH, NT * DV], [("goT", h) for h in range(NH)], BF16)
    for lst in (ktT, go, sT, qA, qB, So):
        for t_, u in lst:
            M.free(u)
    for u in (sg_u, sgb[1][1], Sbf_u, S0_u, S0b_u, junk_u, st_u, k.qT_u, k.kt_u, k.Rp_u):
        M.free(u)
    k.end_phase()


UL = 1200


def phase2(k):
    S, M = k.S, k.M
    hT, hTh = k.hT, k.hTh
    pyT, k.pyT_u = M.alloc("pyT", [128, 8, T], BF16)
    k.pyT = pyT
    Uc = [M.alloc("Uc%d" % i, [128, UL], F32) for i in range(2)]
    W1, W1_u = M.alloc("pW1", [128, UL], F32)
    W2, W2_u = M.alloc("pW2", [128, UL], F32)
    dT = [M.alloc("dT%d" % i, [128, 2, T], BF16) for i in range(2)]
    cl, cl_u = M.alloc("cl", [16, 2, PW], F32)
    wp, wp_u = M.alloc("wp", [128, 4, 2, 256], BF16)
    save, save_u = M.alloc("psave", [128, 8, 3, 16], F32)
    stage, stage_u = M.alloc("pstage", [16, PW], F32)
    t16, t16_u = M.alloc("pt16", [128, 16], F32)
    for s_ in range(2):
        k.ld("cl", cl[0:LAG, s_, :], k.cpool[s_], [("cl",)])
    tok = S.dma("pool", "wp", _mk(lambda e, o, s: e.dma_start(out=o, in_=s), wp[:],
                                  k.w_pool.rearrange("g (cc p) d -> p g cc d", p=128)), writes=[("wp",)])
    k.phase_toks.append(tok)
    for i in range(2):
        S.op("pool", _mk(lambda e, o: e.memset(o, 0.0), Uc[i][0][:, :]), writes=[("Uc", i)])
    groups = [(None, 0, 16, 0), (hT, 0, 512, 16), (hT, 512, 512, 528), (hT, 1024, 64, 1056), (hT, 1088, 64, 1136)]
    mains = [(16, 1024, 0), (1056, 64, 1024), (1136, 64, 1088)]
    nb = 0
    wt = wkey = None
    for uc in range(8):
        if uc % 4 == 0:
            wt, wkey, _ = k.wnext()
        g = uc // 2
        cc = uc % 2
        ub = uc % 2
        U = Uc[ub][0]
        ukey = ("Uc", ub)
        msl = slice((uc % 4) * 128, (uc % 4 + 1) * 128)
        for (src, t0, n, c0) in groups:
            bank = nb % 4
            nb += 1
            pkey = ("ps", bank)
            for kc in range(KC):
                if src is None:
                    rhs = hTh[:, kc, 0:16]
                    rk = [("hTh", kc // 4)]
                else:
                    rhs = hT[:, kc, t0:t0 + n]
                    rk = hT_keys(t0, n, kc)
                _mm(S, k.ps[bank][:, 0:n], wt[:, kc, msl], rhs, kc == 0, kc == KC - 1, rk + [wkey], [pkey], counted=(kc == KC - 1))
            S.op("act", _mk(lambda e, o, a: e.copy(o, a), U[:, c0:c0 + n], k.ps[bank][:, 0:n]), reads=[pkey], writes=[ukey])
        for s_ in range(2):
            tkey = ("ps", 4)
            _tr(S, k.ps[4][:, s_ * 16:s_ * 16 + LAG], cl[0:LAG, s_, uc * 128:(uc + 1) * 128], k.ident_f[0:LAG, 0:LAG],
                [("cl",), ("ident_f",)], [tkey])
            c1 = 1041 + 80 * s_
            S.op("act", _mk(lambda e, o, a: e.copy(o, a), U[:, c1:c1 + LAG], k.ps[4][:, s_ * 16:s_ * 16 + LAG]),
                 reads=[tkey], writes=[ukey])
        for s3, c1 in enumerate((1025, 1105, 1185)):
            S.op("pool", _mk(lambda e, o, a: e.tensor_copy(o, a), save[:, uc, s3, 0:LAG], U[:, c1:c1 + LAG]),
                 reads=[ukey], writes=[("psave", uc)])
        cur, ckey = U, ukey
        sh = 1
        bufs = [(W1, ("pW", 1)), (W2, ("pW", 2))]
        for lvl in range(g + 1):
            dst, dkey = bufs[lvl % 2]
            lo = 2 * sh - 1
            S.op("dve", _mk(lambda e, o, a, b_: e.tensor_tensor(o, a, b_, ALU.add), dst[:, lo:UL], cur[:, lo:UL], cur[:, lo - sh:UL - sh]),
                 reads=[ckey], writes=[dkey])
            cur, ckey = dst, dkey
            sh *= 2
        w = 2 ** (g + 1)
        dt_ = dT[g % 2][0]
        dkey2 = ("dT", g % 2, cc)
        for (c0, n, t0) in mains:
            S.op("dve", _mk(lambda e, o, a, sc, b_: e.scalar_tensor_tensor(o, a, sc, b_, ALU.mult, ALU.subtract),
                            dt_[:, cc, t0:t0 + n], cur[:, c0:c0 + n], 1.0 / w, U[:, c0:c0 + n]),
                 reads=[ckey, ukey], writes=[dkey2])
        S.op("dve", _mk(lambda e, o, a, b_: e.tensor_tensor(o, a, b_, ALU.mult), t16[:, :], cur[:, 16:32], k.invc[:, g * 16:(g + 1) * 16]),
             reads=[ckey, ("invc",)], writes=[("pt16",)])
        S.op("dve", _mk(lambda e, o, a, b_: e.tensor_tensor(o, a, b_, ALU.subtract), dt_[:, cc, 0:16], t16[:, :], U[:, 16:32]),
             reads=[("pt16",), ukey], writes=[dkey2])
        if cc == 1:
            for dc in range(2):
                for (t0, n) in TBLK:
                    bank = 5 + nb % 2
                    nb += 1
                    pkey = ("ps", bank)
                    for c_ in range(2):
                        _mm(S, k.ps[bank][:, 0:n], wp[:, g, c_, dc * 128:(dc + 1) * 128], dt_[:, c_, t0:t0 + n], c_ == 0, c_ == 1,
                            [("wp",), ("dT", g % 2, c_)], [pkey], counted=(c_ == 1))
                    oc = 2 * g + dc
                    S.op("act", _mk(lambda e, o, a, s: e.activation(out=o, in_=a, func=AF.Copy, scale=s),
                                    pyT[:, oc, t0:t0 + n], k.ps[bank][:, 0:n], k.psc[:, oc:oc + 1]),
                         reads=[pkey, ("psc",)], writes=[("pyT", oc)])
    for s3 in range(3):
        for half in range(2):
            bank = 7 if half == 0 else 4
            bkey = ("ps", bank)
            for j in range(4):
                uc = half * 4 + j
                _tr(S, k.ps[bank][0:LAG, j * 128:(j + 1) * 128], save[:, uc, s3, 0:LAG], k.ident_f[:, :],
                    [("psave", uc), ("ident_f",)], [bkey], counted=(j == 3))
            S.op("act", _mk(lambda e, o, a: e.copy(o, a), stage[0:LAG, half * 512:(half + 1) * 512], k.ps[bank][0:LAG, :]),
                 reads=[bkey], writes=[("pstage",)])
        dst = k.pool_p if s3 == 0 else k.pool_s[s3 - 1]
        k.st("pstage", dst, stage[0:LAG, :], [("pstage",)])
    k.dump("pyT", pyT[:], [128, 8, T], [("pyT", oc) for oc in range(8)], BF16)
    for lst in (Uc, dT):
        for t_, u in lst:
            M.free(u)
    for u in (W1_u, W2_u, cl_u, wp_u, save_u, stage_u, t16_u, k.hTh_u):
        M.free(u)
    k.end_phase()


def goT_rhs(k, kc, t0, n):
    h, c2 = kc // 2, kc % 2
    return k.vg[:, h, c2 * T + t0:c2 * T + t0 + n]


def phase5(k):
    S, M = k.S, k.M
    hT, pyT = k.hT, k.pyT
    mT, k.mT_u = M.alloc("mT", [128, KC, T], BF16)
    k.mT = mT
    sigA, sigA_u = M.alloc("sigA", [128, 4, T], BF16)
    part1, part1_u = M.alloc("part1", [128, 4, T], F32)
    tm = [M.alloc("p5t%d" % i, [128, 512], F32) for i in range(2)]
    nb = 0
    nt = 0
    for j in range(4):
        for stage in range(4):
            wt, wkey, blk = k.wnext()
            nk = blk[3]
            for oc4 in range(4):
                msl = slice(oc4 * 128, (oc4 + 1) * 128)
                for (t0, n) in TBLK:
                    bank = nb % 6
                    nb += 1
                    pkey = ("ps", bank)
                    ps = k.ps[bank][:, 0:n]
                    for kc in range(nk):
                        if stage in (0, 2):
                            rhs, rk = hT[:, kc, t0:t0 + n], hT_keys(t0, n, kc)
                        elif stage == 1:
                            rhs, rk = pyT[:, kc, t0:t0 + n], [("pyT", kc)]
                        else:
                            rhs, rk = goT_rhs(k, kc, t0, n), [("goT", kc // 2)]
                        _mm(S, ps, wt[:, kc, msl], rhs, kc == 0, kc == nk - 1, rk + [wkey], [pkey], counted=(kc == nk - 1))
                    skey = ("sigA", oc4, t0)
                    if stage in (0, 2):
                        S.op("act", _mk(lambda e, o, a: e.activation(out=o, in_=a, func=AF.Sigmoid), sigA[:, oc4, t0:t0 + n], ps),
                             reads=[pkey], writes=[skey])
                    elif stage == 1:
                        S.op("dve", _mk(lambda e, o, a, b_: e.tensor_tensor(o, a, b_, ALU.mult), part1[:, oc4, t0:t0 + n], ps, sigA[:, oc4, t0:t0 + n]),
                             reads=[pkey, skey], writes=[("part1", oc4, t0)])
                    else:
                        tb_ = tm[nt % 2]
                        tkey = ("p5t", nt % 2)
                        nt += 1
                        S.op("dve", _mk(lambda e, o, a, b_: e.tensor_tensor(o, a, b_, ALU.mult), tb_[0][:, 0:n], ps, sigA[:, oc4, t0:t0 + n]),
                             reads=[pkey, skey], writes=[tkey])
                        S.op("pool", _mk(lambda e, o, a, b_: e.tensor_tensor(o, a, b_, ALU.add), mT[:, 4 * j + oc4, t0:t0 + n], tb_[0][:, 0:n], part1[:, oc4, t0:t0 + n]),
                             reads=[tkey, ("part1", oc4, t0)], writes=[("mT", 4 * j + oc4, t0)])
    k.dump("mT", mT[:], [128, KC, T], [("mT", oc, t0) for oc in range(KC) for (t0, n) in TBLK], BF16)
    for t_, u in tm:
        M.free(u)
    for u in (sigA_u, part1_u, k.hT_u, k.pyT_u, k.vg_u):
        M.free(u)
    k.end_phase()


def phase6(k):
    S, M = k.S, k.M
    mT = k.mT
    h2T, k.h2T_u = M.alloc("h2T", [128, KC, T], BF16)
    k.h2T = h2T
    gpo, gpo_u = M.alloc("gpo", [128, D], F32)
    k.ld("gpo", gpo[:], k.g_post_mix, [("gpo",)])
    wex = [M.alloc("woex%d" % i, [128, KC, 512], BF16) for i in range(2)]
    xt = [M.alloc("p6x%d" % i, [128, D], F32) for i in range(2)]
    x1 = [M.alloc("p6x1%d" % i, [128, D], F32) for i in range(2)]
    xn = [M.alloc("p6xn%d" % i, [128, D], BF16) for i in range(2)]
    st_, st_u = M.alloc("p6stat", [128, 8 * NT], F32)
    j0 = k.wcur
    assert j0 == k.n_pre_wo and j0 <= k.wloaded <= j0 + 2, (j0, k.wloaded, k.n_pre_wo)
    already = k.wloaded - j0
    wo = []
    for i in range(4):
        if i < 2:
            sl = (j0 + i) % NSLOT
            tile_, key, chan = k.wslot[sl], ("w", sl), "w%d" % sl
        else:
            tile_, key, chan = wex[i - 2][0], ("woex", i - 2), "woex%d" % (i - 2)
        ap2d, r0, c0, nk = k.blocks[j0 + i]
        src = ap2d[r0:r0 + nk * 128, c0:c0 + 512].rearrange("(kc p) n -> p kc n", p=128)
        if i >= already:
            tok = S.dma("pool", chan, _mk(lambda e, o, s: e.dma_start(out=o, in_=s), tile_[:, :, :], src), writes=[key])
            if i >= 2:
                k.phase_toks.append(tok)
        wo.append((tile_, key))
    k.wcur = j0 + 4
    k.wloaded = j0 + 4
    junk6, junk6_u = M.alloc("p6junk", [128, 512], BF16)

    def stage_mm(tt):
        b = tt % 2
        tsl = slice(tt * 128, (tt + 1) * 128)
        xa, xkey = xt[b][0], ("p6x", b)
        k.ld("p6x%d" % b, xa[:], k.x[tsl, :], [xkey])
        x1a, x1key = x1[b][0], ("p6x1", b)
        base = 8 * tt
        for cb in range(4):
            bank = cb
            pkey = ("ps", bank)
            csl = slice(cb * 512, (cb + 1) * 512)
            for kc in range(KC):
                _mm(S, k.ps[bank][:, :], mT[:, kc, tsl], wo[cb][0][:, kc, :], kc == 0, kc == KC - 1,
                    [("mT", kc, (tt * 128) // 512 * 512), wo[cb][1]], [pkey], counted=(kc == KC - 1))
            S.op("act", _mk(lambda e, o, a, s: e.activation(out=o, in_=a, func=AF.Square, accum_out=s),
                            junk6[:, :], k.ps[bank][:, :], st_[:, base + cb:base + cb + 1]),
                 reads=[pkey], writes=[("p6junk",), ("p6ss", tt, cb)])
            S.op("dve", _mk(lambda e, o, a, g: e.tensor_tensor(o, a, g, ALU.mult), x1a[:, csl], k.ps[bank][:, :], gpo[:, csl]),
                 reads=[pkey, ("p6ss", tt, cb), ("gpo",)], writes=[(x1key, cb)])

    def stage_post(tt):
        b = tt % 2
        tsl = slice(tt * 128, (tt + 1) * 128)
        xa, xkey = xt[b][0], ("p6x", b)
        x1a, x1key = x1[b][0], ("p6x1", b)
        xna, xnkey = xn[b][0], ("p6xn", b)
        base = 8 * tt
        x1all = [(x1key, cb) for cb in range(4)]
        ss = st_[:, base + 4:base + 5]
        rs = st_[:, base + 5:base + 6]
        S.op("dve", _mk(lambda e, o, a: e.reduce_sum(o, a, AX.X), ss, st_[:, base:base + 4]),
             reads=[("p6ss", tt, cb) for cb in range(4)], writes=[("p6s", tt)])
        S.op("act", _mk(lambda e, s, ep: e.activation(out=s, in_=s, func=AF.Sqrt, bias=ep, scale=1.0 / D), ss, k.eps_t[:, :]),
             reads=[("p6s", tt), ("eps",)], writes=[("p6s", tt)])
        S.op("dve", _mk(lambda e, o, a: e.reciprocal(o, a), rs, ss), reads=[("p6s", tt)], writes=[("p6r", tt)])
        S.op("dve", _mk(lambda e, o, s, b_: e.scalar_tensor_tensor(o, o, s, b_, ALU.mult, ALU.add), x1a[:, :], rs, xa[:, :]),
             reads=x1all + [("p6r", tt), xkey], writes=x1all + [(x1key, "f")])
        k.st("p6x1_%d" % b, k.x1_d[tsl, :], x1a[:, :], x1all + [(x1key, "f")], writes=[("x1d", tt)])
        ss2 = st_[:, base + 6:base + 7]
        rs2 = st_[:, base + 7:base + 8]
        S.op("act", _mk(lambda e, o, a, s: e.activation(out=o, in_=a, func=AF.Square, accum_out=s), xna[:, :], x1a[:, :], ss2),
             reads=x1all + [(x1key, "f")], writes=[xnkey, ("p6s2", tt)])
        S.op("act", _mk(lambda e, s, ep: e.activation(out=s, in_=s, func=AF.Sqrt, bias=ep, scale=1.0 / D), ss2, k.eps_t[:, :]),
             reads=[("p6s2", tt), ("eps",)], writes=[("p6s2", tt)])
        S.op("dve", _mk(lambda e, o, a: e.reciprocal(o, a), rs2, ss2), reads=[("p6s2", tt)], writes=[("p6r2", tt)])
        S.op("dve", _mk(lambda e, o, a, s: e.tensor_scalar(o, a, s, None, ALU.mult), xna[:, :], x1a[:, :], rs2),
             reads=x1all + [(x1key, "f"), ("p6r2", tt)], writes=[xnkey])

    def stage_post_b(tt):
        b = tt % 2
        tsl = slice(tt * 128, (tt + 1) * 128)
        xna, xnkey = xn[b][0], ("p6xn", b)
        for g4 in range(4):
            bank = 4 + (g4 % 2) + 2 * b
            bkey = ("ps", bank)
            pb = k.ps[bank][:, :].bitcast(BF16)
            for j in range(4):
                kc = g4 * 4 + j
                _tr(S, pb[:, j * 128:(j + 1) * 128], xna[:, kc * 128:(kc + 1) * 128], k.ident_b[:, :],
                    [xnkey, ("ident_b",)], [bkey], counted=(j == 3))
            gb = k.gpf[:, g4 * 4:g4 * 4 + 4].unsqueeze(2).to_broadcast([128, 4, 128])
            S.op("dve", _mk(lambda e, o, a, g: e.tensor_tensor(o, a, g, ALU.mult), h2T[:, g4 * 4:g4 * 4 + 4, tsl],
                            pb[:, 0:512].rearrange("p (j t) -> p j t", j=4), gb),
                 reads=[bkey, ("gpf",)], writes=[("h2T", tt, g4)])

    stage_mm(0)
    for tt in range(1, NT):
        stage_post(tt - 1)
        stage_mm(tt)
        stage_post_b(tt - 1)
    stage_post(NT - 1)
    stage_post_b(NT - 1)
    M.free(junk6_u)
    k.dump("h2T", h2T[:], [128, KC, T], [("h2T", tt, g4) for tt in range(NT) for g4 in range(4)], BF16)
    for lst in (wex, xt, x1, xn):
        for t_, u in lst:
            M.free(u)
    for u in (gpo_u, st_u, k.mT_u):
        M.free(u)
    k.end_phase()


def h2T_keys(t0, n, kc):
    return [("h2T", t, kc // 4) for t in range(t0 // 128, (t0 + n + 127) // 128)]


def phase7(k):
    S, M = k.S, k.M
    h2T = k.h2T
    NFC = FG // 128
    f1T, f1T_u = M.alloc("f1T", [128, NFC, T], BF16)
    facc, facc_u = M.alloc("facc", [128, NT, D], F32)
    gpo, gpo_u = M.alloc("gpo2", [128, D], F32)
    k.ld("gpo2", gpo[:], k.g_post_ffn, [("gpo2",)])
    rt = [M.alloc("p7r%d" % i, [128, 512], F32) for i in range(2)]
    x1r = [M.alloc("p7x1%d" % i, [128, D], F32) for i in range(2)]
    junk, junk_u = M.alloc("p7junk", [128, D], BF16)
    st_, st_u = M.alloc("p7stat", [128, 2 * NT], F32)
    nb = 0
    nr = 0
    ngrp = DFF // FG
    for g in range(ngrp):
        for b2 in range(FG // 512):
            wt, wkey, _ = k.wnext()
            for oc4 in range(4):
                fc = b2 * 4 + oc4
                msl = slice(oc4 * 128, (oc4 + 1) * 128)
                for (t0, n) in TBLK:
                    bank = nb % 4
                    nb += 1
                    pkey = ("ps", bank)
                    ps = k.ps[bank][:, 0:n]
                    for kc in range(KC):
                        _mm(S, ps, wt[:, kc, msl], h2T[:, kc, t0:t0 + n], kc == 0, kc == KC - 1,
                            h2T_keys(t0, n, kc) + [wkey], [pkey], counted=(kc == KC - 1))
                    r_ = rt[nr % 2]
                    rkey = ("p7r", nr % 2)
                    nr += 1
                    S.op("act", _mk(lambda e, o, a: e.activation(out=o, in_=a, func=AF.Relu), r_[0][:, 0:n], ps),
                         reads=[pkey], writes=[rkey])
                    S.op("dve", _mk(lambda e, o, a, b_: e.scalar_tensor_tensor(o, a, 0.0, b_, ALU.max, ALU.mult), f1T[:, fc, t0:t0 + n], ps, r_[0][:, 0:n]),
                         reads=[pkey, rkey], writes=[("f1T", fc, t0)])
        for cb in range(4):
            wt, wkey, blk = k.wnext()
            nk = blk[3]
            csl = slice(cb * 512, (cb + 1) * 512)
            for tt in range(NT):
                tsl = slice(tt * 128, (tt + 1) * 128)
                bank = 4 + nb % 4
                nb += 1
                pkey = ("ps", bank)
                for kc in range(nk):
                    _mm(S, k.ps[bank][:, :], f1T[:, kc, tsl], wt[:, kc, :], kc == 0, kc == nk - 1,
                        [("f1T", kc, (tt * 128) // 512 * 512), wkey], [pkey], counted=(kc == nk - 1))
                fkey = ("facc", tt, cb)
                if g == 0:
                    S.op("act", _mk(lambda e, o, a: e.copy(o, a), facc[:, tt, csl], k.ps[bank][:, :]), reads=[pkey], writes=[fkey])
                else:
                    S.op("dve", _mk(lambda e, o, a: e.tensor_tensor(o, o, a, ALU.add), facc[:, tt, csl], k.ps[bank][:, :]),
                         reads=[pkey, fkey], writes=[fkey])
    def ld_x1(t):
        bb = t % 2
        k.ld("p7x1_%d" % bb, x1r[bb][0][:], k.x1_d[t * 128:(t + 1) * 128, :], [("p7x1", bb)])

    ld_x1(0)
    ld_x1(1)
    for tt in range(NT):
        b = tt % 2
        tsl = slice(tt * 128, (tt + 1) * 128)
        fk = [("facc", tt, cb) for cb in range(4)]
        xr, xrkey = x1r[b][0], ("p7x1", b)
        ss = st_[:, 2 * tt:2 * tt + 1]
        rs = st_[:, 2 * tt + 1:2 * tt + 2]
        S.op("act", _mk(lambda e, o, a, s: e.activation(out=o, in_=a, func=AF.Square, accum_out=s), junk[:, :], facc[:, tt, :], ss),
             reads=fk, writes=[("p7junk",), ("p7s", tt)])
        S.op("act", _mk(lambda e, s, ep: e.activation(out=s, in_=s, func=AF.Sqrt, bias=ep, scale=1.0 / D), ss, k.eps_t[:, :]),
             reads=[("p7s", tt), ("eps",)], writes=[("p7s", tt)])
        S.op("dve", _mk(lambda e, o, a: e.reciprocal(o, a), rs, ss), reads=[("p7s", tt)], writes=[("p7rs", tt)])
        S.op("dve", _mk(lambda e, o, s, g_: e.scalar_tensor_tensor(o, o, s, g_, ALU.mult, ALU.mult), facc[:, tt, :], rs, gpo[:, :]),
             reads=fk + [("p7rs", tt), ("gpo2",)], writes=fk)
        S.op("pool", _mk(lambda e, o, b_: e.tensor_tensor(o, o, b_, ALU.add), facc[:, tt, :], xr[:, :]),
             reads=fk + [xrkey], writes=fk + [("yt", tt)])
        if tt + 2 < NT:
            ld_x1(tt + 2)
        k.st("y%d" % b, k.y[tsl, :], facc[:, tt, :], [("yt", tt)])
    for lst in (rt, x1r):
        for t_, u in lst:
            M.free(u)
    for u in (f1T_u, facc_u, gpo_u, junk_u, st_u, k.h2T_u):
        M.free(u)
    k.end_phase()
```

```python
import bisect
from contextlib import ExitStack

import numpy as np
import concourse.bass as bass
import concourse.mybir as mybir
from concourse.bass_utils import run_bass_kernel_spmd

F32 = mybir.dt.float32
BF16 = mybir.dt.bfloat16
ALU = mybir.AluOpType
AF = mybir.ActivationFunctionType
AX = mybir.AxisListType

NCORES = 8
D = 2048
KC = D // 128
TP = 1024
NT = 9
T = NT * 128
SEQ = 8192
DEC_B, DEC_T = 16, 64
PAST = 2048
PW = 1024
LAG = 15
NH, DK, DV = 8, 128, 256
IN_W = 11264
DFF = 8192
EPS = 1e-6
OFF_U, OFF_Q, OFF_K, OFF_V, OFF_G, OFF_AP, OFF_AR = 0, 1024, 2048, 3072, 5120, 7168, 9216
TBLK = [(0, 512), (512, 512), (1024, 128)]
FG = 1024
NPRE = 56
NSLOT = 2
LOG_GAMMA = [float(np.log1p(-np.exp2(np.float32(-5.0 - h)))) for h in range(NH)]
G128 = [float(np.exp(128.0 * lg)) for lg in LOG_GAMMA]
G64 = [float(np.exp(64.0 * lg)) for lg in LOG_GAMMA]

ENGS = ("pe", "act", "dve", "pool", "sp")


class Sched:
    def __init__(self, nc, es):
        self.nc = nc
        self.es = es
        self.items = {e: [] for e in ENGS}
        self.npos = {e: 0 for e in ENGS}
        self.counted = {e: [] for e in ENGS}
        self.known = {e: {} for e in ENGS}
        self.lw = {}
        self.rd = {}
        self.dma_val = {}
        self.dma_sem = {}
        self.esem = {e: es.enter_context(nc.semaphore("c_" + e)) for e in ENGS}
        self.nwaits = 0
        self.nops = 0
        self.simsem = {}

    def _deps(self, reads, writes):
        deps = []
        for r in reads:
            t = self.lw.get(r)
            if t is not None:
                deps.append(t)
        for w in writes:
            t = self.lw.get(w)
            if t is not None:
                deps.append(t)
            deps.extend(self.rd.get(w, ()))
        return deps

    def _commit(self, tok, reads, writes):
        for r in reads:
            self.rd.setdefault(r, []).append(tok)
        for w in writes:
            self.lw[w] = tok
            self.rd[w] = []

    def op(self, eng, fn, reads=(), writes=(), counted=True):
        reads, writes = list(reads), list(writes)
        deps = self._deps(reads, writes)
        pos = self.npos[eng]
        self.npos[eng] += 1
        if counted:
            self.counted[eng].append(pos)
        it = dict(kind="op", fn=fn, deps=deps, counted=counted, pos=pos)
        self.items[eng].append(it)
        self._commit(("e", eng, pos), reads, writes)
        self.nops += 1
        return it

    def dma(self, queue, chan, fn, reads=(), writes=(), inc=16):
        reads, writes = list(reads), list(writes)
        deps = self._deps(reads, writes)
        if chan not in self.dma_sem:
            self.dma_sem[chan] = self.es.enter_context(self.nc.semaphore("d_" + chan))
            self.dma_val[chan] = 0
        self.dma_val[chan] += inc
        tok = ("d", chan, self.dma_val[chan])
        it = dict(kind="dma", fn=fn, deps=deps, chan=chan, inc=inc)
        self.items[queue].append(it)
        self._commit(tok, reads, writes)
        return tok

    def wait_tok(self, eng, toks):
        self.items[eng].append(dict(kind="wait", deps=list(toks)))

    def _resolve(self, tok):
        if tok[0] == "d":
            return ("d", tok[1]), self.dma_sem[tok[1]], tok[2]
        _, eng, pos = tok
        lst = self.counted[eng]
        i = bisect.bisect_left(lst, pos)
        assert i < len(lst), ("dependency on trailing uncounted op", eng, pos)
        return ("e", eng), self.esem[eng], i + 1

    def flush(self):
        for e in ENGS:
            ops = [it for it in self.items[e] if it["kind"] == "op"]
            if ops and not ops[-1]["counted"]:
                ops[-1]["counted"] = True
                bisect.insort(self.counted[e], ops[-1]["pos"])
        nc = self.nc
        simlog = {e: [] for e in ENGS}
        with nc.Block() as block:
            def emit(ename, eng):
                known = self.known[ename]
                for it in self.items[ename]:
                    for tok in it["deps"]:
                        if tok[0] == "e" and tok[1] == "pe" and ename == "pe":
                            continue
                        key, sem, val = self._resolve(tok)
                        if known.get(key, 0) >= val:
                            continue
                        eng.wait_ge(sem, val)
                        known[key] = val
                        self.nwaits += 1
                        simlog[ename].append(("wait", key, val))
                    if it["kind"] == "wait":
                        continue
                    ins = it["fn"](eng)
                    if it["kind"] == "dma":
                        ins.then_inc(self.dma_sem[it["chan"]], it["inc"])
                        simlog[ename].append(("inc", ("d", it["chan"]), it["inc"]))
                    elif it["counted"]:
                        ins.then_inc(self.esem[ename], 1)
                        simlog[ename].append(("inc", ("e", ename), 1))

            @block.tensor
            def _(eng):
                emit("pe", eng)

            @block.scalar
            def _(eng):
                emit("act", eng)

            @block.vector
            def _(eng):
                emit("dve", eng)

            @block.gpsimd
            def _(eng):
                emit("pool", eng)

            @block.sync
            def _(eng):
                emit("sp", eng)
        self._simulate(simlog)
        for e in ENGS:
            for e2 in ENGS:
                self.known[e][("e", e2)] = len(self.counted[e2])
        self.items = {e: [] for e in ENGS}
        self.lw = {k: v for k, v in self.lw.items() if v[0] == "d"}
        self.rd = {k: [t for t in v if t[0] == "d"] for k, v in self.rd.items()}
        self.rd = {k: v for k, v in self.rd.items() if v}


def _sched_simulate(self, simlog):
    sem = dict(self.simsem)
    pc = {e: 0 for e in ENGS}
    progress = True
    while progress:
        progress = False
        for e in ENGS:
            lst = simlog[e]
            while pc[e] < len(lst):
                kind, key, val = lst[pc[e]]
                if kind == "wait":
                    if sem.get(key, 0) < val:
                        break
                else:
                    sem[key] = sem.get(key, 0) + val
                pc[e] += 1
                progress = True
    stuck = {e: (pc[e], len(simlog[e]), simlog[e][pc[e]]) for e in ENGS if pc[e] < len(simlog[e])}
    assert not stuck, ("DEADLOCK in phase", stuck, {k_: sem.get(k_) for k_ in [v[2][1] for v in stuck.values()]})
    self.simsem = sem


Sched._simulate = _sched_simulate


class Mem:
    BASE = 16512
    LIMIT = 229376

    def __init__(self, nc):
        self.nc = nc
        self.live = {}
        self.pending = []
        self.n = 0
        self.peak = 0

    def alloc(self, name, shape, dtype):
        esz = 2 if dtype == BF16 else 4
        size = int(np.prod(shape[1:])) * esz
        size = (size + 31) // 32 * 32
        segs = sorted(self.live.values())
        off = self.BASE
        for (o, s) in segs:
            if off + size <= o:
                break
            off = max(off, o + s)
        assert off + size <= self.LIMIT, ("SBUF OOM", name, size, off, sorted(self.live.items(), key=lambda kv: kv[1]))
        self.n += 1
        uname = "%s_%d" % (name, self.n)
        self.live[uname] = (off, size)
        self.peak = max(self.peak, off + size)
        t = self.nc.alloc_sbuf_tensor_at(uname, list(shape), dtype, offset=off)
        return t, uname

    def free(self, uname):
        self.pending.append(uname)

    def commit(self):
        for u in self.pending:
            self.live.pop(u, None)
        self.pending = []


class K:
    pass


def _mk(fn, *a, **kw):
    return lambda eng: fn(eng, *a, **kw)


def build_program(stop_after=99, dbg=()):
    nc = bass.Bass("TRN2", target_bir_lowering=False)
    es = ExitStack()
    S = Sched(nc, es)
    M = Mem(nc)
    k = K()
    k.nc, k.S, k.M, k.es = nc, S, M, es
    k.dbg = set(dbg)
    k.dbg_outs = []

    def din(name, shape):
        return nc.dram_tensor(name, list(shape), F32, kind="ExternalInput").ap()

    def dout(name, shape):
        return nc.dram_tensor(name, list(shape), F32, kind="ExternalOutput").ap()

    k.x = din("x", [T, D])
    k.xh = din("xh", [16, D])
    k.xprev = din("xprev", [NPRE * 128, D])
    k.t_cosp = din("t_cosp", [NPRE * 128, 64])
    k.t_sinp = din("t_sinp", [NPRE * 128, 64])
    k.cpool = din("cpool", [2, LAG, PW])
    k.sret = din("sret", [2, NH, DK, DV])
    k.w_in = din("w_in", [D, IN_W])
    k.w_pool = din("w_pool", [4, 256, 256])
    k.w_pool_out = din("w_pool_out", [PW, D])
    k.w_ret_out = din("w_ret_out", [D, D])
    k.w_o = din("w_o", [D, D])
    k.w_up = din("w_up", [D, DFF])
    k.w_down = din("w_down", [DFF, D])
    k.g_pre_mix = din("g_pre_mix", [128, KC])
    k.g_pre_ffn = din("g_pre_ffn", [128, KC])
    k.pscale = din("pscale", [128, 8])
    k.g_post_mix = din("g_post_mix", [128, D])
    k.g_post_ffn = din("g_post_ffn", [128, D])
    k.t_cos = din("t_cos", [T, 64])
    k.t_sin = din("t_sin", [T, 64])
    k.t_dec = din("t_dec", [128, 32])
    k.t_mask = din("t_mask", [128, 256])
    k.t_coef = din("t_coef", [128, 64])
    k.t_invc = din("t_invc", [128, 64])
    k.t_ident = din("t_ident", [128, 128])

    k.y = dout("y", [T, D])
    k.pool_p = dout("pool_p", [LAG, PW])
    k.ret_p = dout("ret_p", [NH, DK, DV])
    k.pool_s = dout("pool_s", [2, LAG, PW])
    k.ret_s = dout("ret_s", [2, NH, DK, DV])

    k.x1_d = nc.dram_tensor("x1_scratch", [T, D], F32).ap()

    k.ps = [nc.alloc_psum_tensor("ps%d" % i, [128, 512], F32) for i in range(8)]
    k.wslot = [M.alloc("wslot%d" % i, [128, KC, 512], BF16)[0] for i in range(NSLOT)]
    k.ident_f, _ = M.alloc("ident_f", [128, 128], F32)
    k.ident_b, _ = M.alloc("ident_b", [128, 128], BF16)
    k.eps_t, _ = M.alloc("eps_t", [128, 1], F32)
    k.gpm, _ = M.alloc("gpm", [128, KC], F32)
    k.gpf, _ = M.alloc("gpf", [128, KC], F32)
    k.psc, _ = M.alloc("psc", [128, 8], F32)
    k.dec, _ = M.alloc("dec", [128, 32], F32)
    k.mask, _ = M.alloc("mask", [128, 256], F32)
    k.coef, _ = M.alloc("coef", [128, 64], F32)
    k.invc, _ = M.alloc("invc", [128, 64], F32)

    k.phase_toks = []

    def ld(chan, dst, src, writes, queue="sp"):
        tok = S.dma(queue, chan, lambda e: e.dma_start(out=dst, in_=src), writes=writes)
        k.phase_toks.append(tok)
        return tok
    k.ld = ld

    def st(chan, dst, src, reads, queue="sp", writes=()):
        tok = S.dma(queue, chan, lambda e: e.dma_start(out=dst, in_=src), reads=reads, writes=writes)
        k.phase_toks.append(tok)
        return tok
    k.st = st

    def end_phase():
        S.wait_tok("sp", k.phase_toks)
        k.phase_toks = []
        S.flush()
        M.commit()
    k.end_phase = end_phase

    def dump(name, sb_ap, shape, reads, dtype=F32):
        if name not in k.dbg:
            return
        d = nc.dram_tensor("dbg_" + name, list(shape), dtype, kind="ExternalOutput").ap()
        k.dbg_outs.append("dbg_" + name)
        st("dbg_" + name, d, sb_ap, reads)
    k.dump = dump

    ld("c_identf", k.ident_f[:], k.t_ident, [("ident_f",)])
    ld("c_identb", k.ident_b[:], k.t_ident, [("ident_b",)], queue="pool")
    ld("c_gpm", k.gpm[:], k.g_pre_mix, [("gpm",)])
    ld("c_gpf", k.gpf[:], k.g_pre_ffn, [("gpf",)])
    ld("c_psc", k.psc[:], k.pscale, [("psc",)])
    ld("c_dec", k.dec[:], k.t_dec, [("dec",)])
    ld("c_mask", k.mask[:], k.t_mask, [("mask",)])
    ld("c_coef", k.coef[:], k.t_coef, [("coef",)])
    ld("c_invc", k.invc[:], k.t_invc, [("invc",)])
    S.op("dve", lambda e: e.memset(k.eps_t[:], EPS), writes=[("eps",)])

    blocks = []
    def wblk(ap2d, r0, c0, nk=KC):
        blocks.append((ap2d, r0, c0, nk))
    for c0 in (OFF_Q, OFF_Q + 512, OFF_K, OFF_K + 512):
        wblk(k.w_in, 0, c0)
    for j in range(4):
        wblk(k.w_in, 0, OFF_V + 512 * j)
    for j in range(4):
        wblk(k.w_in, 0, OFF_G + 512 * j)
    for j in range(2):
        wblk(k.w_in, 0, OFF_U + 512 * j)
    for j in range(4):
        wblk(k.w_in, 0, OFF_AP + 512 * j)
        wblk(k.w_pool_out, 0, 512 * j, 8)
        wblk(k.w_in, 0, OFF_AR + 512 * j)
        wblk(k.w_ret_out, 0, 512 * j)
    k.n_pre_wo = len(blocks)
    for j in range(4):
        wblk(k.w_o, 0, 512 * j)
    for g in range(DFF // FG):
        for j in range(FG // 512):
            wblk(k.w_up, 0, g * FG + 512 * j)
        for j in range(4):
            wblk(k.w_down, g * FG, 512 * j, FG // 128)
    k.blocks = blocks
    k.wcur = 0
    k.wloaded = 0

    def wload(j, slot_tile, key, chan):
        ap2d, r0, c0, nk = k.blocks[j]
        src = ap2d[r0:r0 + nk * 128, c0:c0 + 512].rearrange("(kc p) n -> p kc n", p=128)
        S.dma("pool", chan, lambda e: e.dma_start(out=slot_tile[:, 0:nk, :], in_=src), writes=[key])

    def wnext():
        j = k.wcur
        k.wcur += 1
        while k.wloaded < min(j + NSLOT, len(k.blocks)):
            i = k.wloaded
            wload(i, k.wslot[i % NSLOT], ("w", i % NSLOT), "w%d" % (i % NSLOT))
            k.wloaded += 1
        return k.wslot[j % NSLOT], ("w", j % NSLOT), k.blocks[j]
    k.wnext = wnext

    phases = [phase0_prepare, phase1, phase0, phase3, phase4, phase2, phase5, phase6, phase7]
    for i, ph in enumerate(phases):
        if i >= stop_after:
            break
        ph(k)
    if k.phase_toks or any(S.items[e] for e in ENGS):
        end_phase()
    es.close()
    return nc, k


def norm_to_featmajor(k, name, x_src_fn, ntile, rows_fn, g_tile, gkey, hT, hT_key_fn, hT_cols_fn,
                      x_keep=None, nbuf=2):
    S, M = k.S, k.M
    xt = [M.alloc(name + "_xt%d" % i, [128, D], F32) for i in range(nbuf)]
    xn = [M.alloc(name + "_xn%d" % i, [128, D], BF16) for i in range(nbuf)]
    junk = M.alloc(name + "_junk", [128, D], BF16)
    st_ = M.alloc(name + "_stat", [128, 2 * ntile + 2], F32)
    stat = st_[0]
    def stage_A(i):
        rows = rows_fn(i)
        b = i % nbuf
        if x_keep is None:
            xa = xt[b][0][0:rows, :]
            xkey = (name + "_xt", b)
            k.ld(name + "_x%d" % b, xa, x_src_fn(i), [xkey])
        else:
            xa, xkey = x_keep(i)
        ss = stat[0:rows, 2 * i:2 * i + 1]
        rs = stat[0:rows, 2 * i + 1:2 * i + 2]
        skey = (name + "_stat", i)
        S.op("act", _mk(lambda e, o, a, s: e.activation(out=o, in_=a, func=AF.Square, accum_out=s),
                        junk[0][0:rows, :], xa, ss),
             reads=[xkey], writes=[(name + "_junk",), skey])
        S.op("act", _mk(lambda e, s, ep: e.activation(out=s, in_=s, func=AF.Sqrt, bias=ep, scale=1.0 / D),
                        ss, k.eps_t[0:rows, :]),
             reads=[skey, ("eps",)], writes=[skey])
        S.op("dve", _mk(lambda e, o, a: e.reciprocal(o, a), rs, ss), reads=[skey], writes=[(name + "_rs", i)])
        xna = xn[b][0][0:rows, :]
        xnkey = (name + "_xn", b)
        S.op("dve", _mk(lambda e, o, a, s: e.tensor_scalar(o, a, s, None, ALU.mult), xna, xa, rs),
             reads=[xkey, (name + "_rs", i)], writes=[xnkey])

    def stage_B(i):
        rows = rows_fn(i)
        b = i % nbuf
        xnkey = (name + "_xn", b)
        c0, ncol = hT_cols_fn(i)
        for g4 in range(KC // 4):
            bank = (g4 % 2) + 2 * (i % 2)
            bkey = ("ps", bank)
            pb = k.ps[bank][:, :].bitcast(BF16)
            for j in range(4):
                kc = g4 * 4 + j
                S.op("pe", _mk(lambda e, o, a, idn: e.transpose(o, a, idn),
                               pb[:, j * 128:j * 128 + rows], xn[b][0][0:rows, kc * 128:(kc + 1) * 128],
                               k.ident_b[0:rows, 0:rows]),
                     reads=[xnkey, ("ident_b",)], writes=[bkey], counted=(j == 3))
            src = pb[:, 0:512].rearrange("p (j t) -> p j t", j=4)[:, :, 0:rows]
            dst = hT[:, g4 * 4:g4 * 4 + 4, c0:c0 + ncol]
            gb = g_tile[:, g4 * 4:g4 * 4 + 4].unsqueeze(2).to_broadcast([128, 4, rows])
            S.op("dve", _mk(lambda e, o, a, g: e.tensor_tensor(o, a, g, ALU.mult), dst, src, gb),
                 reads=[bkey, gkey], writes=[hT_key_fn(i, g4)])

    stage_A(0)
    for i in range(ntile):
        if i + 1 < ntile:
            stage_A(i + 1)
        stage_B(i)
    for t_, u in xt + xn + [junk, st_]:
        M.free(u)


def phase1(k):
    S, M = k.S, k.M
    hT, k.hT_u = M.alloc("hT", [128, KC, T], BF16)
    hTh, k.hTh_u = M.alloc("hTh", [128, KC, 16], BF16)
    k.hT, k.hTh = hT, hTh

    def rows_fn(i):
        return 16 if i == 0 else 128

    def src_fn(i):
        return k.xh if i == 0 else k.x[(i - 1) * 128:i * 128, :]

    def key_fn(i, g4):
        return ("hTh", g4) if i == 0 else ("hT", i - 1, g4)

    def run(i_list):
        pass
    class _HT:
        def __getitem__(self_, idx):
            raise NotImplementedError
    norm_to_featmajor(k, "p1h", lambda i: k.xh, 1, lambda i: 16, k.gpm, ("gpm",), hTh,
                      lambda i, g4: ("hTh", g4), lambda i: (0, 16), nbuf=1)
    norm_to_featmajor(k, "p1", lambda i: k.x[i * 128:(i + 1) * 128, :], NT, lambda i: 128, k.gpm, ("gpm",), hT,
                      lambda i, g4: ("hT", i, g4), lambda i: (i * 128, 128))
    k.dump("hT", hT[:], [128, KC, T], [("hT", i, g4) for i in range(NT) for g4 in range(4)], BF16)
    k.end_phase()


def hT_keys(t0, ntok, kc):
    return [("hT", t, kc // 4) for t in range(t0 // 128, (t0 + ntok + 127) // 128)]


def phase3(k):
    S, M = k.S, k.M
    hT = k.hT
    qT, k.qT_u = M.alloc("qT", [128, NH, T], BF16)
    kt, k.kt_u = M.alloc("kt", [128, NT, NH * DK], BF16)
    vg, k.vg_u = M.alloc("vg", [128, NH, NT * DV], BF16)
    k.qT, k.kt, k.vg = qT, kt, vg
    cs_c, u_c = M.alloc("cs_c", [128, NT, 64], F32)
    cs_s, u_s = M.alloc("cs_s", [128, NT, 64], F32)
    k.ld("cs_c", cs_c[:], k.t_cos.rearrange("(t p) f -> p t f", p=128), [("cs_c",)])
    k.ld("cs_s", cs_s[:], k.t_sin.rearrange("(t p) f -> p t f", p=128), [("cs_s",)])
    tmp = [[M.alloc("rt%d_%d" % (s, i), [128, 4, 64], F32) for i in range(6)] for s in range(2)]
    qtok = [M.alloc("qtok%d" % s, [128, 512], BF16) for s in range(3)]
    Rl, u_Rl = M.alloc("Rl", [128, NH, DV], F32)
    it = 0
    pend = []
    for blk in range(4):
        wt, wkey, _ = k.wnext()
        is_q = blk < 2
        hb = blk % 2
        for tt in range(NT):
            var = 0 if tt < 8 else 1
            s = it % 2
            bank = it % 4
            it += 1
            ps = k.ps[bank]
            pkey = ("ps", bank)
            for kc in range(KC):
                S.op("pe", _mk(lambda e, o, l, r, st, sp: e.matmul(o, lhsT=l, rhs=r, start=st, stop=sp),
                               ps[:, :], hT[:, kc, tt * 128:(tt + 1) * 128], wt[:, kc, :], kc == 0, kc == KC - 1),
                     reads=[("hT", tt, kc // 4), wkey], writes=[pkey], counted=(kc == KC - 1))
            pv = ps[:, :].rearrange("p (h f two) -> p h f two", h=4, two=2)
            xe, xo = pv[:, :, :, 0], pv[:, :, :, 1]
            cb = cs_c[:, tt, :].unsqueeze(1).to_broadcast([128, 4, 64])
            sb = cs_s[:, tt, :].unsqueeze(1).to_broadcast([128, 4, 64])
            t1, t2, t3, t4, re, ro = [tmp[s][i][0] for i in range(6)]
            tk = [("rt", s, i) for i in range(6)]
            for (o, a, b_, ky) in ((t1, xe, cb, tk[0]), (t2, xo, sb, tk[1]), (t3, xe, sb, tk[2]), (t4, xo, cb, tk[3])):
                S.op("dve", _mk(lambda e, o, a, b_: e.tensor_tensor(o[:], a, b_, ALU.mult), o, a, b_),
                     reads=[pkey, ("cs_c",), ("cs_s",)], writes=[ky])
            S.op("pool", _mk(lambda e, o, a, b_: e.tensor_tensor(o[:], a[:], b_[:], ALU.subtract), re, t1, t2),
                 reads=[tk[0], tk[1]], writes=[tk[4]])
            S.op("pool", _mk(lambda e, o, a, b_: e.tensor_tensor(o[:], a[:], b_[:], ALU.add), ro, t3, t4),
                 reads=[tk[2], tk[3]], writes=[tk[5]])
            di = (0 if is_q else 16) + var * 8 + hb * 4
            db = k.dec[:, di:di + 4].unsqueeze(2).to_broadcast([128, 4, 64])
            qs = it % 3
            if is_q:
                dst2d = qtok[qs][0][:, :]
                dkey = ("qtok", qs)
            else:
                dst2d = kt[:, tt, hb * 512:(hb + 1) * 512]
                dkey = ("kt", tt, hb)
            dv_ = dst2d.rearrange("p (h f two) -> p h f two", h=4, two=2)
            S.op("pool", _mk(lambda e, o, a, b_: e.tensor_tensor(o, a[:], b_, ALU.mult), dv_[:, :, :, 0], re, db),
                 reads=[tk[4], ("dec",)], writes=[dkey])
            S.op("pool", _mk(lambda e, o, a, b_: e.tensor_tensor(o, a[:], b_, ALU.mult), dv_[:, :, :, 1], ro, db),
                 reads=[tk[5], ("dec",)], writes=[dkey])
            if is_q:
                pend.append((qs, hb, tt, dkey, 4 + (it % 2)))
            while pend and (not is_q or len(pend) > 2 or tt == NT - 1):
                s_, hb_, tt_, dkey_, tb = pend.pop(0)
                pb = k.ps[tb][:, :].bitcast(BF16)
                tkey = ("ps", tb)
                for j in range(4):
                    S.op("pe", _mk(lambda e, o, a, idn: e.transpose(o, a, idn),
                                   pb[:, j * 128:(j + 1) * 128], qtok[s_][0][:, j * 128:(j + 1) * 128], k.ident_b[:, :]),
                         reads=[dkey_, ("ident_b",)], writes=[tkey], counted=(j == 3))
                S.op("act", _mk(lambda e, o, a: e.copy(o, a),
                                qT[:, hb_ * 4:hb_ * 4 + 4, tt_ * 128:(tt_ + 1) * 128],
                                pb[:, 0:512].rearrange("p (j t) -> p j t", j=4)),
                     reads=[tkey], writes=[("qT", hb_ * 4 + j, tt_) for j in range(4)])
    for blk in range(4):
        wt, wkey, _ = k.wnext()
        for tt in range(NT):
            bank = it % 4
            it += 1
            ps = k.ps[bank]
            pkey = ("ps", bank)
            for kc in range(KC):
                S.op("pe", _mk(lambda e, o, l, r, st, sp: e.matmul(o, lhsT=l, rhs=r, start=st, stop=sp),
                               ps[:, :], hT[:, kc, tt * 128:(tt + 1) * 128], wt[:, kc, :], kc == 0, kc == KC - 1),
                     reads=[("hT", tt, kc // 4), wkey], writes=[pkey], counted=(kc == KC - 1))
            S.op("act", _mk(lambda e, o, a: e.copy(o, a),
                            vg[:, 2 * blk:2 * blk + 2, tt * DV:(tt + 1) * DV],
                            ps[:, :].rearrange("p (h e) -> p h e", h=2)),
                 reads=[pkey], writes=[("vg", 2 * blk, tt), ("vg", 2 * blk + 1, tt)])
    k.dump("qT", qT[:], [128, NH, T], [("qT", h, tt) for h in range(NH) for tt in range(NT)], BF16)
    k.dump("kt", kt[:], [128, NT, NH * DK], [("kt", tt, hb) for tt in range(NT) for hb in range(2)], BF16)
    k.dump("vg", vg[:], [128, NH, NT * DV], [("vg", h, tt) for h in range(NH) for tt in range(NT)], BF16)
    for tt in (range(8) if "Sloc" in k.dbg else ()):
        for h in range(NH):
            bank = 4 + (h // 2) % 2 + 2 * (tt % 2)
            half = h % 2
            pkey = ("ps", bank, half)
            pd = k.ps[bank][:, half * 256:(half + 1) * 256]
            S.op("pe", _mk(lambda e, o, l, r: e.matmul(o, lhsT=l, rhs=r, start=True, stop=True),
                           pd, kt[:, tt, h * DK:(h + 1) * DK], vg[:, h, tt * DV:(tt + 1) * DV]),
                 reads=[("kt", tt, h // 4), ("vg", h, tt)], writes=[pkey])
            if tt == 0:
                S.op("dve", _mk(lambda e, o, a: e.tensor_copy(o, a), Rl[:, h, :], pd),
                     reads=[pkey], writes=[("Rl", h)])
            else:
                S.op("dve", _mk(lambda e, o, a, g: e.scalar_tensor_tensor(o, o, g, a, ALU.mult, ALU.add),
                                Rl[:, h, :], pd, G128[h]),
                     reads=[pkey, ("Rl", h)], writes=[("Rl", h)])
    for h in (range(NH) if "Sloc" in k.dbg else ()):
        S.op("act", _mk(lambda e, o, g: e.mul(o, o, g), Rl[:, h, :], G128[h]),
             reads=[("Rl", h)], writes=[("Rl", h)])
    k.dump("Sloc", Rl[:], [128, NH, DV], [("Rl", h) for h in range(NH)])
    for lst in tmp:
        for t_, u in lst:
            M.free(u)
    for t_, u in qtok:
        M.free(u)
    M.free(u_c); M.free(u_s); M.free(u_Rl)
    k.end_phase()


def _tables(c):
    inv = (1.0 / (10000.0 ** np.linspace(0.0, 1.0, DK // 2, dtype=np.float32))).astype(np.float32)
    pos = np.concatenate([c * TP + np.arange(TP), PAST + np.arange(DEC_T), PAST + np.arange(DEC_T)])
    ang = pos.astype(np.float32)[:, None] * inv[None, :]
    t_cos = np.cos(ang).astype(np.float32)
    t_sin = np.sin(ang).astype(np.float32)
    lg = np.array(LOG_GAMMA, dtype=np.float64)
    p = np.arange(128)
    dec = np.zeros((128, 2, 2, NH), dtype=np.float64)
    for var, L in enumerate((128, 64)):
        e = (p % L + 1).astype(np.float64)[:, None]
        dec[:, 0, var, :] = np.exp(e * lg[None, :])
        dec[:, 1, var, :] = np.exp(-e * lg[None, :]) * (DK ** -0.5)
    t_dec = dec.reshape(128, 32).astype(np.float32)
    j = np.arange(128)[:, None]
    i = np.arange(128)[None, :]
    maskP = (i >= j).astype(np.float32)
    maskS = ((i >= j) & ((i // 64) == (j // 64))).astype(np.float32)
    t_mask = np.concatenate([maskP, maskS], axis=1)
    coef = np.zeros((NCORES, NH), dtype=np.float64)
    for r in range(NCORES):
        if r < c:
            coef[r] = np.exp(1024.0 * (c - 1 - r) * lg) / np.exp(128.0 * lg)
    t_coef = np.broadcast_to(coef.reshape(1, 64), (128, 64)).astype(np.float32)
    invc = np.zeros((4, 16), dtype=np.float64)
    for g, w in enumerate((2, 4, 8, 16)):
        invc[g] = 1.0 / np.minimum(c * TP + np.arange(16) + 1, w)
    t_invc = np.broadcast_to(invc.reshape(1, 64), (128, 64)).astype(np.float32)
    ppos = (np.arange(NPRE * 128) - (NCORES - 1 - c) * TP).astype(np.float32)
    pang = ppos[:, None] * inv[None, :]
    t_cosp = np.cos(pang).astype(np.float32)
    t_sinp = np.sin(pang).astype(np.float32)
    return dict(t_cosp=t_cosp, t_sinp=t_sinp, t_cos=t_cos, t_sin=t_sin, t_dec=t_dec, t_mask=t_mask,
                t_coef=np.ascontiguousarray(t_coef), t_invc=np.ascontiguousarray(t_invc),
                t_ident=np.eye(128, dtype=np.float32))


def prep_inputs(x_prompt, x_sample, cache_pool, state_retention, g_pre_mix, w_in, w_pool, pool_scale,
                w_pool_out, w_ret_out, w_o, g_post_mix, g_pre_ffn, w_up, w_down, g_post_ffn):
    f = lambda a: np.ascontiguousarray(np.asarray(a, dtype=np.float32))
    shared = dict(
        w_in=f(w_in[0]), w_pool=f(w_pool[0]), w_pool_out=f(w_pool_out[0]), w_ret_out=f(w_ret_out[0]),
        w_o=f(w_o[0]), w_up=f(w_up[0]), w_down=f(w_down[0]),
        g_pre_mix=f(np.asarray(g_pre_mix[0]).reshape(KC, 128).T),
        g_pre_ffn=f(np.asarray(g_pre_ffn[0]).reshape(KC, 128).T),
        pscale=f(np.asarray(pool_scale[0]).reshape(8, 128).T),
        g_post_mix=f(np.broadcast_to(np.asarray(g_post_mix[0])[None, :], (128, D))),
        g_post_ffn=f(np.broadcast_to(np.asarray(g_post_ffn[0])[None, :], (128, D))),
    )
    xp = np.asarray(x_prompt)[0]
    xs = np.asarray(x_sample)
    maps = []
    for c in range(NCORES):
        m = dict(shared)
        m["x"] = f(np.concatenate([xp[c * TP:(c + 1) * TP], xs[2 * c], xs[2 * c + 1]], axis=0))
        m["xh"] = f(xp[c * TP - 16:c * TP]) if c > 0 else np.zeros((16, D), np.float32)
        m["cpool"] = f(np.asarray(cache_pool)[0, 2 * c:2 * c + 2])
        m["sret"] = f(np.asarray(state_retention)[0, 2 * c:2 * c + 2])
        xpv = np.zeros((NPRE * 128, D), np.float32)
        if c > 0:
            xpv[(NCORES - 1 - c) * TP:] = xp[:c * TP]
        m["xprev"] = xpv
        m.update(_tables(c))
        maps.append(m)
    return maps


_PROG = {}


def kernel(**inputs):
    if "nc" not in _PROG:
        _PROG["nc"], _PROG["k"] = build_program()
    nc = _PROG["nc"]
    maps = prep_inputs(**inputs)
    res = run_bass_kernel_spmd(nc, maps, core_ids=list(range(NCORES)))
    R = res.results
    y = [np.asarray(r["y"]) for r in R]
    y_prompt = np.concatenate([a[:TP] for a in y], axis=0)[None]
    y_sample = np.concatenate([a[TP:].reshape(2, DEC_T, D) for a in y], axis=0)
    pool_prompt = np.asarray(R[NCORES - 1]["pool_p"])[None, None]
    ret_prompt = np.asarray(R[NCORES - 1]["ret_p"])[None, None]
    pool_sample = np.concatenate([np.asarray(r["pool_s"]) for r in R], axis=0)[None]
    ret_sample = np.concatenate([np.asarray(r["ret_s"]) for r in R], axis=0)[None]
    return tuple(np.ascontiguousarray(a.astype(np.float32)) for a in
                 (y_prompt, y_sample, pool_prompt, ret_prompt, pool_sample, ret_sample))


def _mm(S, out, lhsT, rhs, start, stop, reads, writes, counted=True):
    S.op("pe", _mk(lambda e, o, l, r, st, sp: e.matmul(o, lhsT=l, rhs=r, start=st, stop=sp),
                   out, lhsT, rhs, start, stop), reads=reads, writes=writes, counted=counted)


def _tr(S, out, in_, ident, reads, writes, counted=True):
    S.op("pe", _mk(lambda e, o, a, idn: e.transpose(o, a, idn), out, in_, ident),
         reads=reads, writes=writes, counted=counted)


def phase0_prepare(k):
    S, M = k.S, k.M
    Rp, k.Rp_u = M.alloc("Rp", [128, NH, DV], F32)
    k.Rp = Rp
    wk = [k.wslot[0], k.wslot[1]]
    wkk = [("w", 0), ("w", 1)]
    wv = [M.alloc("wv%d" % j, [128, KC, 512], BF16) for j in range(4)]
    k.p0w = (wk, wkk, wv)
    xt = [M.alloc("p0x0", [128, D], F32)]
    cs = [M.alloc("p0cs%d" % i, [128, 2, 64], F32) for i in range(2)]
    k.p0pre = (xt, cs)
    k.ld("p0x0", xt[0][0][:], k.xprev[0:128, :], [("p0x", 0)])
    for pt in range(2):
        k.ld("p0cs%d" % pt, cs[pt][0][:, 0, :], k.t_cosp[pt * 128:(pt + 1) * 128, :], [("p0cs", pt)])
        k.ld("p0cs%d" % pt, cs[pt][0][:, 1, :], k.t_sinp[pt * 128:(pt + 1) * 128, :], [("p0cs", pt)])
    for kind, j in (("k", 1), ("v", 2), ("v", 3), ("k", 0), ("v", 1), ("v", 0)):
        if kind == "k":
            src = k.w_in[:, OFF_K + 512 * j:OFF_K + 512 * (j + 1)].rearrange("(kc p) n -> p kc n", p=128)
            S.dma("pool", "w%d" % j, _mk(lambda e, o, s: e.dma_start(out=o, in_=s), wk[j][:], src), writes=[wkk[j]])
        else:
            src = k.w_in[:, OFF_V + 512 * j:OFF_V + 512 * (j + 1)].rearrange("(kc p) n -> p kc n", p=128)
            tok = S.dma("pool", "wv%d" % j, _mk(lambda e, o, s: e.dma_start(out=o, in_=s), wv[j][0][:], src),
                        writes=[("wv", j)])
            k.p0_toks = getattr(k, "p0_toks", []) + [tok]


def phase0(k):
    S, M = k.S, k.M
    Rp = k.Rp
    wk, wkk, wv = k.p0w
    k.phase_toks.extend(k.p0_toks)
    xt, cs = k.p0pre
    xt = xt + [M.alloc("p0x1", [128, D], F32)]
    xn = [M.alloc("p0xn%d" % i, [128, D], BF16) for i in range(2)]
    hTt = [M.alloc("p0h%d" % i, [128, KC, 128], BF16) for i in range(2)]
    tmp = [[M.alloc("p0rt%d_%d" % (s, i), [128, 4, 64], F32) for i in range(6)] for s in range(2)]
    ktl = [M.alloc("p0k%d" % i, [128, NH * DK], BF16) for i in range(2)]
    vtl = [M.alloc("p0v%d" % i, [128, NH * DV], BF16) for i in range(2)]
    stat_, stat_u = M.alloc("p0stat", [128, 2 * NPRE], F32)
    state = {"pj": 0}

    def hmin_of(pt):
        s = 7 - pt // 8
        return {1: 0, 2: 1, 3: 2, 4: 3, 5: 3, 6: 3, 7: 4}[s]
    started = set()

    def stage_A(pt):
        b = pt % 2
        xa = xt[b][0]
        xkey = ("p0x", b)
        ckey = ("p0cs", b)
        if pt >= 1:
            k.ld("p0x%d" % b, xa[:], k.xprev[pt * 128:(pt + 1) * 128, :], [xkey])
        if pt >= 2:
            k.ld("p0cs%d" % b, cs[b][0][:, 0, :], k.t_cosp[pt * 128:(pt + 1) * 128, :], [ckey])
            k.ld("p0cs%d" % b, cs[b][0][:, 1, :], k.t_sinp[pt * 128:(pt + 1) * 128, :], [ckey])
        ss = stat_[:, 2 * pt:2 * pt + 1]
        rs = stat_[:, 2 * pt + 1:2 * pt + 2]
        xna = xn[b][0]
        xnkey = ("p0xn", b)
        S.op("act", _mk(lambda e, o, a, s: e.activation(out=o, in_=a, func=AF.Square, accum_out=s), xna[:], xa[:], ss),
             reads=[xkey], writes=[xnkey, ("p0ss", pt)])
        S.op("act", _mk(lambda e, s, ep: e.activation(out=s, in_=s, func=AF.Sqrt, bias=ep, scale=1.0 / D), ss, k.eps_t[:, :]),
             reads=[("p0ss", pt), ("eps",)], writes=[("p0ss", pt)])
        S.op("dve", _mk(lambda e, o, a: e.reciprocal(o, a), rs, ss), reads=[("p0ss", pt)], writes=[("p0rs", pt)])
        S.op("dve", _mk(lambda e, o, a, s: e.tensor_scalar(o, a, s, None, ALU.mult), xna[:], xa[:], rs),
             reads=[xkey, ("p0rs", pt)], writes=[xnkey])

    def stage_B(pt):
        b = pt % 2
        xna = xn[b][0]
        xnkey = ("p0xn", b)
        hk = ("p0h", b)
        for g4 in range(4):
            bank = g4 % 2
            bkey = ("ps", bank)
            pb = k.ps[bank][:, :].bitcast(BF16)
            for j in range(4):
                kc = g4 * 4 + j
                _tr(S, pb[:, j * 128:(j + 1) * 128], xna[:, kc * 128:(kc + 1) * 128], k.ident_b[:, :],
                    [xnkey, ("ident_b",)], [bkey], counted=(j == 3))
            srcp = pb[:, 0:512].rearrange("p (j t) -> p j t", j=4)
            gb = k.gpm[:, g4 * 4:g4 * 4 + 4].unsqueeze(2).to_broadcast([128, 4, 128])
            S.op("dve", _mk(lambda e, o, a, g: e.tensor_tensor(o, a, g, ALU.mult), hTt[b][0][:, g4 * 4:g4 * 4 + 4, :], srcp, gb),
                 reads=[bkey, ("gpm",)], writes=[(hk, g4)])

    def stage_C(pt):
        b = pt % 2
        hk = ("p0h", b)
        ckey = ("p0cs", b)
        kkey = ("p0k", b)
        vkey = ("p0v", b)
        hmin = hmin_of(pt)
        for blk in range(2):
            h0 = max(hmin - 4 * blk, 0)
            if h0 >= 4:
                continue
            nh = 4 - h0
            ncol = nh * 128
            bank = 2 + state["pj"] % 4
            state["pj"] += 1
            ps = k.ps[bank]
            pkey = ("ps", bank)
            for kc in range(KC):
                _mm(S, ps[:, 0:ncol], hTt[b][0][:, kc, :], wk[blk][:, kc, h0 * 128:512], kc == 0, kc == KC - 1,
                    [(hk, kc // 4), wkk[blk]], [pkey], counted=(kc == KC - 1))
            s = state["pj"] % 2
            pv = ps[:, 0:ncol].rearrange("p (h f two) -> p h f two", h=nh, two=2)
            xe, xo = pv[:, :, :, 0], pv[:, :, :, 1]
            cb = cs[b][0][:, 0, :].unsqueeze(1).to_broadcast([128, nh, 64])
            sb = cs[b][0][:, 1, :].unsqueeze(1).to_broadcast([128, nh, 64])
            t1, t2, t3, t4, re, ro = [tmp[s][i][0][:, 0:nh, :] for i in range(6)]
            tk = [("p0rt", s, i) for i in range(6)]
            for (o, a, b_, ky) in ((t1, xe, cb, tk[0]), (t2, xo, sb, tk[1]), (t3, xe, sb, tk[2]), (t4, xo, cb, tk[3])):
                S.op("dve", _mk(lambda e, o, a, b_: e.tensor_tensor(o, a, b_, ALU.mult), o, a, b_),
                     reads=[pkey, ckey], writes=[ky])
            S.op("pool", _mk(lambda e, o, a, b_: e.tensor_tensor(o, a, b_, ALU.subtract), re, t1, t2),
                 reads=[tk[0], tk[1]], writes=[tk[4]])
            S.op("pool", _mk(lambda e, o, a, b_: e.tensor_tensor(o, a, b_, ALU.add), ro, t3, t4),
                 reads=[tk[2], tk[3]], writes=[tk[5]])
            di = 16 + blk * 4 + h0
            db = k.dec[:, di:di + nh].unsqueeze(2).to_broadcast([128, nh, 64])
            dv_ = ktl[b][0][:, blk * 512 + h0 * 128:(blk + 1) * 512].rearrange("p (h f two) -> p h f two", h=nh, two=2)
            S.op("pool", _mk(lambda e, o, a, b_: e.tensor_tensor(o, a, b_, ALU.mult), dv_[:, :, :, 0], re, db),
                 reads=[tk[4], ("dec",)], writes=[(kkey, blk)])
            S.op("pool", _mk(lambda e, o, a, b_: e.tensor_tensor(o, a, b_, ALU.mult), dv_[:, :, :, 1], ro, db),
                 reads=[tk[5], ("dec",)], writes=[(kkey, blk)])

    def stage_Cv(pt):
        b = pt % 2
        hk = ("p0h", b)
        vkey = ("p0v", b)
        hmin = hmin_of(pt)
        for blk in range(4):
            h0 = max(hmin - 2 * blk, 0)
            if h0 >= 2:
                continue
            c0 = h0 * 256
            ncol = 512 - c0
            bank = 2 + state["pj"] % 4
            state["pj"] += 1
            ps = k.ps[bank]
            pkey = ("ps", bank)
            for kc in range(KC):
                _mm(S, ps[:, 0:ncol], hTt[b][0][:, kc, :], wv[blk][0][:, kc, c0:512], kc == 0, kc == KC - 1,
                    [(hk, kc // 4), ("wv", blk)], [pkey], counted=(kc == KC - 1))
            S.op("act", _mk(lambda e, o, a: e.copy(o, a), vtl[b][0][:, blk * 512 + c0:(blk + 1) * 512], ps[:, 0:ncol]),
                 reads=[pkey], writes=[(vkey, blk)])

    def stage_D(pt):
        b = pt % 2
        kkey = ("p0k", b)
        vkey = ("p0v", b)
        for h in range(hmin_of(pt), NH):
            bank = 6 + (h // 2) % 2
            half = h % 2
            pkey = ("ps", bank, half)
            pd = k.ps[bank][:, half * 256:(half + 1) * 256]
            _mm(S, pd, ktl[b][0][:, h * DK:(h + 1) * DK], vtl[b][0][:, h * DV:(h + 1) * DV], True, True,
                [(kkey, h // 4), (vkey, h // 2)], [pkey])
            if h not in started:
                started.add(h)
                S.op("dve", _mk(lambda e, o, a: e.tensor_copy(o, a), Rp[:, h, :], pd), reads=[pkey], writes=[("Rp", h)])
            else:
                S.op("dve", _mk(lambda e, o, a, g: e.scalar_tensor_tensor(o, o, g, a, ALU.mult, ALU.add), Rp[:, h, :], pd, G128[h]),
                     reads=[pkey, ("Rp", h)], writes=[("Rp", h)])

    stage_A(0)
    for pt in range(NPRE):
        stage_B(pt)
        stage_C(pt)
        if pt > 0:
            stage_D(pt - 1)
        if pt + 1 < NPRE:
            stage_A(pt + 1)
        stage_Cv(pt)
    stage_D(NPRE - 1)
    k.dump("Rp", Rp[:], [128, NH, DV], [("Rp", h) for h in range(NH)])
    for lst in (wv, xt, xn, hTt, cs, ktl, vtl):
        for t_, u in lst:
            M.free(u)
    for lst in tmp:
        for t_, u in lst:
            M.free(u)
    M.free(stat_u)
    k.end_phase()


def phase4(k):
    S, M = k.S, k.M
    hT, qT, kt, vg, Rp = k.hT, k.qT, k.kt, k.vg, k.Rp
    sg, sg_u = M.alloc("sg", [128, NT, 512], BF16)
    ktT = [M.alloc("ktT%d" % i, [128, T], BF16) for i in range(2)]
    go = [M.alloc("go%d" % i, [128, NT, DV], BF16) for i in range(2)]
    Sbf, Sbf_u = M.alloc("Sbf", [128, 2, DV], BF16)
    S0, S0_u = M.alloc("S0", [128, 2, 2, DV], F32)
    S0b, S0b_u = M.alloc("S0b", [128, 2, 2, DV], BF16)
    sT = [M.alloc("sT%d" % i, [128, 128], BF16) for i in range(4)]
    qA = [M.alloc("qA%d" % i, [128, 128], BF16) for i in range(2)]
    qB = [M.alloc("qB%d" % i, [128, 128], BF16) for i in range(2)]
    junk, junk_u = M.alloc("p4junk", [128, DV], BF16)
    st_, st_u = M.alloc("p4stat", [128, 2 * NT * NH], F32)
    So = [M.alloc("So%d" % i, [128, DV], F32) for i in range(4)]
    for i in range(2):
        S.op("pool", _mk(lambda e, o: e.memset(o, 0.0), qA[i][0][:, :]), writes=[("qA", i)])
        S.op("pool", _mk(lambda e, o: e.memset(o, 0.0), qB[i][0][:, :]), writes=[("qB", i)])
    nsT = 0
    nSo = 0
    nps = 0
    sgb = [(sg, sg_u), M.alloc("sg2", [128, NT, 512], BF16)]

    def gr_proj(hp_):
        wt, wkey, _ = k.wnext()
        sgt = sgb[hp_ % 2][0]
        for tt in range(NT):
            bank = tt % 2
            pkey = ("ps", bank)
            for kc in range(KC):
                _mm(S, k.ps[bank][:, :], hT[:, kc, tt * 128:(tt + 1) * 128], wt[:, kc, :], kc == 0, kc == KC - 1,
                    [("hT", tt, kc // 4), wkey], [pkey], counted=(kc == KC - 1))
            S.op("act", _mk(lambda e, o, a: e.activation(out=o, in_=a, func=AF.Silu), sgt[:, tt, :], k.ps[bank][:, :]),
                 reads=[pkey], writes=[("sg", hp_ % 2, tt)])

    gr_proj(0)
    for hp in range(4):
        sg = sgb[hp % 2][0]
        for hl in range(2):
            h = 2 * hp + hl
            for (t0, n) in ((0, 4), (4, 4), (8, 1)):
                bank = 2 + nps % 2
                nps += 1
                bkey = ("ps", bank)
                pb = k.ps[bank][:, :].bitcast(BF16)
                for j in range(n):
                    _tr(S, pb[:, j * 128:(j + 1) * 128], kt[:, t0 + j, h * DK:(h + 1) * DK], k.ident_b[:, :],
                        [("kt", t0 + j, h // 4), ("ident_b",)], [bkey], counted=(j == n - 1))
                S.op("act", _mk(lambda e, o, a: e.copy(o, a), ktT[hl][0][:, t0 * 128:(t0 + n) * 128], pb[:, 0:n * 128]),
                     reads=[bkey], writes=[("ktT", hl)])
            S.op("act", _mk(lambda e, o, a, g: e.mul(o, a, g), Sbf[:, hl, :], Rp[:, h, :], G128[h]),
                 reads=[("Rp", h)], writes=[("Sbf", hl)])
            for s_ in range(2):
                k.ld("S0_%d_%d" % (s_, hl), S0[:, s_, hl, :], k.sret[s_, h], [("S0", s_, hl)])
                S.op("act", _mk(lambda e, o, a: e.copy(o, a), S0b[:, s_, hl, :], S0[:, s_, hl, :]),
                     reads=[("S0", s_, hl)], writes=[("S0b", s_, hl)])
                S.op("act", _mk(lambda e, o, g: e.mul(o, o, g), S0[:, s_, hl, :], G64[h]),
                     reads=[("S0", s_, hl), ("S0b", s_, hl)], writes=[("S0", s_, hl)])
            S.op("pool", _mk(lambda e, o, a: e.tensor_copy(o, a), qA[hl][0][:, 0:64], qT[:, h, 1024:1088]),
                 reads=[("qT", h, 8)], writes=[("qA", hl)])
            S.op("pool", _mk(lambda e, o, a: e.tensor_copy(o, a), qB[hl][0][:, 64:128], qT[:, h, 1088:1152]),
                 reads=[("qT", h, 8)], writes=[("qB", hl)])
        ctx = {}

        def st_A(tt, hl):
            h = 2 * hp + hl
            hc = slice(h * DK, (h + 1) * DK)
            tsl = slice(tt * 128, (tt + 1) * 128)
            vsl = slice(tt * DV, (tt + 1) * DV)
            c = ctx[(tt, hl)] = {}
            if tt < 8:
                c["dkey"] = ("ps", 7, hl)
                c["pd"] = k.ps[7][:, hl * 256:(hl + 1) * 256]
                _mm(S, c["pd"], kt[:, tt, hc], vg[:, h, vsl], True, True, [("kt", tt, h // 4), ("vg", h, tt)], [c["dkey"]])
            else:
                c["dkeyA"] = ("ps", 7, hl)
                c["dkeyB"] = ("ps", 3)
                c["pdA"] = k.ps[7][:, hl * 256:(hl + 1) * 256]
                c["pdB"] = k.ps[3][:, hl * 256:(hl + 1) * 256]
                _mm(S, c["pdA"], kt[0:64, tt, hc], vg[0:64, h, vsl], True, True, [("kt", tt, h // 4), ("vg", h, tt)], [c["dkeyA"]])
                _mm(S, c["pdB"], kt[64:128, tt, hc], vg[64:128, h, vsl], True, True, [("kt", tt, h // 4), ("vg", h, tt)], [c["dkeyB"]])
            q4 = (2 * tt + hl) % 4
            skey = ("ps", 4, q4)
            pS = k.ps[4][:, q4 * 128:(q4 + 1) * 128]
            _mm(S, pS, ktT[hl][0][:, tsl], qT[:, h, tsl], True, True, [("ktT", hl), ("qT", h, tt)], [skey])
            si = (2 * tt + hl) % 4
            c["si"] = si
            mk = k.mask[:, 0:128] if tt < 8 else k.mask[:, 128:256]
            S.op("dve", _mk(lambda e, o, a, m: e.tensor_tensor(o, a, m, ALU.mult), sT[si][0][:, :], pS, mk),
                 reads=[skey, ("mask",)], writes=[("sT", si)])

        def st_B(tt, hl):
            nonlocal nSo
            h = 2 * hp + hl
            tsl = slice(tt * 128, (tt + 1) * 128)
            vsl = slice(tt * DV, (tt + 1) * DV)
            c = ctx[(tt, hl)]
            si = c["si"]
            ob = 5 if hl == 0 else 6
            oh = tt % 2
            okey = ("ps", ob, oh)
            pO = k.ps[ob][:, oh * 256:(oh + 1) * 256]
            c["okey"], c["pO"] = okey, pO
            _mm(S, pO, sT[si][0][:, :], vg[:, h, vsl], True, False, [("sT", si), ("vg", h, tt)], [okey], counted=False)
            if tt < 8:
                _mm(S, pO, qT[:, h, tsl], Sbf[:, hl, :], False, True, [("qT", h, tt), ("Sbf", hl)], [okey])
            else:
                _mm(S, pO, qA[hl][0][:, :], S0b[:, 0, hl, :], False, False, [("qA", hl), ("S0b", 0, hl)], [okey], counted=False)
                _mm(S, pO, qB[hl][0][:, :], S0b[:, 1, hl, :], False, True, [("qB", hl), ("S0b", 1, hl)], [okey])
            if tt < 8:
                S.op("dve", _mk(lambda e, o, a, g: e.scalar_tensor_tensor(o, o, g, a, ALU.mult, ALU.add), Rp[:, h, :], c["pd"], G128[h]),
                     reads=[c["dkey"], ("Rp", h)], writes=[("Rp", h)])
                if tt < 7:
                    S.op("act", _mk(lambda e, o, a, g: e.mul(o, a, g), Sbf[:, hl, :], Rp[:, h, :], G128[h]),
                         reads=[("Rp", h)], writes=[("Sbf", hl)])
                else:
                    so = So[nSo % 4]
                    sok = ("So", nSo % 4)
                    nSo += 1
                    S.op("act", _mk(lambda e, o, a, g: e.mul(o, a, g), so[0][:, :], Rp[:, h, :], G128[h]),
                         reads=[("Rp", h)], writes=[sok])
                    k.st("So%d" % (int(sok[1])), k.ret_p[h], so[0][:, :], [sok])
            else:
                for s_, (pdx, dk_) in enumerate(((c["pdA"], c["dkeyA"]), (c["pdB"], c["dkeyB"]))):
                    so = So[nSo % 4]
                    sok = ("So", nSo % 4)
                    nSo += 1
                    S.op("dve", _mk(lambda e, o, a, g, b_: e.scalar_tensor_tensor(o, a, g, b_, ALU.mult, ALU.add),
                                    so[0][:, :], pdx, G64[h], S0[:, s_, hl, :]),
                         reads=[dk_, ("S0", s_, hl)], writes=[sok])
                    k.st("So%d" % (int(sok[1])), k.ret_s[s_, h], so[0][:, :], [sok])

        def st_C(tt, hl):
            h = 2 * hp + hl
            c = ctx.pop((tt, hl))
            okey, pO = c["okey"], c["pO"]
            ssa = st_[:, h * NT + tt:h * NT + tt + 1]
            S.op("act", _mk(lambda e, o, a, s: e.activation(out=o, in_=a, func=AF.Square, accum_out=s), junk[:, :], pO, ssa),
                 reads=[okey], writes=[("p4junk",), ("p4ss", tt, h)])
            S.op("dve", _mk(lambda e, o, a, g: e.tensor_tensor(o, a, g, ALU.mult),
                            go[hl][0][:, tt, :], pO, sg[:, tt, hl * DV:(hl + 1) * DV]),
                 reads=[okey, ("p4ss", tt, h), ("sg", hp % 2, tt)], writes=[("go", hl, tt)])

        for tt in range(NT):
            for hl in range(2):
                st_A(tt, hl)
            for hl in range(2):
                st_B(tt, hl)
            for hl in range(2):
                st_C(tt, hl)
        if hp + 1 < 4:
            gr_proj(hp + 1)
        for hl in range(2):
            h = 2 * hp + hl
            ssv = st_[:, h * NT:(h + 1) * NT]
            rsv = st_[:, NH * NT + h * NT:NH * NT + (h + 1) * NT]
            S.op("act", _mk(lambda e, s, ep: e.activation(out=s, in_=s, func=AF.Sqrt, bias=ep, scale=1.0 / DV), ssv, k.eps_t[:, :]),
                 reads=[("p4ss", tt, h) for tt in range(NT)] + [("eps",)], writes=[("p4ssv", h)])
            S.op("dve", _mk(lambda e, o, a: e.reciprocal(o, a), rsv, ssv), reads=[("p4ssv", h)], writes=[("p4rs", h)])
            S.op("dve", _mk(lambda e, o, r: e.tensor_tensor(o, o, r, ALU.mult), go[hl][0][:, :, :],
                            rsv.unsqueeze(2).to_broadcast([128, NT, DV])),
                 reads=[("go", hl, tt) for tt in range(NT)] + [("p4rs", h)], writes=[("go", hl, tt) for tt in range(NT)])
        for hl in range(2):
            h = 2 * hp + hl
            for c2 in range(2):
                for (t0, n) in ((0, 4), (4, 4), (8, 1)):
                    bank = 2 + nps % 2
                    nps += 1
                    bkey = ("ps", bank)
                    pb = k.ps[bank][:, :].bitcast(BF16)
                    for j in range(n):
                        _tr(S, pb[:, j * 128:(j + 1) * 128], go[hl][0][:, t0 + j, c2 * 128:(c2 + 1) * 128], k.ident_b[:, :],
                            [("go", hl, t0 + j), ("ident_b",)], [bkey], counted=(j == n - 1))
                    S.op("act", _mk(lambda e, o, a: e.copy(o, a),
                                    vg[:, h, c2 * T + t0 * 128:c2 * T + (t0 + n) * 128], pb[:, 0:n * 128]),
                         reads=[bkey], writes=[("vg", h, t) for t in range(NT)] + [("goT", h)])
    k.dump("goT", vg[:], [128, NH, NT * DV], [("goT", h) for h in range(NH)], BF16)
    for lst in (ktT, go, sT, qA, qB, So):
        for t_, u in lst:
            M.free(u)
    for u in (sg_u, sgb[1][1], Sbf_u, S0_u, S0b_u, junk_u, st_u, k.qT_u, k.kt_u, k.Rp_u):
        M.free(u)
    k.end_phase()


UL = 1200


def phase2(k):
    S, M = k.S, k.M
    hT, hTh = k.hT, k.hTh
    pyT, k.pyT_u = M.alloc("pyT", [128, 8, T], BF16)
    k.pyT = pyT
    Uc = [M.alloc("Uc%d" % i, [128, UL], F32) for i in range(2)]
    W1, W1_u = M.alloc("pW1", [128, UL], F32)
    W2, W2_u = M.alloc("pW2", [128, UL], F32)
    dT = [M.alloc("dT%d" % i, [128, 2, T], BF16) for i in range(2)]
    cl, cl_u = M.alloc("cl", [16, 2, PW], F32)
    wp, wp_u = M.alloc("wp", [128, 4, 2, 256], BF16)
    save, save_u = M.alloc("psave", [128, 8, 3, 16], F32)
    stage, stage_u = M.alloc("pstage", [16, PW], F32)
    t16, t16_u = M.alloc("pt16", [128, 16], F32)
    for s_ in range(2):
        k.ld("cl", cl[0:LAG, s_, :], k.cpool[s_], [("cl",)])
    tok = S.dma("pool", "wp", _mk(lambda e, o, s: e.dma_start(out=o, in_=s), wp[:],
                                  k.w_pool.rearrange("g (cc p) d -> p g cc d", p=128)), writes=[("wp",)])
    k.phase_toks.append(tok)
    for i in range(2):
        S.op("pool", _mk(lambda e, o: e.memset(o, 0.0), Uc[i][0][:, :]), writes=[("Uc", i)])
    groups = [(None, 0, 16, 0), (hT, 0, 512, 16), (hT, 512, 512, 528), (hT, 1024, 64, 1056), (hT, 1088, 64, 1136)]
    mains = [(16, 1024, 0), (1056, 64, 1024), (1136, 64, 1088)]
    nb = 0
    wt = wkey = None
    for uc in range(8):
        if uc % 4 == 0:
            wt, wkey, _ = k.wnext()
        g = uc // 2
        cc = uc % 2
        ub = uc % 2
        U = Uc[ub][0]
        ukey = ("Uc", ub)
        msl = slice((uc % 4) * 128, (uc % 4 + 1) * 128)
        for (src, t0, n, c0) in groups:
            bank = nb % 4
            nb += 1
            pkey = ("ps", bank)
            for kc in range(KC):
                if src is None:
                    rhs = hTh[:, kc, 0:16]
                    rk = [("hTh", kc // 4)]
                else:
                    rhs = hT[:, kc, t0:t0 + n]
                    rk = hT_keys(t0, n, kc)
                _mm(S, k.ps[bank][:, 0:n], wt[:, kc, msl], rhs, kc == 0, kc == KC - 1, rk + [wkey], [pkey], counted=(kc == KC - 1))
            S.op("act", _mk(lambda e, o, a: e.copy(o, a), U[:, c0:c0 + n], k.ps[bank][:, 0:n]), reads=[pkey], writes=[ukey])
        for s_ in range(2):
            tkey = ("ps", 4)
            _tr(S, k.ps[4][:, s_ * 16:s_ * 16 + LAG], cl[0:LAG, s_, uc * 128:(uc + 1) * 128], k.ident_f[0:LAG, 0:LAG],
                [("cl",), ("ident_f",)], [tkey])
            c1 = 1041 + 80 * s_
            S.op("act", _mk(lambda e, o, a: e.copy(o, a), U[:, c1:c1 + LAG], k.ps[4][:, s_ * 16:s_ * 16 + LAG]),
                 reads=[tkey], writes=[ukey])
        for s3, c1 in enumerate((1025, 1105, 1185)):
            S.op("pool", _mk(lambda e, o, a: e.tensor_copy(o, a), save[:, uc, s3, 0:LAG], U[:, c1:c1 + LAG]),
                 reads=[ukey], writes=[("psave", uc)])
        cur, ckey = U, ukey
        sh = 1
        bufs = [(W1, ("pW", 1)), (W2, ("pW", 2))]
        for lvl in range(g + 1):
            dst, dkey = bufs[lvl % 2]
            lo = 2 * sh - 1
            S.op("dve", _mk(lambda e, o, a, b_: e.tensor_tensor(o, a, b_, ALU.add), dst[:, lo:UL], cur[:, lo:UL], cur[:, lo - sh:UL - sh]),
                 reads=[ckey], writes=[dkey])
            cur, ckey = dst, dkey
            sh *= 2
        w = 2 ** (g + 1)
        dt_ = dT[g % 2][0]
        dkey2 = ("dT", g % 2, cc)
        for (c0, n, t0) in mains:
            S.op("dve", _mk(lambda e, o, a, sc, b_: e.scalar_tensor_tensor(o, a, sc, b_, ALU.mult, ALU.subtract),
                            dt_[:, cc, t0:t0 + n], cur[:, c0:c0 + n], 1.0 / w, U[:, c0:c0 + n]),
                 reads=[ckey, ukey], writes=[dkey2])
        S.op("dve", _mk(lambda e, o, a, b_: e.tensor_tensor(o, a, b_, ALU.mult), t16[:, :], cur[:, 16:32], k.invc[:, g * 16:(g + 1) * 16]),
             reads=[ckey, ("invc",)], writes=[("pt16",)])
        S.op("dve", _mk(lambda e, o, a, b_: e.tensor_tensor(o, a, b_, ALU.subtract), dt_[:, cc, 0:16], t16[:, :], U[:, 16:32]),
             reads=[("pt16",), ukey], writes=[dkey2])
        if cc == 1:
            for dc in range(2):
                for (t0, n) in TBLK:
                    bank = 5 + nb % 2
                    nb += 1
                    pkey = ("ps", bank)
                    for c_ in range(2):
                        _mm(S, k.ps[bank][:, 0:n], wp[:, g, c_, dc * 128:(dc + 1) * 128], dt_[:, c_, t0:t0 + n], c_ == 0, c_ == 1,
                            [("wp",), ("dT", g % 2, c_)], [pkey], counted=(c_ == 1))
                    oc = 2 * g + dc
                    S.op("act", _mk(lambda e, o, a, s: e.activation(out=o, in_=a, func=AF.Copy, scale=s),
                                    pyT[:, oc, t0:t0 + n], k.ps[bank][:, 0:n], k.psc[:, oc:oc + 1]),
                         reads=[pkey, ("psc",)], writes=[("pyT", oc)])
    for s3 in range(3):
        for half in range(2):
            bank = 7 if half == 0 else 4
            bkey = ("ps", bank)
            for j in range(4):
                uc = half * 4 + j
                _tr(S, k.ps[bank][0:LAG, j * 128:(j + 1) * 128], save[:, uc, s3, 0:LAG], k.ident_f[:, :],
                    [("psave", uc), ("ident_f",)], [bkey], counted=(j == 3))
            S.op("act", _mk(lambda e, o, a: e.copy(o, a), stage[0:LAG, half * 512:(half + 1) * 512], k.ps[bank][0:LAG, :]),
                 reads=[bkey], writes=[("pstage",)])
        dst = k.pool_p if s3 == 0 else k.pool_s[s3 - 1]
        k.st("pstage", dst, stage[0:LAG, :], [("pstage",)])
    k.dump("pyT", pyT[:], [128, 8, T], [("pyT", oc) for oc in range(8)], BF16)
    for lst in (Uc, dT):
        for t_, u in lst:
            M.free(u)
    for u in (W1_u, W2_u, cl_u, wp_u, save_u, stage_u, t16_u, k.hTh_u):
        M.free(u)
    k.end_phase()


def goT_rhs(k, kc, t0, n):
    h, c2 = kc // 2, kc % 2
    return k.vg[:, h, c2 * T + t0:c2 * T + t0 + n]


def phase5(k):
    S, M = k.S, k.M
    hT, pyT = k.hT, k.pyT
    mT, k.mT_u = M.alloc("mT", [128, KC, T], BF16)
    k.mT = mT
    sigA, sigA_u = M.alloc("sigA", [128, 4, T], BF16)
    part1, part1_u = M.alloc("part1", [128, 4, T], F32)
    tm = [M.alloc("p5t%d" % i, [128, 512], F32) for i in range(2)]
    nb = 0
    nt = 0
    for j in range(4):
        for stage in range(4):
            wt, wkey, blk = k.wnext()
            nk = blk[3]
            for oc4 in range(4):
                msl = slice(oc4 * 128, (oc4 + 1) * 128)
                for (t0, n) in TBLK:
                    bank = nb % 6
                    nb += 1
                    pkey = ("ps", bank)
                    ps = k.ps[bank][:, 0:n]
                    for kc in range(nk):
                        if stage in (0, 2):
                            rhs, rk = hT[:, kc, t0:t0 + n], hT_keys(t0, n, kc)
                        elif stage == 1:
                            rhs, rk = pyT[:, kc, t0:t0 + n], [("pyT", kc)]
                        else:
                            rhs, rk = goT_rhs(k, kc, t0, n), [("goT", kc // 2)]
                        _mm(S, ps, wt[:, kc, msl], rhs, kc == 0, kc == nk - 1, rk + [wkey], [pkey], counted=(kc == nk - 1))
                    skey = ("sigA", oc4, t0)
                    if stage in (0, 2):
                        S.op("act", _mk(lambda e, o, a: e.activation(out=o, in_=a, func=AF.Sigmoid), sigA[:, oc4, t0:t0 + n], ps),
                             reads=[pkey], writes=[skey])
                    elif stage == 1:
                        S.op("dve", _mk(lambda e, o, a, b_: e.tensor_tensor(o, a, b_, ALU.mult), part1[:, oc4, t0:t0 + n], ps, sigA[:, oc4, t0:t0 + n]),
                             reads=[pkey, skey], writes=[("part1", oc4, t0)])
                    else:
                        tb_ = tm[nt % 2]
                        tkey = ("p5t", nt % 2)
                        nt += 1
                        S.op("dve", _mk(lambda e, o, a, b_: e.tensor_tensor(o, a, b_, ALU.mult), tb_[0][:, 0:n], ps, sigA[:, oc4, t0:t0 + n]),
                             reads=[pkey, skey], writes=[tkey])
                        S.op("pool", _mk(lambda e, o, a, b_: e.tensor_tensor(o, a, b_, ALU.add), mT[:, 4 * j + oc4, t0:t0 + n], tb_[0][:, 0:n], part1[:, oc4, t0:t0 + n]),
                             reads=[tkey, ("part1", oc4, t0)], writes=[("mT", 4 * j + oc4, t0)])
    k.dump("mT", mT[:], [128, KC, T], [("mT", oc, t0) for oc in range(KC) for (t0, n) in TBLK], BF16)
    for t_, u in tm:
        M.free(u)
    for u in (sigA_u, part1_u, k.hT_u, k.pyT_u, k.vg_u):
        M.free(u)
    k.end_phase()


def phase6(k):
    S, M = k.S, k.M
    mT = k.mT
    h2T, k.h2T_u = M.alloc("h2T", [128, KC, T], BF16)
    k.h2T = h2T
    gpo, gpo_u = M.alloc("gpo", [128, D], F32)
    k.ld("gpo", gpo[:], k.g_post_mix, [("gpo",)])
    wex = [M.alloc("woex%d" % i, [128, KC, 512], BF16) for i in range(2)]
    xt = [M.alloc("p6x%d" % i, [128, D], F32) for i in range(2)]
    x1 = [M.alloc("p6x1%d" % i, [128, D], F32) for i in range(2)]
    xn = [M.alloc("p6xn%d" % i, [128, D], BF16) for i in range(2)]
    st_, st_u = M.alloc("p6stat", [128, 8 * NT], F32)
    j0 = k.wcur
    assert j0 == k.n_pre_wo and j0 <= k.wloaded <= j0 + 2, (j0, k.wloaded, k.n_pre_wo)
    already = k.wloaded - j0
    wo = []
    for i in range(4):
        if i < 2:
            sl = (j0 + i) % NSLOT
            tile_, key, chan = k.wslot[sl], ("w", sl), "w%d" % sl
        else:
            tile_, key, chan = wex[i - 2][0], ("woex", i - 2), "woex%d" % (i - 2)
        ap2d, r0, c0, nk = k.blocks[j0 + i]
        src = ap2d[r0:r0 + nk * 128, c0:c0 + 512].rearrange("(kc p) n -> p kc n", p=128)
        if i >= already:
            tok = S.dma("pool", chan, _mk(lambda e, o, s: e.dma_start(out=o, in_=s), tile_[:, :, :], src), writes=[key])
            if i >= 2:
                k.phase_toks.append(tok)
        wo.append((tile_, key))
    k.wcur = j0 + 4
    k.wloaded = j0 + 4
    junk6, junk6_u = M.alloc("p6junk", [128, 512], BF16)

    def stage_mm(tt):
        b = tt % 2
        tsl = slice(tt * 128, (tt + 1) * 128)
        xa, xkey = xt[b][0], ("p6x", b)
        k.ld("p6x%d" % b, xa[:], k.x[tsl, :], [xkey])
        x1a, x1key = x1[b][0], ("p6x1", b)
        base = 8 * tt
        for cb in range(4):
            bank = cb
            pkey = ("ps", bank)
            csl = slice(cb * 512, (cb + 1) * 512)
            for kc in range(KC):
                _mm(S, k.ps[bank][:, :], mT[:, kc, tsl], wo[cb][0][:, kc, :], kc == 0, kc == KC - 1,
                    [("mT", kc, (tt * 128) // 512 * 512), wo[cb][1]], [pkey], counted=(kc == KC - 1))
            S.op("act", _mk(lambda e, o, a, s: e.activation(out=o, in_=a, func=AF.Square, accum_out=s),
                            junk6[:, :], k.ps[bank][:, :], st_[:, base + cb:base + cb + 1]),
                 reads=[pkey], writes=[("p6junk",), ("p6ss", tt, cb)])
            S.op("dve", _mk(lambda e, o, a, g: e.tensor_tensor(o, a, g, ALU.mult), x1a[:, csl], k.ps[bank][:, :], gpo[:, csl]),
                 reads=[pkey, ("p6ss", tt, cb), ("gpo",)], writes=[(x1key, cb)])

    def stage_post(tt):
        b = tt % 2
        tsl = slice(tt * 128, (tt + 1) * 128)
        xa, xkey = xt[b][0], ("p6x", b)
        x1a, x1key = x1[b][0], ("p6x1", b)
        xna, xnkey = xn[b][0], ("p6xn", b)
        base = 8 * tt
        x1all = [(x1key, cb) for cb in range(4)]
        ss = st_[:, base + 4:base + 5]
        rs = st_[:, base + 5:base + 6]
        S.op("dve", _mk(lambda e, o, a: e.reduce_sum(o, a, AX.X), ss, st_[:, base:base + 4]),
             reads=[("p6ss", tt, cb) for cb in range(4)], writes=[("p6s", tt)])
        S.op("act", _mk(lambda e, s, ep: e.activation(out=s, in_=s, func=AF.Sqrt, bias=ep, scale=1.0 / D), ss, k.eps_t[:, :]),
             reads=[("p6s", tt), ("eps",)], writes=[("p6s", tt)])
        S.op("dve", _mk(lambda e, o, a: e.reciprocal(o, a), rs, ss), reads=[("p6s", tt)], writes=[("p6r", tt)])
        S.op("dve", _mk(lambda e, o, s, b_: e.scalar_tensor_tensor(o, o, s, b_, ALU.mult, ALU.add), x1a[:, :], rs, xa[:, :]),
             reads=x1all + [("p6r", tt), xkey], writes=x1all + [(x1key, "f")])
        k.st("p6x1_%d" % b, k.x1_d[tsl, :], x1a[:, :], x1all + [(x1key, "f")], writes=[("x1d", tt)])
        ss2 = st_[:, base + 6:base + 7]
        rs2 = st_[:, base + 7:base + 8]
        S.op("act", _mk(lambda e, o, a, s: e.activation(out=o, in_=a, func=AF.Square, accum_out=s), xna[:, :], x1a[:, :], ss2),
             reads=x1all + [(x1key, "f")], writes=[xnkey, ("p6s2", tt)])
        S.op("act", _mk(lambda e, s, ep: e.activation(out=s, in_=s, func=AF.Sqrt, bias=ep, scale=1.0 / D), ss2, k.eps_t[:, :]),
             reads=[("p6s2", tt), ("eps",)], writes=[("p6s2", tt)])
        S.op("dve", _mk(lambda e, o, a: e.reciprocal(o, a), rs2, ss2), reads=[("p6s2", tt)], writes=[("p6r2", tt)])
        S.op("dve", _mk(lambda e, o, a, s: e.tensor_scalar(o, a, s, None, ALU.mult), xna[:, :], x1a[:, :], rs2),
             reads=x1all + [(x1key, "f"), ("p6r2", tt)], writes=[xnkey])

    def stage_post_b(tt):
        b = tt % 2
        tsl = slice(tt * 128, (tt + 1) * 128)
        xna, xnkey = xn[b][0], ("p6xn", b)
        for g4 in range(4):
            bank = 4 + (g4 % 2) + 2 * b
            bkey = ("ps", bank)
            pb = k.ps[bank][:, :].bitcast(BF16)
            for j in range(4):
                kc = g4 * 4 + j
                _tr(S, pb[:, j * 128:(j + 1) * 128], xna[:, kc * 128:(kc + 1) * 128], k.ident_b[:, :],
                    [xnkey, ("ident_b",)], [bkey], counted=(j == 3))
            gb = k.gpf[:, g4 * 4:g4 * 4 + 4].unsqueeze(2).to_broadcast([128, 4, 128])
            S.op("dve", _mk(lambda e, o, a, g: e.tensor_tensor(o, a, g, ALU.mult), h2T[:, g4 * 4:g4 * 4 + 4, tsl],
                            pb[:, 0:512].rearrange("p (j t) -> p j t", j=4), gb),
                 reads=[bkey, ("gpf",)], writes=[("h2T", tt, g4)])

    stage_mm(0)
    for tt in range(1, NT):
        stage_post(tt - 1)
        stage_mm(tt)
        stage_post_b(tt - 1)
    stage_post(NT - 1)
    stage_post_b(NT - 1)
    M.free(junk6_u)
    k.dump("h2T", h2T[:], [128, KC, T], [("h2T", tt, g4) for tt in range(NT) for g4 in range(4)], BF16)
    for lst in (wex, xt, x1, xn):
        for t_, u in lst:
            M.free(u)
    for u in (gpo_u, st_u, k.mT_u):
        M.free(u)
    k.end_phase()


def h2T_keys(t0, n, kc):
    return [("h2T", t, kc // 4) for t in range(t0 // 128, (t0 + n + 127) // 128)]


def phase7(k):
    S, M = k.S, k.M
    h2T = k.h2T
    NFC = FG // 128
    f1T, f1T_u = M.alloc("f1T", [128, NFC, T], BF16)
    facc, facc_u = M.alloc("facc", [128, NT, D], F32)
    gpo, gpo_u = M.alloc("gpo2", [128, D], F32)
    k.ld("gpo2", gpo[:], k.g_post_ffn, [("gpo2",)])
    rt = [M.alloc("p7r%d" % i, [128, 512], F32) for i in range(2)]
    x1r = [M.alloc("p7x1%d" % i, [128, D], F32) for i in range(2)]
    junk, junk_u = M.alloc("p7junk", [128, D], BF16)
    st_, st_u = M.alloc("p7stat", [128, 2 * NT], F32)
    nb = 0
    nr = 0
    ngrp = DFF // FG
    for g in range(ngrp):
        for b2 in range(FG // 512):
            wt, wkey, _ = k.wnext()
            for oc4 in range(4):
                fc = b2 * 4 + oc4
                msl = slice(oc4 * 128, (oc4 + 1) * 128)
                for (t0, n) in TBLK:
                    bank = nb % 4
                    nb += 1
                    pkey = ("ps", bank)
                    ps = k.ps[bank][:, 0:n]
                    for kc in range(KC):
                        _mm(S, ps, wt[:, kc, msl], h2T[:, kc, t0:t0 + n], kc == 0, kc == KC - 1,
                            h2T_keys(t0, n, kc) + [wkey], [pkey], counted=(kc == KC - 1))
                    r_ = rt[nr % 2]
                    rkey = ("p7r", nr % 2)
                    nr += 1
                    S.op("act", _mk(lambda e, o, a: e.activation(out=o, in_=a, func=AF.Relu), r_[0][:, 0:n], ps),
                         reads=[pkey], writes=[rkey])
                    S.op("dve", _mk(lambda e, o, a, b_: e.scalar_tensor_tensor(o, a, 0.0, b_, ALU.max, ALU.mult), f1T[:, fc, t0:t0 + n], ps, r_[0][:, 0:n]),
                         reads=[pkey, rkey], writes=[("f1T", fc, t0)])
        for cb in range(4):
            wt, wkey, blk = k.wnext()
            nk = blk[3]
            csl = slice(cb * 512, (cb + 1) * 512)
            for tt in range(NT):
                tsl = slice(tt * 128, (tt + 1) * 128)
                bank = 4 + nb % 4
                nb += 1
                pkey = ("ps", bank)
                for kc in range(nk):
                    _mm(S, k.ps[bank][:, :], f1T[:, kc, tsl], wt[:, kc, :], kc == 0, kc == nk - 1,
                        [("f1T", kc, (tt * 128) // 512 * 512), wkey], [pkey], counted=(kc == nk - 1))
                fkey = ("facc", tt, cb)
                if g == 0:
                    S.op("act", _mk(lambda e, o, a: e.copy(o, a), facc[:, tt, csl], k.ps[bank][:, :]), reads=[pkey], writes=[fkey])
                else:
                    S.op("dve", _mk(lambda e, o, a: e.tensor_tensor(o, o, a, ALU.add), facc[:, tt, csl], k.ps[bank][:, :]),
                         reads=[pkey, fkey], writes=[fkey])
    def ld_x1(t):
        bb = t % 2
        k.ld("p7x1_%d" % bb, x1r[bb][0][:], k.x1_d[t * 128:(t + 1) * 128, :], [("p7x1", bb)])

    ld_x1(0)
    ld_x1(1)
    for tt in range(NT):
        b = tt % 2
        tsl = slice(tt * 128, (tt + 1) * 128)
        fk = [("facc", tt, cb) for cb in range(4)]
        xr, xrkey = x1r[b][0], ("p7x1", b)
        ss = st_[:, 2 * tt:2 * tt + 1]
        rs = st_[:, 2 * tt + 1:2 * tt + 2]
        S.op("act", _mk(lambda e, o, a, s: e.activation(out=o, in_=a, func=AF.Square, accum_out=s), junk[:, :], facc[:, tt, :], ss),
             reads=fk, writes=[("p7junk",), ("p7s", tt)])
        S.op("act", _mk(lambda e, s, ep: e.activation(out=s, in_=s, func=AF.Sqrt, bias=ep, scale=1.0 / D), ss, k.eps_t[:, :]),
             reads=[("p7s", tt), ("eps",)], writes=[("p7s", tt)])
        S.op("dve", _mk(lambda e, o, a: e.reciprocal(o, a), rs, ss), reads=[("p7s", tt)], writes=[("p7rs", tt)])
        S.op("dve", _mk(lambda e, o, s, g_: e.scalar_tensor_tensor(o, o, s, g_, ALU.mult, ALU.mult), facc[:, tt, :], rs, gpo[:, :]),
             reads=fk + [("p7rs", tt), ("gpo2",)], writes=fk)
        S.op("pool", _mk(lambda e, o, b_: e.tensor_tensor(o, o, b_, ALU.add), facc[:, tt, :], xr[:, :]),
             reads=fk + [xrkey], writes=fk + [("yt", tt)])
        if tt + 2 < NT:
            ld_x1(tt + 2)
        k.st("y%d" % b, k.y[tsl, :], facc[:, tt, :], [("yt", tt)])
    for lst in (rt, x1r):
        for t_, u in lst:
            M.free(u)
    for u in (f1T_u, facc_u, gpo_u, junk_u, st_u, k.h2T_u):
        M.free(u)
    k.end_phase()
```

```python
import bisect
from contextlib import ExitStack

import numpy as np
import concourse.bass as bass
import concourse.mybir as mybir
from concourse.bass_utils import run_bass_kernel_spmd

F32 = mybir.dt.float32
BF16 = mybir.dt.bfloat16
ALU = mybir.AluOpType
AF = mybir.ActivationFunctionType
AX = mybir.AxisListType

NCORES = 8
D = 2048
KC = D // 128
TP = 1024
NT = 9
T = NT * 128
SEQ = 8192
DEC_B, DEC_T = 16, 64
PAST = 2048
PW = 1024
LAG = 15
NH, DK, DV = 8, 128, 256
IN_W = 11264
DFF = 8192
EPS = 1e-6
OFF_U, OFF_Q, OFF_K, OFF_V, OFF_G, OFF_AP, OFF_AR = 0, 1024, 2048, 3072, 5120, 7168, 9216
TBLK = [(0, 512), (512, 512), (1024, 128)]
FG = 1024
NPRE = 56
NSLOT = 2
LOG_GAMMA = [float(np.log1p(-np.exp2(np.float32(-5.0 - h)))) for h in range(NH)]
G128 = [float(np.exp(128.0 * lg)) for lg in LOG_GAMMA]
G64 = [float(np.exp(64.0 * lg)) for lg in LOG_GAMMA]

ENGS = ("pe", "act", "dve", "pool", "sp")


class Sched:
    def __init__(self, nc, es):
        self.nc = nc
        self.es = es
        self.items = {e: [] for e in ENGS}
        self.npos = {e: 0 for e in ENGS}
        self.counted = {e: [] for e in ENGS}
        self.known = {e: {} for e in ENGS}
        self.lw = {}
        self.rd = {}
        self.dma_val = {}
        self.dma_sem = {}
        self.esem = {e: es.enter_context(nc.semaphore("c_" + e)) for e in ENGS}
        self.nwaits = 0
        self.nops = 0
        self.simsem = {}

    def _deps(self, reads, writes):
        deps = []
        for r in reads:
            t = self.lw.get(r)
            if t is not None:
                deps.append(t)
        for w in writes:
            t = self.lw.get(w)
            if t is not None:
                deps.append(t)
            deps.extend(self.rd.get(w, ()))
        return deps

    def _commit(self, tok, reads, writes):
        for r in reads:
            self.rd.setdefault(r, []).append(tok)
        for w in writes:
            self.lw[w] = tok
            self.rd[w] = []

    def op(self, eng, fn, reads=(), writes=(), counted=True):
        reads, writes = list(reads), list(writes)
        deps = self._deps(reads, writes)
        pos = self.npos[eng]
        self.npos[eng] += 1
        if counted:
            self.counted[eng].append(pos)
        it = dict(kind="op", fn=fn, deps=deps, counted=counted, pos=pos)
        self.items[eng].append(it)
        self._commit(("e", eng, pos), reads, writes)
        self.nops += 1
        return it

    def dma(self, queue, chan, fn, reads=(), writes=(), inc=16):
        reads, writes = list(reads), list(writes)
        deps = self._deps(reads, writes)
        if chan not in self.dma_sem:
            self.dma_sem[chan] = self.es.enter_context(self.nc.semaphore("d_" + chan))
            self.dma_val[chan] = 0
        self.dma_val[chan] += inc
        tok = ("d", chan, self.dma_val[chan])
        it = dict(kind="dma", fn=fn, deps=deps, chan=chan, inc=inc)
        self.items[queue].append(it)
        self._commit(tok, reads, writes)
        return tok

    def wait_tok(self, eng, toks):
        self.items[eng].append(dict(kind="wait", deps=list(toks)))

    def _resolve(self, tok):
        if tok[0] == "d":
            return ("d", tok[1]), self.dma_sem[tok[1]], tok[2]
        _, eng, pos = tok
        lst = self.counted[eng]
        i = bisect.bisect_left(lst, pos)
        assert i < len(lst), ("dependency on trailing uncounted op", eng, pos)
        return ("e", eng), self.esem[eng], i + 1

    def flush(self):
        for e in ENGS:
            ops = [it for it in self.items[e] if it["kind"] == "op"]
            if ops and not ops[-1]["counted"]:
                ops[-1]["counted"] = True
                bisect.insort(self.counted[e], ops[-1]["pos"])
        nc = self.nc
        simlog = {e: [] for e in ENGS}
        with nc.Block() as block:
            def emit(ename, eng):
                known = self.known[ename]
                for it in self.items[ename]:
                    for tok in it["deps"]:
                        if tok[0] == "e" and tok[1] == "pe" and ename == "pe":
                            continue
                        key, sem, val = self._resolve(tok)
                        if known.get(key, 0) >= val:
                            continue
                        eng.wait_ge(sem, val)
                        known[key] = val
                        self.nwaits += 1
                        simlog[ename].append(("wait", key, val))
                    if it["kind"] == "wait":
                        continue
                    ins = it["fn"](eng)
                    if it["kind"] == "dma":
                        ins.then_inc(self.dma_sem[it["chan"]], it["inc"])
                        simlog[ename].append(("inc", ("d", it["chan"]), it["inc"]))
                    elif it["counted"]:
                        ins.then_inc(self.esem[ename], 1)
                        simlog[ename].append(("inc", ("e", ename), 1))

            @block.tensor
            def _(eng):
                emit("pe", eng)

            @block.scalar
            def _(eng):
                emit("act", eng)

            @block.vector
            def _(eng):
                emit("dve", eng)

            @block.gpsimd
            def _(eng):
                emit("pool", eng)

            @block.sync
            def _(eng):
                emit("sp", eng)
        self._simulate(simlog)
        for e in ENGS:
            for e2 in ENGS:
                self.known[e][("e", e2)] = len(self.counted[e2])
        self.items = {e: [] for e in ENGS}
        self.lw = {k: v for k, v in self.lw.items() if v[0] == "d"}
        self.rd = {k: [t for t in v if t[0] == "d"] for k, v in self.rd.items()}
        self.rd = {k: v for k, v in self.rd.items() if v}


def _sched_simulate(self, simlog):
    sem = dict(self.simsem)
    pc = {e: 0 for e in ENGS}
    progress = True
    while progress:
        progress = False
        for e in ENGS:
            lst = simlog[e]
            while pc[e] < len(lst):
                kind, key, val = lst[pc[e]]
                if kind == "wait":
                    if sem.get(key, 0) < val:
                        break
                else:
                    sem[key] = sem.get(key, 0) + val
                pc[e] += 1
                progress = True
    stuck = {e: (pc[e], len(simlog[e]), simlog[e][pc[e]]) for e in ENGS if pc[e] < len(simlog[e])}
    assert not stuck, ("DEADLOCK in phase", stuck, {k_: sem.get(k_) for k_ in [v[2][1] for v in stuck.values()]})
    self.simsem = sem


Sched._simulate = _sched_simulate


class Mem:
    BASE = 16512
    LIMIT = 229376

    def __init__(self, nc):
        self.nc = nc
        self.live = {}
        self.pending = []
        self.n = 0
        self.peak = 0

    def alloc(self, name, shape, dtype):
        esz = 2 if dtype == BF16 else 4
        size = int(np.prod(shape[1:])) * esz
        size = (size + 31) // 32 * 32
        segs = sorted(self.live.values())
        off = self.BASE
        for (o, s) in segs:
            if off + size <= o:
                break
            off = max(off, o + s)
        assert off + size <= self.LIMIT, ("SBUF OOM", name, size, off, sorted(self.live.items(), key=lambda kv: kv[1]))
        self.n += 1
        uname = "%s_%d" % (name, self.n)
        self.live[uname] = (off, size)
        self.peak = max(self.peak, off + size)
        t = self.nc.alloc_sbuf_tensor_at(uname, list(shape), dtype, offset=off)
        return t, uname

    def free(self, uname):
        self.pending.append(uname)

    def commit(self):
        for u in self.pending:
            self.live.pop(u, None)
        self.pending = []


class K:
    pass


def _mk(fn, *a, **kw):
    return lambda eng: fn(eng, *a, **kw)


def build_program(stop_after=99, dbg=()):
    nc = bass.Bass("TRN2", target_bir_lowering=False)
    es = ExitStack()
    S = Sched(nc, es)
    M = Mem(nc)
    k = K()
    k.nc, k.S, k.M, k.es = nc, S, M, es
    k.dbg = set(dbg)
    k.dbg_outs = []

    def din(name, shape):
        return nc.dram_tensor(name, list(shape), F32, kind="ExternalInput").ap()

    def dout(name, shape):
        return nc.dram_tensor(name, list(shape), F32, kind="ExternalOutput").ap()

    k.x = din("x", [T, D])
    k.xh = din("xh", [16, D])
    k.xprev = din("xprev", [NPRE * 128, D])
    k.t_cosp = din("t_cosp", [NPRE * 128, 64])
    k.t_sinp = din("t_sinp", [NPRE * 128, 64])
    k.cpool = din("cpool", [2, LAG, PW])
    k.sret = din("sret", [2, NH, DK, DV])
    k.w_in = din("w_in", [D, IN_W])
    k.w_pool = din("w_pool", [4, 256, 256])
    k.w_pool_out = din("w_pool_out", [PW, D])
    k.w_ret_out = din("w_ret_out", [D, D])
    k.w_o = din("w_o", [D, D])
    k.w_up = din("w_up", [D, DFF])
    k.w_down = din("w_down", [DFF, D])
    k.g_pre_mix = din("g_pre_mix", [128, KC])
    k.g_pre_ffn = din("g_pre_ffn", [128, KC])
    k.pscale = din("pscale", [128, 8])
    k.g_post_mix = din("g_post_mix", [128, D])
    k.g_post_ffn = din("g_post_ffn", [128, D])
    k.t_cos = din("t_cos", [T, 64])
    k.t_sin = din("t_sin", [T, 64])
    k.t_dec = din("t_dec", [128, 32])
    k.t_mask = din("t_mask", [128, 256])
    k.t_coef = din("t_coef", [128, 64])
    k.t_invc = din("t_invc", [128, 64])
    k.t_ident = din("t_ident", [128, 128])

    k.y = dout("y", [T, D])
    k.pool_p = dout("pool_p", [LAG, PW])
    k.ret_p = dout("ret_p", [NH, DK, DV])
    k.pool_s = dout("pool_s", [2, LAG, PW])
    k.ret_s = dout("ret_s", [2, NH, DK, DV])

    k.x1_d = nc.dram_tensor("x1_scratch", [T, D], F32).ap()

    k.ps = [nc.alloc_psum_tensor("ps%d" % i, [128, 512], F32) for i in range(8)]
    k.wslot = [M.alloc("wslot%d" % i, [128, KC, 512], BF16)[0] for i in range(NSLOT)]
    k.ident_f, _ = M.alloc("ident_f", [128, 128], F32)
    k.ident_b, _ = M.alloc("ident_b", [128, 128], BF16)
    k.eps_t, _ = M.alloc("eps_t", [128, 1], F32)
    k.gpm, _ = M.alloc("gpm", [128, KC], F32)
    k.gpf, _ = M.alloc("gpf", [128, KC], F32)
    k.psc, _ = M.alloc("psc", [128, 8], F32)
    k.dec, _ = M.alloc("dec", [128, 32], F32)
    k.mask, _ = M.alloc("mask", [128, 256], F32)
    k.coef, _ = M.alloc("coef", [128, 64], F32)
    k.invc, _ = M.alloc("invc", [128, 64], F32)

    k.phase_toks = []

    def ld(chan, dst, src, writes, queue="sp"):
        tok = S.dma(queue, chan, lambda e: e.dma_start(out=dst, in_=src), writes=writes)
        k.phase_toks.append(tok)
        return tok
    k.ld = ld

    def st(chan, dst, src, reads, queue="sp", writes=()):
        tok = S.dma(queue, chan, lambda e: e.dma_start(out=dst, in_=src), reads=reads, writes=writes)
        k.phase_toks.append(tok)
        return tok
    k.st = st

    def end_phase():
        S.wait_tok("sp", k.phase_toks)
        k.phase_toks = []
        S.flush()
        M.commit()
    k.end_phase = end_phase

    def dump(name, sb_ap, shape, reads, dtype=F32):
        if name not in k.dbg:
            return
        d = nc.dram_tensor("dbg_" + name, list(shape), dtype, kind="ExternalOutput").ap()
        k.dbg_outs.append("dbg_" + name)
        st("dbg_" + name, d, sb_ap, reads)
    k.dump = dump

    ld("c_identf", k.ident_f[:], k.t_ident, [("ident_f",)])
    ld("c_identb", k.ident_b[:], k.t_ident, [("ident_b",)], queue="pool")
    ld("c_gpm", k.gpm[:], k.g_pre_mix, [("gpm",)])
    ld("c_gpf", k.gpf[:], k.g_pre_ffn, [("gpf",)])
    ld("c_psc", k.psc[:], k.pscale, [("psc",)])
    ld("c_dec", k.dec[:], k.t_dec, [("dec",)])
    ld("c_mask", k.mask[:], k.t_mask, [("mask",)])
    ld("c_coef", k.coef[:], k.t_coef, [("coef",)])
    ld("c_invc", k.invc[:], k.t_invc, [("invc",)])
    S.op("dve", lambda e: e.memset(k.eps_t[:], EPS), writes=[("eps",)])

    blocks = []
    def wblk(ap2d, r0, c0, nk=KC):
        blocks.append((ap2d, r0, c0, nk))
    for c0 in (OFF_Q, OFF_Q + 512, OFF_K, OFF_K + 512):
        wblk(k.w_in, 0, c0)
    for j in range(4):
        wblk(k.w_in, 0, OFF_V + 512 * j)
    for j in range(4):
        wblk(k.w_in, 0, OFF_G + 512 * j)
    for j in range(2):
        wblk(k.w_in, 0, OFF_U + 512 * j)
    for j in range(4):
        wblk(k.w_in, 0, OFF_AP + 512 * j)
        wblk(k.w_pool_out, 0, 512 * j, 8)
        wblk(k.w_in, 0, OFF_AR + 512 * j)
        wblk(k.w_ret_out, 0, 512 * j)
    k.n_pre_wo = len(blocks)
    for j in range(4):
        wblk(k.w_o, 0, 512 * j)
    for g in range(DFF // FG):
        for j in range(FG // 512):
            wblk(k.w_up, 0, g * FG + 512 * j)
        for j in range(4):
            wblk(k.w_down, g * FG, 512 * j, FG // 128)
    k.blocks = blocks
    k.wcur = 0
    k.wloaded = 0

    def wload(j, slot_tile, key, chan):
        ap2d, r0, c0, nk = k.blocks[j]
        src = ap2d[r0:r0 + nk * 128, c0:c0 + 512].rearrange("(kc p) n -> p kc n", p=128)
        S.dma("pool", chan, lambda e: e.dma_start(out=slot_tile[:, 0:nk, :], in_=src), writes=[key])

    def wprefetch(n):
        for _ in range(n):
            i = k.wloaded
            if i >= len(k.blocks) or i >= k.wcur + NSLOT:
                break
            wload(i, k.wslot[i % NSLOT], ("w", i % NSLOT), "w%d" % (i % NSLOT))
            k.wloaded += 1
    k.wprefetch = wprefetch

    def wnext():
        j = k.wcur
        k.wcur += 1
        while k.wloaded < min(j + NSLOT, len(k.blocks)):
            i = k.wloaded
            wload(i, k.wslot[i % NSLOT], ("w", i % NSLOT), "w%d" % (i % NSLOT))
            k.wloaded += 1
        return k.wslot[j % NSLOT], ("w", j % NSLOT), k.blocks[j]
    k.wnext = wnext

    phases = [phase0_prepare, phase1, phase0, phase3, phase4, phase2, phase5, phase6, phase7]
    for i, ph in enumerate(phases):
        if i >= stop_after:
            break
        ph(k)
    if k.phase_toks or any(S.items[e] for e in ENGS):
        end_phase()
    es.close()
    return nc, k


def norm_to_featmajor(k, name, x_src_fn, ntile, rows_fn, g_tile, gkey, hT, hT_key_fn, hT_cols_fn,
                      x_keep=None, nbuf=2):
    S, M = k.S, k.M
    xt = [M.alloc(name + "_xt%d" % i, [128, D], F32) for i in range(nbuf)]
    xn = [M.alloc(name + "_xn%d" % i, [128, D], BF16) for i in range(nbuf)]
    junk = M.alloc(name + "_junk", [128, D], BF16)
    st_ = M.alloc(name + "_stat", [128, 2 * ntile + 2], F32)
    stat = st_[0]
    def stage_A(i):
        rows = rows_fn(i)
        b = i % nbuf
        if x_keep is None:
            xa = xt[b][0][0:rows, :]
            xkey = (name + "_xt", b)
            k.ld(name + "_x%d" % b, xa, x_src_fn(i), [xkey])
        else:
            xa, xkey = x_keep(i)
        ss = stat[0:rows, 2 * i:2 * i + 1]
        rs = stat[0:rows, 2 * i + 1:2 * i + 2]
        skey = (name + "_stat", i)
        S.op("act", _mk(lambda e, o, a, s: e.activation(out=o, in_=a, func=AF.Square, accum_out=s),
                        junk[0][0:rows, :], xa, ss),
             reads=[xkey], writes=[(name + "_junk",), skey])
        S.op("act", _mk(lambda e, s, ep: e.activation(out=s, in_=s, func=AF.Sqrt, bias=ep, scale=1.0 / D),
                        ss, k.eps_t[0:rows, :]),
             reads=[skey, ("eps",)], writes=[skey])
        S.op("dve", _mk(lambda e, o, a: e.reciprocal(o, a), rs, ss), reads=[skey], writes=[(name + "_rs", i)])
        xna = xn[b][0][0:rows, :]
        xnkey = (name + "_xn", b)
        S.op("dve", _mk(lambda e, o, a, s: e.tensor_scalar(o, a, s, None, ALU.mult), xna, xa, rs),
             reads=[xkey, (name + "_rs", i)], writes=[xnkey])

    def stage_B(i):
        rows = rows_fn(i)
        b = i % nbuf
        xnkey = (name + "_xn", b)
        c0, ncol = hT_cols_fn(i)
        for g4 in range(KC // 4):
            bank = (g4 % 2) + 2 * (i % 2)
            bkey = ("ps", bank)
            pb = k.ps[bank][:, :].bitcast(BF16)
            for j in range(4):
                kc = g4 * 4 + j
                S.op("pe", _mk(lambda e, o, a, idn: e.transpose(o, a, idn),
                               pb[:, j * 128:j * 128 + rows], xn[b][0][0:rows, kc * 128:(kc + 1) * 128],
                               k.ident_b[0:rows, 0:rows]),
                     reads=[xnkey, ("ident_b",)], writes=[bkey], counted=(j == 3))
            src = pb[:, 0:512].rearrange("p (j t) -> p j t", j=4)[:, :, 0:rows]
            dst = hT[:, g4 * 4:g4 * 4 + 4, c0:c0 + ncol]
            gb = g_tile[:, g4 * 4:g4 * 4 + 4].unsqueeze(2).to_broadcast([128, 4, rows])
            S.op("dve", _mk(lambda e, o, a, g: e.tensor_tensor(o, a, g, ALU.mult), dst, src, gb),
                 reads=[bkey, gkey], writes=[hT_key_fn(i, g4)])

    stage_A(0)
    for i in range(ntile):
        if i + 1 < ntile:
            stage_A(i + 1)
        stage_B(i)
    for t_, u in xt + xn + [junk, st_]:
        M.free(u)


def phase1(k):
    S, M = k.S, k.M
    hT, k.hT_u = M.alloc("hT", [128, KC, T], BF16)
    hTh, k.hTh_u = M.alloc("hTh", [128, KC, 16], BF16)
    k.hT, k.hTh = hT, hTh

    def rows_fn(i):
        return 16 if i == 0 else 128

    def src_fn(i):
        return k.xh if i == 0 else k.x[(i - 1) * 128:i * 128, :]

    def key_fn(i, g4):
        return ("hTh", g4) if i == 0 else ("hT", i - 1, g4)

    def run(i_list):
        pass
    class _HT:
        def __getitem__(self_, idx):
            raise NotImplementedError
    norm_to_featmajor(k, "p1h", lambda i: k.xh, 1, lambda i: 16, k.gpm, ("gpm",), hTh,
                      lambda i, g4: ("hTh", g4), lambda i: (0, 16), nbuf=1)
    norm_to_featmajor(k, "p1", lambda i: k.x[i * 128:(i + 1) * 128, :], NT, lambda i: 128, k.gpm, ("gpm",), hT,
                      lambda i, g4: ("hT", i, g4), lambda i: (i * 128, 128))
    k.dump("hT", hT[:], [128, KC, T], [("hT", i, g4) for i in range(NT) for g4 in range(4)], BF16)
    k.end_phase()


def hT_keys(t0, ntok, kc):
    return [("hT", t, kc // 4) for t in range(t0 // 128, (t0 + ntok + 127) // 128)]


def phase3(k):
    S, M = k.S, k.M
    hT = k.hT
    qT, k.qT_u = M.alloc("qT", [128, NH, T], BF16)
    kt, k.kt_u = M.alloc("kt", [128, NT, NH * DK], BF16)
    vg, k.vg_u = M.alloc("vg", [128, NH, NT * DV], BF16)
    k.qT, k.kt, k.vg = qT, kt, vg
    cs_c, u_c = M.alloc("cs_c", [128, NT, 64], F32)
    cs_s, u_s = M.alloc("cs_s", [128, NT, 64], F32)
    k.ld("cs_c", cs_c[:], k.t_cos.rearrange("(t p) f -> p t f", p=128), [("cs_c",)])
    k.ld("cs_s", cs_s[:], k.t_sin.rearrange("(t p) f -> p t f", p=128), [("cs_s",)])
    tmp = [[M.alloc("rt%d_%d" % (s, i), [128, 4, 64], F32) for i in range(6)] for s in range(2)]
    qtok = [M.alloc("qtok%d" % s, [128, 512], BF16) for s in range(3)]
    Rl, u_Rl = M.alloc("Rl", [128, NH, DV], F32)
    it = 0
    pend = []
    for blk in range(4):
        wt, wkey, _ = k.wnext()
        is_q = blk < 2
        hb = blk % 2
        for tt in range(NT):
            var = 0 if tt < 8 else 1
            s = it % 2
            bank = it % 4
            it += 1
            ps = k.ps[bank]
            pkey = ("ps", bank)
            for kc in range(KC):
                S.op("pe", _mk(lambda e, o, l, r, st, sp: e.matmul(o, lhsT=l, rhs=r, start=st, stop=sp),
                               ps[:, :], hT[:, kc, tt * 128:(tt + 1) * 128], wt[:, kc, :], kc == 0, kc == KC - 1),
                     reads=[("hT", tt, kc // 4), wkey], writes=[pkey], counted=(kc == KC - 1))
            pv = ps[:, :].rearrange("p (h f two) -> p h f two", h=4, two=2)
            xe, xo = pv[:, :, :, 0], pv[:, :, :, 1]
            cb = cs_c[:, tt, :].unsqueeze(1).to_broadcast([128, 4, 64])
            sb = cs_s[:, tt, :].unsqueeze(1).to_broadcast([128, 4, 64])
            t1, t2, t3, t4, re, ro = [tmp[s][i][0] for i in range(6)]
            tk = [("rt", s, i) for i in range(6)]
            for (o, a, b_, ky) in ((t1, xe, cb, tk[0]), (t2, xo, sb, tk[1]), (t3, xe, sb, tk[2]), (t4, xo, cb, tk[3])):
                S.op("dve", _mk(lambda e, o, a, b_: e.tensor_tensor(o[:], a, b_, ALU.mult), o, a, b_),
                     reads=[pkey, ("cs_c",), ("cs_s",)], writes=[ky])
            S.op("pool", _mk(lambda e, o, a, b_: e.tensor_tensor(o[:], a[:], b_[:], ALU.subtract), re, t1, t2),
                 reads=[tk[0], tk[1]], writes=[tk[4]])
            S.op("pool", _mk(lambda e, o, a, b_: e.tensor_tensor(o[:], a[:], b_[:], ALU.add), ro, t3, t4),
                 reads=[tk[2], tk[3]], writes=[tk[5]])
            di = (0 if is_q else 16) + var * 8 + hb * 4
            db = k.dec[:, di:di + 4].unsqueeze(2).to_broadcast([128, 4, 64])
            qs = it % 3
            if is_q:
                dst2d = qtok[qs][0][:, :]
                dkey = ("qtok", qs)
            else:
                dst2d = kt[:, tt, hb * 512:(hb + 1) * 512]
                dkey = ("kt", tt, hb)
            dv_ = dst2d.rearrange("p (h f two) -> p h f two", h=4, two=2)
            S.op("pool", _mk(lambda e, o, a, b_: e.tensor_tensor(o, a[:], b_, ALU.mult), dv_[:, :, :, 0], re, db),
                 reads=[tk[4], ("dec",)], writes=[dkey])
            S.op("pool", _mk(lambda e, o, a, b_: e.tensor_tensor(o, a[:], b_, ALU.mult), dv_[:, :, :, 1], ro, db),
                 reads=[tk[5], ("dec",)], writes=[dkey])
            if is_q:
                pend.append((qs, hb, tt, dkey, 4 + (it % 2)))
            while pend and (not is_q or len(pend) > 2 or tt == NT - 1):
                s_, hb_, tt_, dkey_, tb = pend.pop(0)
                pb = k.ps[tb][:, :].bitcast(BF16)
                tkey = ("ps", tb)
                for j in range(4):
                    S.op("pe", _mk(lambda e, o, a, idn: e.transpose(o, a, idn),
                                   pb[:, j * 128:(j + 1) * 128], qtok[s_][0][:, j * 128:(j + 1) * 128], k.ident_b[:, :]),
                         reads=[dkey_, ("ident_b",)], writes=[tkey], counted=(j == 3))
                S.op("act", _mk(lambda e, o, a: e.copy(o, a),
                                qT[:, hb_ * 4:hb_ * 4 + 4, tt_ * 128:(tt_ + 1) * 128],
                                pb[:, 0:512].rearrange("p (j t) -> p j t", j=4)),
                     reads=[tkey], writes=[("qT", hb_ * 4 + j, tt_) for j in range(4)])
    for blk in range(4):
        wt, wkey, _ = k.wnext()
        for tt in range(NT):
            bank = it % 4
            it += 1
            ps = k.ps[bank]
            pkey = ("ps", bank)
            for kc in range(KC):
                S.op("pe", _mk(lambda e, o, l, r, st, sp: e.matmul(o, lhsT=l, rhs=r, start=st, stop=sp),
                               ps[:, :], hT[:, kc, tt * 128:(tt + 1) * 128], wt[:, kc, :], kc == 0, kc == KC - 1),
                     reads=[("hT", tt, kc // 4), wkey], writes=[pkey], counted=(kc == KC - 1))
            S.op("act", _mk(lambda e, o, a: e.copy(o, a),
                            vg[:, 2 * blk:2 * blk + 2, tt * DV:(tt + 1) * DV],
                            ps[:, :].rearrange("p (h e) -> p h e", h=2)),
                 reads=[pkey], writes=[("vg", 2 * blk, tt), ("vg", 2 * blk + 1, tt)])
    k.dump("qT", qT[:], [128, NH, T], [("qT", h, tt) for h in range(NH) for tt in range(NT)], BF16)
    k.dump("kt", kt[:], [128, NT, NH * DK], [("kt", tt, hb) for tt in range(NT) for hb in range(2)], BF16)
    k.dump("vg", vg[:], [128, NH, NT * DV], [("vg", h, tt) for h in range(NH) for tt in range(NT)], BF16)
    for tt in (range(8) if "Sloc" in k.dbg else ()):
        for h in range(NH):
            bank = 4 + (h // 2) % 2 + 2 * (tt % 2)
            half = h % 2
            pkey = ("ps", bank, half)
            pd = k.ps[bank][:, half * 256:(half + 1) * 256]
            S.op("pe", _mk(lambda e, o, l, r: e.matmul(o, lhsT=l, rhs=r, start=True, stop=True),
                           pd, kt[:, tt, h * DK:(h + 1) * DK], vg[:, h, tt * DV:(tt + 1) * DV]),
                 reads=[("kt", tt, h // 4), ("vg", h, tt)], writes=[pkey])
            if tt == 0:
                S.op("dve", _mk(lambda e, o, a: e.tensor_copy(o, a), Rl[:, h, :], pd),
                     reads=[pkey], writes=[("Rl", h)])
            else:
                S.op("dve", _mk(lambda e, o, a, g: e.scalar_tensor_tensor(o, o, g, a, ALU.mult, ALU.add),
                                Rl[:, h, :], pd, G128[h]),
                     reads=[pkey, ("Rl", h)], writes=[("Rl", h)])
    for h in (range(NH) if "Sloc" in k.dbg else ()):
        S.op("act", _mk(lambda e, o, g: e.mul(o, o, g), Rl[:, h, :], G128[h]),
             reads=[("Rl", h)], writes=[("Rl", h)])
    k.dump("Sloc", Rl[:], [128, NH, DV], [("Rl", h) for h in range(NH)])
    for lst in tmp:
        for t_, u in lst:
            M.free(u)
    for t_, u in qtok:
        M.free(u)
    M.free(u_c); M.free(u_s); M.free(u_Rl)
    k.end_phase()


def _tables(c):
    inv = (1.0 / (10000.0 ** np.linspace(0.0, 1.0, DK // 2, dtype=np.float32))).astype(np.float32)
    pos = np.concatenate([c * TP + np.arange(TP), PAST + np.arange(DEC_T), PAST + np.arange(DEC_T)])
    ang = pos.astype(np.float32)[:, None] * inv[None, :]
    t_cos = np.cos(ang).astype(np.float32)
    t_sin = np.sin(ang).astype(np.float32)
    lg = np.array(LOG_GAMMA, dtype=np.float64)
    p = np.arange(128)
    dec = np.zeros((128, 2, 2, NH), dtype=np.float64)
    for var, L in enumerate((128, 64)):
        e = (p % L + 1).astype(np.float64)[:, None]
        dec[:, 0, var, :] = np.exp(e * lg[None, :])
        dec[:, 1, var, :] = np.exp(-e * lg[None, :]) * (DK ** -0.5)
    t_dec = dec.reshape(128, 32).astype(np.float32)
    j = np.arange(128)[:, None]
    i = np.arange(128)[None, :]
    maskP = (i >= j).astype(np.float32)
    maskS = ((i >= j) & ((i // 64) == (j // 64))).astype(np.float32)
    t_mask = np.concatenate([maskP, maskS], axis=1)
    coef = np.zeros((NCORES, NH), dtype=np.float64)
    for r in range(NCORES):
        if r < c:
            coef[r] = np.exp(1024.0 * (c - 1 - r) * lg) / np.exp(128.0 * lg)
    t_coef = np.broadcast_to(coef.reshape(1, 64), (128, 64)).astype(np.float32)
    invc = np.zeros((4, 16), dtype=np.float64)
    for g, w in enumerate((2, 4, 8, 16)):
        invc[g] = 1.0 / np.minimum(c * TP + np.arange(16) + 1, w)
    t_invc = np.broadcast_to(invc.reshape(1, 64), (128, 64)).astype(np.float32)
    ppos = (np.arange(NPRE * 128) - (NCORES - 1 - c) * TP).astype(np.float32)
    pang = ppos[:, None] * inv[None, :]
    t_cosp = np.cos(pang).astype(np.float32)
    t_sinp = np.sin(pang).astype(np.float32)
    return dict(t_cosp=t_cosp, t_sinp=t_sinp, t_cos=t_cos, t_sin=t_sin, t_dec=t_dec, t_mask=t_mask,
                t_coef=np.ascontiguousarray(t_coef), t_invc=np.ascontiguousarray(t_invc),
                t_ident=np.eye(128, dtype=np.float32))


def prep_inputs(x_prompt, x_sample, cache_pool, state_retention, g_pre_mix, w_in, w_pool, pool_scale,
                w_pool_out, w_ret_out, w_o, g_post_mix, g_pre_ffn, w_up, w_down, g_post_ffn):
    f = lambda a: np.ascontiguousarray(np.asarray(a, dtype=np.float32))
    shared = dict(
        w_in=f(w_in[0]), w_pool=f(w_pool[0]), w_pool_out=f(w_pool_out[0]), w_ret_out=f(w_ret_out[0]),
        w_o=f(w_o[0]), w_up=f(w_up[0]), w_down=f(w_down[0]),
        g_pre_mix=f(np.asarray(g_pre_mix[0]).reshape(KC, 128).T),
        g_pre_ffn=f(np.asarray(g_pre_ffn[0]).reshape(KC, 128).T),
        pscale=f(np.asarray(pool_scale[0]).reshape(8, 128).T),
        g_post_mix=f(np.broadcast_to(np.asarray(g_post_mix[0])[None, :], (128, D))),
        g_post_ffn=f(np.broadcast_to(np.asarray(g_post_ffn[0])[None, :], (128, D))),
    )
    xp = np.asarray(x_prompt)[0]
    xs = np.asarray(x_sample)
    maps = []
    for c in range(NCORES):
        m = dict(shared)
        m["x"] = f(np.concatenate([xp[c * TP:(c + 1) * TP], xs[2 * c], xs[2 * c + 1]], axis=0))
        m["xh"] = f(xp[c * TP - 16:c * TP]) if c > 0 else np.zeros((16, D), np.float32)
        m["cpool"] = f(np.asarray(cache_pool)[0, 2 * c:2 * c + 2])
        m["sret"] = f(np.asarray(state_retention)[0, 2 * c:2 * c + 2])
        xpv = np.zeros((NPRE * 128, D), np.float32)
        if c > 0:
            xpv[(NCORES - 1 - c) * TP:] = xp[:c * TP]
        m["xprev"] = xpv
        m.update(_tables(c))
        maps.append(m)
    return maps


_PROG = {}


def kernel(**inputs):
    if "nc" not in _PROG:
        _PROG["nc"], _PROG["k"] = build_program()
    nc = _PROG["nc"]
    maps = prep_inputs(**inputs)
    res = run_bass_kernel_spmd(nc, maps, core_ids=list(range(NCORES)))
    R = res.results
    y = [np.asarray(r["y"]) for r in R]
    y_prompt = np.concatenate([a[:TP] for a in y], axis=0)[None]
    y_sample = np.concatenate([a[TP:].reshape(2, DEC_T, D) for a in y], axis=0)
    pool_prompt = np.asarray(R[NCORES - 1]["pool_p"])[None, None]
    ret_prompt = np.asarray(R[NCORES - 1]["ret_p"])[None, None]
    pool_sample = np.concatenate([np.asarray(r["pool_s"]) for r in R], axis=0)[None]
    ret_sample = np.concatenate([np.asarray(r["ret_s"]) for r in R], axis=0)[None]
    return tuple(np.ascontiguousarray(a.astype(np.float32)) for a in
                 (y_prompt, y_sample, pool_prompt, ret_prompt, pool_sample, ret_sample))


def _mm(S, out, lhsT, rhs, start, stop, reads, writes, counted=True):
    S.op("pe", _mk(lambda e, o, l, r, st, sp: e.matmul(o, lhsT=l, rhs=r, start=st, stop=sp),
                   out, lhsT, rhs, start, stop), reads=reads, writes=writes, counted=counted)


def _tr(S, out, in_, ident, reads, writes, counted=True):
    S.op("pe", _mk(lambda e, o, a, idn: e.transpose(o, a, idn), out, in_, ident),
         reads=reads, writes=writes, counted=counted)


def phase0_prepare(k):
    S, M = k.S, k.M
    Rp, k.Rp_u = M.alloc("Rp", [128, NH, DV], F32)
    k.Rp = Rp
    wk = [k.wslot[0], k.wslot[1]]
    wkk = [("w", 0), ("w", 1)]
    wv = [M.alloc("wv%d" % j, [128, KC, 512], BF16) for j in range(4)]
    k.p0w = (wk, wkk, wv)
    xt = [M.alloc("p0x0", [128, D], F32)]
    cs = [M.alloc("p0cs%d" % i, [128, 2, 64], F32) for i in range(2)]
    k.p0pre = (xt, cs)
    k.ld("p0x0", xt[0][0][:], k.xprev[0:128, :], [("p0x", 0)])
    for pt in range(2):
        k.ld("p0cs%d" % pt, cs[pt][0][:, 0, :], k.t_cosp[pt * 128:(pt + 1) * 128, :], [("p0cs", pt)])
        k.ld("p0cs%d" % pt, cs[pt][0][:, 1, :], k.t_sinp[pt * 128:(pt + 1) * 128, :], [("p0cs", pt)])
    for kind, j in (("k", 1), ("v", 2), ("v", 3), ("k", 0), ("v", 1), ("v", 0)):
        if kind == "k":
            src = k.w_in[:, OFF_K + 512 * j:OFF_K + 512 * (j + 1)].rearrange("(kc p) n -> p kc n", p=128)
            S.dma("pool", "w%d" % j, _mk(lambda e, o, s: e.dma_start(out=o, in_=s), wk[j][:], src), writes=[wkk[j]])
        else:
            src = k.w_in[:, OFF_V + 512 * j:OFF_V + 512 * (j + 1)].rearrange("(kc p) n -> p kc n", p=128)
            tok = S.dma("pool", "wv%d" % j, _mk(lambda e, o, s: e.dma_start(out=o, in_=s), wv[j][0][:], src),
                        writes=[("wv", j)])
            k.p0_toks = getattr(k, "p0_toks", []) + [tok]


def phase0(k):
    S, M = k.S, k.M
    Rp = k.Rp
    wk, wkk, wv = k.p0w
    k.phase_toks.extend(k.p0_toks)
    xt, cs = k.p0pre
    xt = xt + [M.alloc("p0x1", [128, D], F32)]
    xn = [M.alloc("p0xn%d" % i, [128, D], BF16) for i in range(2)]
    hTt = [M.alloc("p0h%d" % i, [128, KC, 128], BF16) for i in range(2)]
    tmp = [[M.alloc("p0rt%d_%d" % (s, i), [128, 4, 64], F32) for i in range(6)] for s in range(2)]
    ktl = [M.alloc("p0k%d" % i, [128, NH * DK], BF16) for i in range(2)]
    vtl = [M.alloc("p0v%d" % i, [128, NH * DV], BF16) for i in range(2)]
    stat_, stat_u = M.alloc("p0stat", [128, 2 * NPRE], F32)
    state = {"pj": 0}

    def hmin_of(pt):
        s = 7 - pt // 8
        return {1: 0, 2: 1, 3: 2, 4: 3, 5: 3, 6: 3, 7: 4}[s]
    started = set()

    def stage_A(pt):
        b = pt % 2
        xa = xt[b][0]
        xkey = ("p0x", b)
        ckey = ("p0cs", b)
        if pt >= 1:
            k.ld("p0x%d" % b, xa[:], k.xprev[pt * 128:(pt + 1) * 128, :], [xkey])
        if pt >= 2:
            k.ld("p0cs%d" % b, cs[b][0][:, 0, :], k.t_cosp[pt * 128:(pt + 1) * 128, :], [ckey])
            k.ld("p0cs%d" % b, cs[b][0][:, 1, :], k.t_sinp[pt * 128:(pt + 1) * 128, :], [ckey])
        ss = stat_[:, 2 * pt:2 * pt + 1]
        rs = stat_[:, 2 * pt + 1:2 * pt + 2]
        xna = xn[b][0]
        xnkey = ("p0xn", b)
        S.op("act", _mk(lambda e, o, a, s: e.activation(out=o, in_=a, func=AF.Square, accum_out=s), xna[:], xa[:], ss),
             reads=[xkey], writes=[xnkey, ("p0ss", pt)])
        S.op("act", _mk(lambda e, s, ep: e.activation(out=s, in_=s, func=AF.Sqrt, bias=ep, scale=1.0 / D), ss, k.eps_t[:, :]),
             reads=[("p0ss", pt), ("eps",)], writes=[("p0ss", pt)])
        S.op("dve", _mk(lambda e, o, a: e.reciprocal(o, a), rs, ss), reads=[("p0ss", pt)], writes=[("p0rs", pt)])
        S.op("dve", _mk(lambda e, o, a, s: e.tensor_scalar(o, a, s, None, ALU.mult), xna[:], xa[:], rs),
             reads=[xkey, ("p0rs", pt)], writes=[xnkey])

    def stage_B(pt):
        b = pt % 2
        xna = xn[b][0]
        xnkey = ("p0xn", b)
        hk = ("p0h", b)
        for g4 in range(4):
            bank = g4 % 2
            bkey = ("ps", bank)
            pb = k.ps[bank][:, :].bitcast(BF16)
            for j in range(4):
                kc = g4 * 4 + j
                _tr(S, pb[:, j * 128:(j + 1) * 128], xna[:, kc * 128:(kc + 1) * 128], k.ident_b[:, :],
                    [xnkey, ("ident_b",)], [bkey], counted=(j == 3))
            srcp = pb[:, 0:512].rearrange("p (j t) -> p j t", j=4)
            gb = k.gpm[:, g4 * 4:g4 * 4 + 4].unsqueeze(2).to_broadcast([128, 4, 128])
            S.op("dve", _mk(lambda e, o, a, g: e.tensor_tensor(o, a, g, ALU.mult), hTt[b][0][:, g4 * 4:g4 * 4 + 4, :], srcp, gb),
                 reads=[bkey, ("gpm",)], writes=[(hk, g4)])

    def stage_C(pt):
        b = pt % 2
        hk = ("p0h", b)
        ckey = ("p0cs", b)
        kkey = ("p0k", b)
        vkey = ("p0v", b)
        hmin = hmin_of(pt)
        for blk in range(2):
            h0 = max(hmin - 4 * blk, 0)
            if h0 >= 4:
                continue
            nh = 4 - h0
            ncol = nh * 128
            bank = 2 + state["pj"] % 4
            state["pj"] += 1
            ps = k.ps[bank]
            pkey = ("ps", bank)
            for kc in range(KC):
                _mm(S, ps[:, 0:ncol], hTt[b][0][:, kc, :], wk[blk][:, kc, h0 * 128:512], kc == 0, kc == KC - 1,
                    [(hk, kc // 4), wkk[blk]], [pkey], counted=(kc == KC - 1))
            s = state["pj"] % 2
            pv = ps[:, 0:ncol].rearrange("p (h f two) -> p h f two", h=nh, two=2)
            xe, xo = pv[:, :, :, 0], pv[:, :, :, 1]
            cb = cs[b][0][:, 0, :].unsqueeze(1).to_broadcast([128, nh, 64])
            sb = cs[b][0][:, 1, :].unsqueeze(1).to_broadcast([128, nh, 64])
            t1, t2, t3, t4, re, ro = [tmp[s][i][0][:, 0:nh, :] for i in range(6)]
            tk = [("p0rt", s, i) for i in range(6)]
            for (o, a, b_, ky) in ((t1, xe, cb, tk[0]), (t2, xo, sb, tk[1]), (t3, xe, sb, tk[2]), (t4, xo, cb, tk[3])):
                S.op("dve", _mk(lambda e, o, a, b_: e.tensor_tensor(o, a, b_, ALU.mult), o, a, b_),
                     reads=[pkey, ckey], writes=[ky])
            S.op("pool", _mk(lambda e, o, a, b_: e.tensor_tensor(o, a, b_, ALU.subtract), re, t1, t2),
                 reads=[tk[0], tk[1]], writes=[tk[4]])
            S.op("pool", _mk(lambda e, o, a, b_: e.tensor_tensor(o, a, b_, ALU.add), ro, t3, t4),
                 reads=[tk[2], tk[3]], writes=[tk[5]])
            di = 16 + blk * 4 + h0
            db = k.dec[:, di:di + nh].unsqueeze(2).to_broadcast([128, nh, 64])
            dv_ = ktl[b][0][:, blk * 512 + h0 * 128:(blk + 1) * 512].rearrange("p (h f two) -> p h f two", h=nh, two=2)
            S.op("pool", _mk(lambda e, o, a, b_: e.tensor_tensor(o, a, b_, ALU.mult), dv_[:, :, :, 0], re, db),
                 reads=[tk[4], ("dec",)], writes=[(kkey, blk)])
            S.op("pool", _mk(lambda e, o, a, b_: e.tensor_tensor(o, a, b_, ALU.mult), dv_[:, :, :, 1], ro, db),
                 reads=[tk[5], ("dec",)], writes=[(kkey, blk)])

    def stage_Cv(pt):
        b = pt % 2
        hk = ("p0h", b)
        vkey = ("p0v", b)
        hmin = hmin_of(pt)
        for blk in range(4):
            h0 = max(hmin - 2 * blk, 0)
            if h0 >= 2:
                continue
            c0 = h0 * 256
            ncol = 512 - c0
            bank = 2 + state["pj"] % 4
            state["pj"] += 1
            ps = k.ps[bank]
            pkey = ("ps", bank)
            for kc in range(KC):
                _mm(S, ps[:, 0:ncol], hTt[b][0][:, kc, :], wv[blk][0][:, kc, c0:512], kc == 0, kc == KC - 1,
                    [(hk, kc // 4), ("wv", blk)], [pkey], counted=(kc == KC - 1))
            S.op("act", _mk(lambda e, o, a: e.copy(o, a), vtl[b][0][:, blk * 512 + c0:(blk + 1) * 512], ps[:, 0:ncol]),
                 reads=[pkey], writes=[(vkey, blk)])

    def stage_D(pt):
        b = pt % 2
        kkey = ("p0k", b)
        vkey = ("p0v", b)
        for h in range(hmin_of(pt), NH):
            bank = 6 + (h // 2) % 2
            half = h % 2
            pkey = ("ps", bank, half)
            pd = k.ps[bank][:, half * 256:(half + 1) * 256]
            _mm(S, pd, ktl[b][0][:, h * DK:(h + 1) * DK], vtl[b][0][:, h * DV:(h + 1) * DV], True, True,
                [(kkey, h // 4), (vkey, h // 2)], [pkey])
            if h not in started:
                started.add(h)
                S.op("dve", _mk(lambda e, o, a: e.tensor_copy(o, a), Rp[:, h, :], pd), reads=[pkey], writes=[("Rp", h)])
            else:
                S.op("dve", _mk(lambda e, o, a, g: e.scalar_tensor_tensor(o, o, g, a, ALU.mult, ALU.add), Rp[:, h, :], pd, G128[h]),
                     reads=[pkey, ("Rp", h)], writes=[("Rp", h)])

    stage_A(0)
    for pt in range(NPRE):
        stage_B(pt)
        stage_C(pt)
        if pt > 0:
            stage_D(pt - 1)
        if pt + 1 < NPRE:
            stage_A(pt + 1)
        stage_Cv(pt)
    stage_D(NPRE - 1)
    k.wprefetch(2)
    k.dump("Rp", Rp[:], [128, NH, DV], [("Rp", h) for h in range(NH)])
    for lst in (wv, xt, xn, hTt, cs, ktl, vtl):
        for t_, u in lst:
            M.free(u)
    for lst in tmp:
        for t_, u in lst:
            M.free(u)
    M.free(stat_u)
    k.end_phase()


def phase4(k):
    S, M = k.S, k.M
    hT, qT, kt, vg, Rp = k.hT, k.qT, k.kt, k.vg, k.Rp
    sg, sg_u = M.alloc("sg", [128, NT, 512], BF16)
    ktT = [M.alloc("ktT%d" % i, [128, T], BF16) for i in range(2)]
    go = [M.alloc("go%d" % i, [128, NT, DV], BF16) for i in range(2)]
    Sbf, Sbf_u = M.alloc("Sbf", [128, 2, DV], BF16)
    S0, S0_u = M.alloc("S0", [128, 2, 2, DV], F32)
    S0b, S0b_u = M.alloc("S0b", [128, 2, 2, DV], BF16)
    sT = [M.alloc("sT%d" % i, [128, 128], BF16) for i in range(4)]
    qA = [M.alloc("qA%d" % i, [128, 128], BF16) for i in range(2)]
    qB = [M.alloc("qB%d" % i, [128, 128], BF16) for i in range(2)]
    junk, junk_u = M.alloc("p4junk", [128, DV], BF16)
    st_, st_u = M.alloc("p4stat", [128, 2 * NT * NH], F32)
    So = [M.alloc("So%d" % i, [128, DV], F32) for i in range(4)]
    for i in range(2):
        S.op("pool", _mk(lambda e, o: e.memset(o, 0.0), qA[i][0][:, :]), writes=[("qA", i)])
        S.op("pool", _mk(lambda e, o: e.memset(o, 0.0), qB[i][0][:, :]), writes=[("qB", i)])
    nsT = 0
    nSo = 0
    nps = 0
    sgb = [(sg, sg_u), M.alloc("sg2", [128, NT, 512], BF16)]

    def gr_proj(hp_):
        wt, wkey, _ = k.wnext()
        sgt = sgb[hp_ % 2][0]
        for tt in range(NT):
            bank = tt % 2
            pkey = ("ps", bank)
            for kc in range(KC):
                _mm(S, k.ps[bank][:, :], hT[:, kc, tt * 128:(tt + 1) * 128], wt[:, kc, :], kc == 0, kc == KC - 1,
                    [("hT", tt, kc // 4), wkey], [pkey], counted=(kc == KC - 1))
            S.op("act", _mk(lambda e, o, a: e.activation(out=o, in_=a, func=AF.Silu), sgt[:, tt, :], k.ps[bank][:, :]),
                 reads=[pkey], writes=[("sg", hp_ % 2, tt)])

    gr_proj(0)
    for hp in range(4):
        sg = sgb[hp % 2][0]
        for hl in range(2):
            h = 2 * hp + hl
            for (t0, n) in ((0, 4), (4, 4), (8, 1)):
                bank = 2 + nps % 2
                nps += 1
                bkey = ("ps", bank)
                pb = k.ps[bank][:, :].bitcast(BF16)
                for j in range(n):
                    _tr(S, pb[:, j * 128:(j + 1) * 128], kt[:, t0 + j, h * DK:(h + 1) * DK], k.ident_b[:, :],
                        [("kt", t0 + j, h // 4), ("ident_b",)], [bkey], counted=(j == n - 1))
                S.op("act", _mk(lambda e, o, a: e.copy(o, a), ktT[hl][0][:, t0 * 128:(t0 + n) * 128], pb[:, 0:n * 128]),
                     reads=[bkey], writes=[("ktT", hl)])
            S.op("act", _mk(lambda e, o, a, g: e.mul(o, a, g), Sbf[:, hl, :], Rp[:, h, :], G128[h]),
                 reads=[("Rp", h)], writes=[("Sbf", hl)])
            for s_ in range(2):
                k.ld("S0_%d_%d" % (s_, hl), S0[:, s_, hl, :], k.sret[s_, h], [("S0", s_, hl)])
                S.op("act", _mk(lambda e, o, a: e.copy(o, a), S0b[:, s_, hl, :], S0[:, s_, hl, :]),
                     reads=[("S0", s_, hl)], writes=[("S0b", s_, hl)])
                S.op("act", _mk(lambda e, o, g: e.mul(o, o, g), S0[:, s_, hl, :], G64[h]),
                     reads=[("S0", s_, hl), ("S0b", s_, hl)], writes=[("S0", s_, hl)])
            S.op("pool", _mk(lambda e, o, a: e.tensor_copy(o, a), qA[hl][0][:, 0:64], qT[:, h, 1024:1088]),
                 reads=[("qT", h, 8)], writes=[("qA", hl)])
            S.op("pool", _mk(lambda e, o, a: e.tensor_copy(o, a), qB[hl][0][:, 64:128], qT[:, h, 1088:1152]),
                 reads=[("qT", h, 8)], writes=[("qB", hl)])
        ctx = {}

        def st_A(tt, hl):
            h = 2 * hp + hl
            hc = slice(h * DK, (h + 1) * DK)
            tsl = slice(tt * 128, (tt + 1) * 128)
            vsl = slice(tt * DV, (tt + 1) * DV)
            c = ctx[(tt, hl)] = {}
            if tt < 8:
                c["dkey"] = ("ps", 7, hl)
                c["pd"] = k.ps[7][:, hl * 256:(hl + 1) * 256]
                _mm(S, c["pd"], kt[:, tt, hc], vg[:, h, vsl], True, True, [("kt", tt, h // 4), ("vg", h, tt)], [c["dkey"]])
            else:
                c["dkeyA"] = ("ps", 7, hl)
                c["dkeyB"] = ("ps", 3)
                c["pdA"] = k.ps[7][:, hl * 256:(hl + 1) * 256]
                c["pdB"] = k.ps[3][:, hl * 256:(hl + 1) * 256]
                _mm(S, c["pdA"], kt[0:64, tt, hc], vg[0:64, h, vsl], True, True, [("kt", tt, h // 4), ("vg", h, tt)], [c["dkeyA"]])
                _mm(S, c["pdB"], kt[64:128, tt, hc], vg[64:128, h, vsl], True, True, [("kt", tt, h // 4), ("vg", h, tt)], [c["dkeyB"]])
            q4 = (2 * tt + hl) % 4
            skey = ("ps", 4, q4)
            pS = k.ps[4][:, q4 * 128:(q4 + 1) * 128]
            _mm(S, pS, ktT[hl][0][:, tsl], qT[:, h, tsl], True, True, [("ktT", hl), ("qT", h, tt)], [skey])
            si = (2 * tt + hl) % 4
            c["si"] = si
            mk = k.mask[:, 0:128] if tt < 8 else k.mask[:, 128:256]
            S.op("dve", _mk(lambda e, o, a, m: e.tensor_tensor(o, a, m, ALU.mult), sT[si][0][:, :], pS, mk),
                 reads=[skey, ("mask",)], writes=[("sT", si)])

        def st_B(tt, hl):
            nonlocal nSo
            h = 2 * hp + hl
            tsl = slice(tt * 128, (tt + 1) * 128)
            vsl = slice(tt * DV, (tt + 1) * DV)
            c = ctx[(tt, hl)]
            si = c["si"]
            ob = 5 if hl == 0 else 6
            oh = tt % 2
            okey = ("ps", ob, oh)
            pO = k.ps[ob][:, oh * 256:(oh + 1) * 256]
            c["okey"], c["pO"] = okey, pO
            _mm(S, pO, sT[si][0][:, :], vg[:, h, vsl], True, False, [("sT", si), ("vg", h, tt)], [okey], counted=False)
            if tt < 8:
                _mm(S, pO, qT[:, h, tsl], Sbf[:, hl, :], False, True, [("qT", h, tt), ("Sbf", hl)], [okey])
            else:
                _mm(S, pO, qA[hl][0][:, :], S0b[:, 0, hl, :], False, False, [("qA", hl), ("S0b", 0, hl)], [okey], counted=False)
                _mm(S, pO, qB[hl][0][:, :], S0b[:, 1, hl, :], False, True, [("qB", hl), ("S0b", 1, hl)], [okey])
            if tt < 8:
                S.op("dve", _mk(lambda e, o, a, g: e.scalar_tensor_tensor(o, o, g, a, ALU.mult, ALU.add), Rp[:, h, :], c["pd"], G128[h]),
                     reads=[c["dkey"], ("Rp", h)], writes=[("Rp", h)])
                if tt < 7:
                    S.op("act", _mk(lambda e, o, a, g: e.mul(o, a, g), Sbf[:, hl, :], Rp[:, h, :], G128[h]),
                         reads=[("Rp", h)], writes=[("Sbf", hl)])
                else:
                    so = So[nSo % 4]
                    sok = ("So", nSo % 4)
                    nSo += 1
                    S.op("act", _mk(lambda e, o, a, g: e.mul(o, a, g), so[0][:, :], Rp[:, h, :], G128[h]),
                         reads=[("Rp", h)], writes=[sok])
                    k.st("So%d" % (int(sok[1])), k.ret_p[h], so[0][:, :], [sok])
            else:
                for s_, (pdx, dk_) in enumerate(((c["pdA"], c["dkeyA"]), (c["pdB"], c["dkeyB"]))):
                    so = So[nSo % 4]
                    sok = ("So", nSo % 4)
                    nSo += 1
                    S.op("dve", _mk(lambda e, o, a, g, b_: e.scalar_tensor_tensor(o, a, g, b_, ALU.mult, ALU.add),
                                    so[0][:, :], pdx, G64[h], S0[:, s_, hl, :]),
                         reads=[dk_, ("S0", s_, hl)], writes=[sok])
                    k.st("So%d" % (int(sok[1])), k.ret_s[s_, h], so[0][:, :], [sok])

        def st_C(tt, hl):
            h = 2 * hp + hl
            c = ctx.pop((tt, hl))
            okey, pO = c["okey"], c["pO"]
            ssa = st_[:, h * NT + tt:h * NT + tt + 1]
            S.op("act", _mk(lambda e, o, a, s: e.activation(out=o, in_=a, func=AF.Square, accum_out=s), junk[:, :], pO, ssa),
                 reads=[okey], writes=[("p4junk",), ("p4ss", tt, h)])
            S.op("dve", _mk(lambda e, o, a, g: e.tensor_tensor(o, a, g, ALU.mult),
                            go[hl][0][:, tt, :], pO, sg[:, tt, hl * DV:(hl + 1) * DV]),
                 reads=[okey, ("p4ss", tt, h), ("sg", hp % 2, tt)], writes=[("go", hl, tt)])

        for tt in range(NT):
            for hl in range(2):
                st_A(tt, hl)
            for hl in range(2):
                st_B(tt, hl)
            for hl in range(2):
                st_C(tt, hl)
        if hp + 1 < 4:
            gr_proj(hp + 1)
        for hl in range(2):
            h = 2 * hp + hl
            ssv = st_[:, h * NT:(h + 1) * NT]
            rsv = st_[:, NH * NT + h * NT:NH * NT + (h + 1) * NT]
            S.op("act", _mk(lambda e, s, ep: e.activation(out=s, in_=s, func=AF.Sqrt, bias=ep, scale=1.0 / DV), ssv, k.eps_t[:, :]),
                 reads=[("p4ss", tt, h) for tt in range(NT)] + [("eps",)], writes=[("p4ssv", h)])
            S.op("dve", _mk(lambda e, o, a: e.reciprocal(o, a), rsv, ssv), reads=[("p4ssv", h)], writes=[("p4rs", h)])
            S.op("dve", _mk(lambda e, o, r: e.tensor_tensor(o, o, r, ALU.mult), go[hl][0][:, :, :],
                            rsv.unsqueeze(2).to_broadcast([128, NT, DV])),
                 reads=[("go", hl, tt) for tt in range(NT)] + [("p4rs", h)], writes=[("go", hl, tt) for tt in range(NT)])
        for hl in range(2):
            h = 2 * hp + hl
            for c2 in range(2):
                for (t0, n) in ((0, 4), (4, 4), (8, 1)):
                    bank = 2 + nps % 2
                    nps += 1
                    bkey = ("ps", bank)
                    pb = k.ps[bank][:, :].bitcast(BF16)
                    for j in range(n):
                        _tr(S, pb[:, j * 128:(j + 1) * 128], go[hl][0][:, t0 + j, c2 * 128:(c2 + 1) * 128], k.ident_b[:, :],
                            [("go", hl, t0 + j), ("ident_b",)], [bkey], counted=(j == n - 1))
                    S.op("act", _mk(lambda e, o, a: e.copy(o, a),
                                    vg[:, h, c2 * T + t0 * 128:c2 * T + (t0 + n) * 128], pb[:, 0:n * 128]),
                         reads=[bkey], writes=[("vg", h, t) for t in range(NT)] + [("goT", h)])
    k.dump("goT", vg[:], [128, NH, NT * DV], [("goT", h) for h in range(NH)], BF16)
    for lst in (ktT, go, sT, qA, qB, So):
        for t_, u in lst:
            M.free(u)
    for u in (sg_u, sgb[1][1], Sbf_u, S0_u, S0b_u, junk_u, st_u, k.qT_u, k.kt_u, k.Rp_u):
        M.free(u)
    k.end_phase()


UL = 1200


def phase2(k):
    S, M = k.S, k.M
    hT, hTh = k.hT, k.hTh
    pyT, k.pyT_u = M.alloc("pyT", [128, 8, T], BF16)
    k.pyT = pyT
    Uc = [M.alloc("Uc%d" % i, [128, UL], F32) for i in range(2)]
    W1, W1_u = M.alloc("pW1", [128, UL], F32)
    W2, W2_u = M.alloc("pW2", [128, UL], F32)
    dT = [M.alloc("dT%d" % i, [128, 2, T], BF16) for i in range(2)]
    cl, cl_u = M.alloc("cl", [16, 2, PW], F32)
    wp, wp_u = M.alloc("wp", [128, 4, 2, 256], BF16)
    save, save_u = M.alloc("psave", [128, 8, 3, 16], F32)
    stage, stage_u = M.alloc("pstage", [16, PW], F32)
    t16, t16_u = M.alloc("pt16", [128, 16], F32)
    for s_ in range(2):
        k.ld("cl", cl[0:LAG, s_, :], k.cpool[s_], [("cl",)])
    tok = S.dma("pool", "wp", _mk(lambda e, o, s: e.dma_start(out=o, in_=s), wp[:],
                                  k.w_pool.rearrange("g (cc p) d -> p g cc d", p=128)), writes=[("wp",)])
    k.phase_toks.append(tok)
    for i in range(2):
        S.op("pool", _mk(lambda e, o: e.memset(o, 0.0), Uc[i][0][:, :]), writes=[("Uc", i)])
    groups = [(None, 0, 16, 0), (hT, 0, 512, 16), (hT, 512, 512, 528), (hT, 1024, 64, 1056), (hT, 1088, 64, 1136)]
    mains = [(16, 1024, 0), (1056, 64, 1024), (1136, 64, 1088)]
    nb = 0
    wt = wkey = None
    for uc in range(8):
        if uc % 4 == 0:
            wt, wkey, _ = k.wnext()
        g = uc // 2
        cc = uc % 2
        ub = uc % 2
        U = Uc[ub][0]
        ukey = ("Uc", ub)
        msl = slice((uc % 4) * 128, (uc % 4 + 1) * 128)
        for (src, t0, n, c0) in groups:
            bank = nb % 4
            nb += 1
            pkey = ("ps", bank)
            for kc in range(KC):
                if src is None:
                    rhs = hTh[:, kc, 0:16]
                    rk = [("hTh", kc // 4)]
                else:
                    rhs = hT[:, kc, t0:t0 + n]
                    rk = hT_keys(t0, n, kc)
                _mm(S, k.ps[bank][:, 0:n], wt[:, kc, msl], rhs, kc == 0, kc == KC - 1, rk + [wkey], [pkey], counted=(kc == KC - 1))
            S.op("act", _mk(lambda e, o, a: e.copy(o, a), U[:, c0:c0 + n], k.ps[bank][:, 0:n]), reads=[pkey], writes=[ukey])
        for s_ in range(2):
            tkey = ("ps", 4)
            _tr(S, k.ps[4][:, s_ * 16:s_ * 16 + LAG], cl[0:LAG, s_, uc * 128:(uc + 1) * 128], k.ident_f[0:LAG, 0:LAG],
                [("cl",), ("ident_f",)], [tkey])
            c1 = 1041 + 80 * s_
            S.op("act", _mk(lambda e, o, a: e.copy(o, a), U[:, c1:c1 + LAG], k.ps[4][:, s_ * 16:s_ * 16 + LAG]),
                 reads=[tkey], writes=[ukey])
        for s3, c1 in enumerate((1025, 1105, 1185)):
            S.op("pool", _mk(lambda e, o, a: e.tensor_copy(o, a), save[:, uc, s3, 0:LAG], U[:, c1:c1 + LAG]),
                 reads=[ukey], writes=[("psave", uc)])
        cur, ckey = U, ukey
        sh = 1
        bufs = [(W1, ("pW", 1)), (W2, ("pW", 2))]
        for lvl in range(g + 1):
            dst, dkey = bufs[lvl % 2]
            lo = 2 * sh - 1
            S.op("dve", _mk(lambda e, o, a, b_: e.tensor_tensor(o, a, b_, ALU.add), dst[:, lo:UL], cur[:, lo:UL], cur[:, lo - sh:UL - sh]),
                 reads=[ckey], writes=[dkey])
            cur, ckey = dst, dkey
            sh *= 2
        w = 2 ** (g + 1)
        dt_ = dT[g % 2][0]
        dkey2 = ("dT", g % 2, cc)
        for (c0, n, t0) in mains:
            S.op("dve", _mk(lambda e, o, a, sc, b_: e.scalar_tensor_tensor(o, a, sc, b_, ALU.mult, ALU.subtract),
                            dt_[:, cc, t0:t0 + n], cur[:, c0:c0 + n], 1.0 / w, U[:, c0:c0 + n]),
                 reads=[ckey, ukey], writes=[dkey2])
        S.op("dve", _mk(lambda e, o, a, b_: e.tensor_tensor(o, a, b_, ALU.mult), t16[:, :], cur[:, 16:32], k.invc[:, g * 16:(g + 1) * 16]),
             reads=[ckey, ("invc",)], writes=[("pt16",)])
        S.op("dve", _mk(lambda e, o, a, b_: e.tensor_tensor(o, a, b_, ALU.subtract), dt_[:, cc, 0:16], t16[:, :], U[:, 16:32]),
             reads=[("pt16",), ukey], writes=[dkey2])
        if cc == 1:
            for dc in range(2):
                for (t0, n) in TBLK:
                    bank = 5 + nb % 2
                    nb += 1
                    pkey = ("ps", bank)
                    for c_ in range(2):
                        _mm(S, k.ps[bank][:, 0:n], wp[:, g, c_, dc * 128:(dc + 1) * 128], dt_[:, c_, t0:t0 + n], c_ == 0, c_ == 1,
                            [("wp",), ("dT", g % 2, c_)], [pkey], counted=(c_ == 1))
                    oc = 2 * g + dc
                    S.op("act", _mk(lambda e, o, a, s: e.activation(out=o, in_=a, func=AF.Copy, scale=s),
                                    pyT[:, oc, t0:t0 + n], k.ps[bank][:, 0:n], k.psc[:, oc:oc + 1]),
                         reads=[pkey, ("psc",)], writes=[("pyT", oc)])
    for s3 in range(3):
        for half in range(2):
            bank = 7 if half == 0 else 4
            bkey = ("ps", bank)
            for j in range(4):
                uc = half * 4 + j
                _tr(S, k.ps[bank][0:LAG, j * 128:(j + 1) * 128], save[:, uc, s3, 0:LAG], k.ident_f[:, :],
                    [("psave", uc), ("ident_f",)], [bkey], counted=(j == 3))
            S.op("act", _mk(lambda e, o, a: e.copy(o, a), stage[0:LAG, half * 512:(half + 1) * 512], k.ps[bank][0:LAG, :]),
                 reads=[bkey], writes=[("pstage",)])
        dst = k.pool_p if s3 == 0 else k.pool_s[s3 - 1]
        k.st("pstage", dst, stage[0:LAG, :], [("pstage",)])
    k.dump("pyT", pyT[:], [128, 8, T], [("pyT", oc) for oc in range(8)], BF16)
    for lst in (Uc, dT):
        for t_, u in lst:
            M.free(u)
    for u in (W1_u, W2_u, cl_u, wp_u, save_u, stage_u, t16_u, k.hTh_u):
        M.free(u)
    k.end_phase()


def goT_rhs(k, kc, t0, n):
    h, c2 = kc // 2, kc % 2
    return k.vg[:, h, c2 * T + t0:c2 * T + t0 + n]


def phase5(k):
    S, M = k.S, k.M
    hT, pyT = k.hT, k.pyT
    mT, k.mT_u = M.alloc("mT", [128, KC, T], BF16)
    k.mT = mT
    sigA, sigA_u = M.alloc("sigA", [128, 4, T], BF16)
    part1, part1_u = M.alloc("part1", [128, 4, T], F32)
    tm = [M.alloc("p5t%d" % i, [128, 512], F32) for i in range(2)]
    nb = 0
    nt = 0
    for j in range(4):
        for stage in range(4):
            wt, wkey, blk = k.wnext()
            nk = blk[3]
            for oc4 in range(4):
                msl = slice(oc4 * 128, (oc4 + 1) * 128)
                for (t0, n) in TBLK:
                    bank = nb % 6
                    nb += 1
                    pkey = ("ps", bank)
                    ps = k.ps[bank][:, 0:n]
                    for kc in range(nk):
                        if stage in (0, 2):
                            rhs, rk = hT[:, kc, t0:t0 + n], hT_keys(t0, n, kc)
                        elif stage == 1:
                            rhs, rk = pyT[:, kc, t0:t0 + n], [("pyT", kc)]
                        else:
                            rhs, rk = goT_rhs(k, kc, t0, n), [("goT", kc // 2)]
                        _mm(S, ps, wt[:, kc, msl], rhs, kc == 0, kc == nk - 1, rk + [wkey], [pkey], counted=(kc == nk - 1))
                    skey = ("sigA", oc4, t0)
                    if stage in (0, 2):
                        S.op("act", _mk(lambda e, o, a: e.activation(out=o, in_=a, func=AF.Sigmoid), sigA[:, oc4, t0:t0 + n], ps),
                             reads=[pkey], writes=[skey])
                    elif stage == 1:
                        S.op("dve", _mk(lambda e, o, a, b_: e.tensor_tensor(o, a, b_, ALU.mult), part1[:, oc4, t0:t0 + n], ps, sigA[:, oc4, t0:t0 + n]),
                             reads=[pkey, skey], writes=[("part1", oc4, t0)])
                    else:
                        tb_ = tm[nt % 2]
                        tkey = ("p5t", nt % 2)
                        nt += 1
                        S.op("dve", _mk(lambda e, o, a, b_: e.tensor_tensor(o, a, b_, ALU.mult), tb_[0][:, 0:n], ps, sigA[:, oc4, t0:t0 + n]),
                             reads=[pkey, skey], writes=[tkey])
                        S.op("pool", _mk(lambda e, o, a, b_: e.tensor_tensor(o, a, b_, ALU.add), mT[:, 4 * j + oc4, t0:t0 + n], tb_[0][:, 0:n], part1[:, oc4, t0:t0 + n]),
                             reads=[tkey, ("part1", oc4, t0)], writes=[("mT", 4 * j + oc4, t0)])
    k.dump("mT", mT[:], [128, KC, T], [("mT", oc, t0) for oc in range(KC) for (t0, n) in TBLK], BF16)
    for t_, u in tm:
        M.free(u)
    for u in (sigA_u, part1_u, k.hT_u, k.pyT_u, k.vg_u):
        M.free(u)
    k.end_phase()


def phase6(k):
    S, M = k.S, k.M
    mT = k.mT
    h2T, k.h2T_u = M.alloc("h2T", [128, KC, T], BF16)
    k.h2T = h2T
    gpo, gpo_u = M.alloc("gpo", [128, D], F32)
    k.ld("gpo", gpo[:], k.g_post_mix, [("gpo",)])
    wex = [M.alloc("woex%d" % i, [128, KC, 512], BF16) for i in range(2)]
    xt = [M.alloc("p6x%d" % i, [128, D], F32) for i in range(2)]
    x1 = [M.alloc("p6x1%d" % i, [128, D], F32) for i in range(2)]
    xn = [M.alloc("p6xn%d" % i, [128, D], BF16) for i in range(2)]
    st_, st_u = M.alloc("p6stat", [128, 8 * NT], F32)
    j0 = k.wcur
    assert j0 == k.n_pre_wo and j0 <= k.wloaded <= j0 + 2, (j0, k.wloaded, k.n_pre_wo)
    already = k.wloaded - j0
    wo = []
    for i in range(4):
        if i < 2:
            sl = (j0 + i) % NSLOT
            tile_, key, chan = k.wslot[sl], ("w", sl), "w%d" % sl
        else:
            tile_, key, chan = wex[i - 2][0], ("woex", i - 2), "woex%d" % (i - 2)
        ap2d, r0, c0, nk = k.blocks[j0 + i]
        src = ap2d[r0:r0 + nk * 128, c0:c0 + 512].rearrange("(kc p) n -> p kc n", p=128)
        if i >= already:
            tok = S.dma("pool", chan, _mk(lambda e, o, s: e.dma_start(out=o, in_=s), tile_[:, :, :], src), writes=[key])
            if i >= 2:
                k.phase_toks.append(tok)
        wo.append((tile_, key))
    k.wcur = j0 + 4
    k.wloaded = j0 + 4
    junk6, junk6_u = M.alloc("p6junk", [128, 512], BF16)

    def stage_mm(tt):
        b = tt % 2
        tsl = slice(tt * 128, (tt + 1) * 128)
        xa, xkey = xt[b][0], ("p6x", b)
        k.ld("p6x%d" % b, xa[:], k.x[tsl, :], [xkey])
        x1a, x1key = x1[b][0], ("p6x1", b)
        base = 8 * tt
        for cb in range(4):
            bank = cb
            pkey = ("ps", bank)
            csl = slice(cb * 512, (cb + 1) * 512)
            for kc in range(KC):
                _mm(S, k.ps[bank][:, :], mT[:, kc, tsl], wo[cb][0][:, kc, :], kc == 0, kc == KC - 1,
                    [("mT", kc, (tt * 128) // 512 * 512), wo[cb][1]], [pkey], counted=(kc == KC - 1))
            S.op("act", _mk(lambda e, o, a, s: e.activation(out=o, in_=a, func=AF.Square, accum_out=s),
                            junk6[:, :], k.ps[bank][:, :], st_[:, base + cb:base + cb + 1]),
                 reads=[pkey], writes=[("p6junk",), ("p6ss", tt, cb)])
            S.op("dve", _mk(lambda e, o, a, g: e.tensor_tensor(o, a, g, ALU.mult), x1a[:, csl], k.ps[bank][:, :], gpo[:, csl]),
                 reads=[pkey, ("p6ss", tt, cb), ("gpo",)], writes=[(x1key, cb)])

    def stage_post(tt):
        b = tt % 2
        tsl = slice(tt * 128, (tt + 1) * 128)
        xa, xkey = xt[b][0], ("p6x", b)
        x1a, x1key = x1[b][0], ("p6x1", b)
        xna, xnkey = xn[b][0], ("p6xn", b)
        base = 8 * tt
        x1all = [(x1key, cb) for cb in range(4)]
        ss = st_[:, base + 4:base + 5]
        rs = st_[:, base + 5:base + 6]
        S.op("dve", _mk(lambda e, o, a: e.reduce_sum(o, a, AX.X), ss, st_[:, base:base + 4]),
             reads=[("p6ss", tt, cb) for cb in range(4)], writes=[("p6s", tt)])
        S.op("act", _mk(lambda e, s, ep: e.activation(out=s, in_=s, func=AF.Sqrt, bias=ep, scale=1.0 / D), ss, k.eps_t[:, :]),
             reads=[("p6s", tt), ("eps",)], writes=[("p6s", tt)])
        S.op("dve", _mk(lambda e, o, a: e.reciprocal(o, a), rs, ss), reads=[("p6s", tt)], writes=[("p6r", tt)])
        S.op("dve", _mk(lambda e, o, s, b_: e.scalar_tensor_tensor(o, o, s, b_, ALU.mult, ALU.add), x1a[:, :], rs, xa[:, :]),
             reads=x1all + [("p6r", tt), xkey], writes=x1all + [(x1key, "f")])
        k.st("p6x1_%d" % b, k.x1_d[tsl, :], x1a[:, :], x1all + [(x1key, "f")], writes=[("x1d", tt)])
        ss2 = st_[:, base + 6:base + 7]
        rs2 = st_[:, base + 7:base + 8]
        S.op("act", _mk(lambda e, o, a, s: e.activation(out=o, in_=a, func=AF.Square, accum_out=s), xna[:, :], x1a[:, :], ss2),
             reads=x1all + [(x1key, "f")], writes=[xnkey, ("p6s2", tt)])
        S.op("act", _mk(lambda e, s, ep: e.activation(out=s, in_=s, func=AF.Sqrt, bias=ep, scale=1.0 / D), ss2, k.eps_t[:, :]),
             reads=[("p6s2", tt), ("eps",)], writes=[("p6s2", tt)])
        S.op("dve", _mk(lambda e, o, a: e.reciprocal(o, a), rs2, ss2), reads=[("p6s2", tt)], writes=[("p6r2", tt)])
        S.op("dve", _mk(lambda e, o, a, s: e.tensor_scalar(o, a, s, None, ALU.mult), xna[:, :], x1a[:, :], rs2),
             reads=x1all + [(x1key, "f"), ("p6r2", tt)], writes=[xnkey])

    def stage_post_b(tt):
        b = tt % 2
        tsl = slice(tt * 128, (tt + 1) * 128)
        xna, xnkey = xn[b][0], ("p6xn", b)
        for g4 in range(4):
            bank = 4 + (g4 % 2) + 2 * b
            bkey = ("ps", bank)
            pb = k.ps[bank][:, :].bitcast(BF16)
            for j in range(4):
                kc = g4 * 4 + j
                _tr(S, pb[:, j * 128:(j + 1) * 128], xna[:, kc * 128:(kc + 1) * 128], k.ident_b[:, :],
                    [xnkey, ("ident_b",)], [bkey], counted=(j == 3))
            gb = k.gpf[:, g4 * 4:g4 * 4 + 4].unsqueeze(2).to_broadcast([128, 4, 128])
            S.op("dve", _mk(lambda e, o, a, g: e.tensor_tensor(o, a, g, ALU.mult), h2T[:, g4 * 4:g4 * 4 + 4, tsl],
                            pb[:, 0:512].rearrange("p (j t) -> p j t", j=4), gb),
                 reads=[bkey, ("gpf",)], writes=[("h2T", tt, g4)])

    stage_mm(0)
    for tt in range(1, NT):
        stage_post(tt - 1)
        stage_mm(tt)
        stage_post_b(tt - 1)
    stage_post(NT - 1)
    stage_post_b(NT - 1)
    k.wprefetch(2)
    M.free(junk6_u)
    k.dump("h2T", h2T[:], [128, KC, T], [("h2T", tt, g4) for tt in range(NT) for g4 in range(4)], BF16)
    for lst in (wex, xt, x1, xn):
        for t_, u in lst:
            M.free(u)
    for u in (gpo_u, st_u, k.mT_u):
        M.free(u)
    k.end_phase()


def h2T_keys(t0, n, kc):
    return [("h2T", t, kc // 4) for t in range(t0 // 128, (t0 + n + 127) // 128)]


def phase7(k):
    S, M = k.S, k.M
    h2T = k.h2T
    NFC = FG // 128
    f1T, f1T_u = M.alloc("f1T", [128, NFC, T], BF16)
    facc, facc_u = M.alloc("facc", [128, NT, D], F32)
    gpo, gpo_u = M.alloc("gpo2", [128, D], F32)
    k.ld("gpo2", gpo[:], k.g_post_ffn, [("gpo2",)])
    rt = [M.alloc("p7r%d" % i, [128, 512], F32) for i in range(2)]
    x1r = [M.alloc("p7x1%d" % i, [128, D], F32) for i in range(2)]
    junk, junk_u = M.alloc("p7junk", [128, D], BF16)
    st_, st_u = M.alloc("p7stat", [128, 2 * NT], F32)
    nb = 0
    nr = 0
    ngrp = DFF // FG
    for g in range(ngrp):
        for b2 in range(FG // 512):
            wt, wkey, _ = k.wnext()
            for oc4 in range(4):
                fc = b2 * 4 + oc4
                msl = slice(oc4 * 128, (oc4 + 1) * 128)
                for (t0, n) in TBLK:
                    bank = nb % 4
                    nb += 1
                    pkey = ("ps", bank)
                    ps = k.ps[bank][:, 0:n]
                    for kc in range(KC):
                        _mm(S, ps, wt[:, kc, msl], h2T[:, kc, t0:t0 + n], kc == 0, kc == KC - 1,
                            h2T_keys(t0, n, kc) + [wkey], [pkey], counted=(kc == KC - 1))
                    r_ = rt[nr % 2]
                    rkey = ("p7r", nr % 2)
                    nr += 1
                    S.op("act", _mk(lambda e, o, a: e.activation(out=o, in_=a, func=AF.Relu), r_[0][:, 0:n], ps),
                         reads=[pkey], writes=[rkey])
                    S.op("dve", _mk(lambda e, o, a, b_: e.scalar_tensor_tensor(o, a, 0.0, b_, ALU.max, ALU.mult), f1T[:, fc, t0:t0 + n], ps, r_[0][:, 0:n]),
                         reads=[pkey, rkey], writes=[("f1T", fc, t0)])
        for cb in range(4):
            wt, wkey, blk = k.wnext()
            nk = blk[3]
            csl = slice(cb * 512, (cb + 1) * 512)
            for tt in range(NT):
                tsl = slice(tt * 128, (tt + 1) * 128)
                bank = 4 + nb % 4
                nb += 1
                pkey = ("ps", bank)
                for kc in range(nk):
                    _mm(S, k.ps[bank][:, :], f1T[:, kc, tsl], wt[:, kc, :], kc == 0, kc == nk - 1,
                        [("f1T", kc, (tt * 128) // 512 * 512), wkey], [pkey], counted=(kc == nk - 1))
                fkey = ("facc", tt, cb)
                if g == 0:
                    S.op("act", _mk(lambda e, o, a: e.copy(o, a), facc[:, tt, csl], k.ps[bank][:, :]), reads=[pkey], writes=[fkey])
                else:
                    S.op("dve", _mk(lambda e, o, a: e.tensor_tensor(o, o, a, ALU.add), facc[:, tt, csl], k.ps[bank][:, :]),
                         reads=[pkey, fkey], writes=[fkey])
    def ld_x1(t):
        bb = t % 2
        k.ld("p7x1_%d" % bb, x1r[bb][0][:], k.x1_d[t * 128:(t + 1) * 128, :], [("p7x1", bb)])

    ld_x1(0)
    ld_x1(1)
    for tt in range(NT):
        b = tt % 2
        tsl = slice(tt * 128, (tt + 1) * 128)
        fk = [("facc", tt, cb) for cb in range(4)]
        xr, xrkey = x1r[b][0], ("p7x1", b)
        ss = st_[:, 2 * tt:2 * tt + 1]
        rs = st_[:, 2 * tt + 1:2 * tt + 2]
        S.op("act", _mk(lambda e, o, a, s: e.activation(out=o, in_=a, func=AF.Square, accum_out=s), junk[:, :], facc[:, tt, :], ss),
             reads=fk, writes=[("p7junk",), ("p7s", tt)])
        S.op("act", _mk(lambda e, s, ep: e.activation(out=s, in_=s, func=AF.Sqrt, bias=ep, scale=1.0 / D), ss, k.eps_t[:, :]),
             reads=[("p7s", tt), ("eps",)], writes=[("p7s", tt)])
        S.op("dve", _mk(lambda e, o, a: e.reciprocal(o, a), rs, ss), reads=[("p7s", tt)], writes=[("p7rs", tt)])
        S.op("dve", _mk(lambda e, o, s, g_: e.scalar_tensor_tensor(o, o, s, g_, ALU.mult, ALU.mult), facc[:, tt, :], rs, gpo[:, :]),
             reads=fk + [("p7rs", tt), ("gpo2",)], writes=fk)
        S.op("pool", _mk(lambda e, o, b_: e.tensor_tensor(o, o, b_, ALU.add), facc[:, tt, :], xr[:, :]),
             reads=fk + [xrkey], writes=fk + [("yt", tt)])
        if tt + 2 < NT:
            ld_x1(tt + 2)
        k.st("y%d" % b, k.y[tsl, :], facc[:, tt, :], [("yt", tt)])
    for lst in (rt, x1r):
        for t_, u in lst:
            M.free(u)
    for u in (f1T_u, facc_u, gpo_u, junk_u, st_u, k.h2T_u):
        M.free(u)
    k.end_phase()
```

```python
import bisect
from contextlib import ExitStack

import numpy as np
import concourse.bass as bass
import concourse.mybir as mybir
from concourse.bass_utils import run_bass_kernel_spmd

F32 = mybir.dt.float32
BF16 = mybir.dt.bfloat16
ALU = mybir.AluOpType
AF = mybir.ActivationFunctionType
AX = mybir.AxisListType

NCORES = 8
D = 2048
KC = D // 128
TP = 1024
NT = 9
T = NT * 128
SEQ = 8192
DEC_B, DEC_T = 16, 64
PAST = 2048
PW = 1024
LAG = 15
NH, DK, DV = 8, 128, 256
IN_W = 11264
DFF = 8192
EPS = 1e-6
OFF_U, OFF_Q, OFF_K, OFF_V, OFF_G, OFF_AP, OFF_AR = 0, 1024, 2048, 3072, 5120, 7168, 9216
TBLK = [(0, 512), (512, 512), (1024, 128)]
FG = 1024
NPRE = 56
NSLOT = 2
LOG_GAMMA = [float(np.log1p(-np.exp2(np.float32(-5.0 - h)))) for h in range(NH)]
G128 = [float(np.exp(128.0 * lg)) for lg in LOG_GAMMA]
G64 = [float(np.exp(64.0 * lg)) for lg in LOG_GAMMA]

ENGS = ("pe", "act", "dve", "pool", "sp")


class Sched:
    def __init__(self, nc, es):
        self.nc = nc
        self.es = es
        self.items = {e: [] for e in ENGS}
        self.npos = {e: 0 for e in ENGS}
        self.counted = {e: [] for e in ENGS}
        self.known = {e: {} for e in ENGS}
        self.lw = {}
        self.rd = {}
        self.dma_val = {}
        self.dma_sem = {}
        self.esem = {e: es.enter_context(nc.semaphore("c_" + e)) for e in ENGS}
        self.nwaits = 0
        self.nops = 0
        self.simsem = {}

    def _deps(self, reads, writes):
        deps = []
        for r in reads:
            t = self.lw.get(r)
            if t is not None:
                deps.append(t)
        for w in writes:
            t = self.lw.get(w)
            if t is not None:
                deps.append(t)
            deps.extend(self.rd.get(w, ()))
        return deps

    def _commit(self, tok, reads, writes):
        for r in reads:
            self.rd.setdefault(r, []).append(tok)
        for w in writes:
            self.lw[w] = tok
            self.rd[w] = []

    def op(self, eng, fn, reads=(), writes=(), counted=True):
        reads, writes = list(reads), list(writes)
        deps = self._deps(reads, writes)
        pos = self.npos[eng]
        self.npos[eng] += 1
        if counted:
            self.counted[eng].append(pos)
        it = dict(kind="op", fn=fn, deps=deps, counted=counted, pos=pos)
        self.items[eng].append(it)
        self._commit(("e", eng, pos), reads, writes)
        self.nops += 1
        return it

    def dma(self, queue, chan, fn, reads=(), writes=(), inc=16):
        reads, writes = list(reads), list(writes)
        deps = self._deps(reads, writes)
        if chan not in self.dma_sem:
            self.dma_sem[chan] = self.es.enter_context(self.nc.semaphore("d_" + chan))
            self.dma_val[chan] = 0
        self.dma_val[chan] += inc
        tok = ("d", chan, self.dma_val[chan])
        it = dict(kind="dma", fn=fn, deps=deps, chan=chan, inc=inc)
        self.items[queue].append(it)
        self._commit(tok, reads, writes)
        return tok

    def wait_tok(self, eng, toks):
        self.items[eng].append(dict(kind="wait", deps=list(toks)))

    def _resolve(self, tok):
        if tok[0] == "d":
            return ("d", tok[1]), self.dma_sem[tok[1]], tok[2]
        _, eng, pos = tok
        lst = self.counted[eng]
        i = bisect.bisect_left(lst, pos)
        assert i < len(lst), ("dependency on trailing uncounted op", eng, pos)
        return ("e", eng), self.esem[eng], i + 1

    def flush(self):
        for e in ENGS:
            ops = [it for it in self.items[e] if it["kind"] == "op"]
            if ops and not ops[-1]["counted"]:
                ops[-1]["counted"] = True
                bisect.insort(self.counted[e], ops[-1]["pos"])
        nc = self.nc
        simlog = {e: [] for e in ENGS}
        with nc.Block() as block:
            def emit(ename, eng):
                known = self.known[ename]
                for it in self.items[ename]:
                    for tok in it["deps"]:
                        if tok[0] == "e" and tok[1] == "pe" and ename == "pe":
                            continue
                        key, sem, val = self._resolve(tok)
                        if known.get(key, 0) >= val:
                            continue
                        eng.wait_ge(sem, val)
                        known[key] = val
                        self.nwaits += 1
                        simlog[ename].append(("wait", key, val))
                    if it["kind"] == "wait":
                        continue
                    ins = it["fn"](eng)
                    if it["kind"] == "dma":
                        ins.then_inc(self.dma_sem[it["chan"]], it["inc"])
                        simlog[ename].append(("inc", ("d", it["chan"]), it["inc"]))
                    elif it["counted"]:
                        ins.then_inc(self.esem[ename], 1)
                        simlog[ename].append(("inc", ("e", ename), 1))

            @block.tensor
            def _(eng):
                emit("pe", eng)

            @block.scalar
            def _(eng):
                emit("act", eng)

            @block.vector
            def _(eng):
                emit("dve", eng)

            @block.gpsimd
            def _(eng):
                emit("pool", eng)

            @block.sync
            def _(eng):
                emit("sp", eng)
        self._simulate(simlog)
        for e in ENGS:
            for e2 in ENGS:
                self.known[e][("e", e2)] = len(self.counted[e2])
        self.items = {e: [] for e in ENGS}
        self.lw = {k: v for k, v in self.lw.items() if v[0] == "d"}
        self.rd = {k: [t for t in v if t[0] == "d"] for k, v in self.rd.items()}
        self.rd = {k: v for k, v in self.rd.items() if v}


def _sched_simulate(self, simlog):
    sem = dict(self.simsem)
    pc = {e: 0 for e in ENGS}
    progress = True
    while progress:
        progress = False
        for e in ENGS:
            lst = simlog[e]
            while pc[e] < len(lst):
                kind, key, val = lst[pc[e]]
                if kind == "wait":
                    if sem.get(key, 0) < val:
                        break
                else:
                    sem[key] = sem.get(key, 0) + val
                pc[e] += 1
                progress = True
    stuck = {e: (pc[e], len(simlog[e]), simlog[e][pc[e]]) for e in ENGS if pc[e] < len(simlog[e])}
    assert not stuck, ("DEADLOCK in phase", stuck, {k_: sem.get(k_) for k_ in [v[2][1] for v in stuck.values()]})
    self.simsem = sem


Sched._simulate = _sched_simulate


class Mem:
    BASE = 16512
    LIMIT = 229376

    def __init__(self, nc):
        self.nc = nc
        self.live = {}
        self.pending = []
        self.n = 0
        self.peak = 0

    def alloc(self, name, shape, dtype):
        esz = 2 if dtype == BF16 else 4
        size = int(np.prod(shape[1:])) * esz
        size = (size + 31) // 32 * 32
        segs = sorted(self.live.values())
        off = self.BASE
        for (o, s) in segs:
            if off + size <= o:
                break
            off = max(off, o + s)
        assert off + size <= self.LIMIT, ("SBUF OOM", name, size, off, sorted(self.live.items(), key=lambda kv: kv[1]))
        self.n += 1
        uname = "%s_%d" % (name, self.n)
        self.live[uname] = (off, size)
        self.peak = max(self.peak, off + size)
        t = self.nc.alloc_sbuf_tensor_at(uname, list(shape), dtype, offset=off)
        return t, uname

    def free(self, uname):
        self.pending.append(uname)

    def commit(self):
        for u in self.pending:
            self.live.pop(u, None)
        self.pending = []


class K:
    pass


def _mk(fn, *a, **kw):
    return lambda eng: fn(eng, *a, **kw)


def build_program(stop_after=99, dbg=()):
    nc = bass.Bass("TRN2", target_bir_lowering=False)
    es = ExitStack()
    S = Sched(nc, es)
    M = Mem(nc)
    k = K()
    k.nc, k.S, k.M, k.es = nc, S, M, es
    k.dbg = set(dbg)
    k.dbg_outs = []

    def din(name, shape):
        return nc.dram_tensor(name, list(shape), F32, kind="ExternalInput").ap()

    def dout(name, shape):
        return nc.dram_tensor(name, list(shape), F32, kind="ExternalOutput").ap()

    k.x = din("x", [T, D])
    k.xh = din("xh", [16, D])
    k.xprev = din("xprev", [NPRE * 128, D])
    k.t_cosp = din("t_cosp", [NPRE * 128, 64])
    k.t_sinp = din("t_sinp", [NPRE * 128, 64])
    k.cpool = din("cpool", [2, LAG, PW])
    k.sret = din("sret", [2, NH, DK, DV])
    k.w_in = din("w_in", [D, IN_W])
    k.w_pool = din("w_pool", [4, 256, 256])
    k.w_pool_out = din("w_pool_out", [PW, D])
    k.w_ret_out = din("w_ret_out", [D, D])
    k.w_o = din("w_o", [D, D])
    k.w_up = din("w_up", [D, DFF])
    k.w_down = din("w_down", [DFF, D])
    k.g_pre_mix = din("g_pre_mix", [128, KC])
    k.g_pre_ffn = din("g_pre_ffn", [128, KC])
    k.pscale = din("pscale", [128, 8])
    k.g_post_mix = din("g_post_mix", [128, D])
    k.g_post_ffn = din("g_post_ffn", [128, D])
    k.t_cos = din("t_cos", [T, 64])
    k.t_sin = din("t_sin", [T, 64])
    k.t_dec = din("t_dec", [128, 32])
    k.t_mask = din("t_mask", [128, 256])
    k.t_coef = din("t_coef", [128, 64])
    k.t_invc = din("t_invc", [128, 64])
    k.t_ident = din("t_ident", [128, 128])

    k.y = dout("y", [T, D])
    k.pool_p = dout("pool_p", [LAG, PW])
    k.ret_p = dout("ret_p", [NH, DK, DV])
    k.pool_s = dout("pool_s", [2, LAG, PW])
    k.ret_s = dout("ret_s", [2, NH, DK, DV])

    k.x1_d = nc.dram_tensor("x1_scratch", [T, D], F32).ap()

    k.ps = [nc.alloc_psum_tensor("ps%d" % i, [128, 512], F32) for i in range(8)]
    k.wslot = [M.alloc("wslot%d" % i, [128, KC, 512], BF16)[0] for i in range(NSLOT)]
    k.ident_f, _ = M.alloc("ident_f", [128, 128], F32)
    k.ident_b, _ = M.alloc("ident_b", [128, 128], BF16)
    k.eps_t, _ = M.alloc("eps_t", [128, 1], F32)
    k.gpm, _ = M.alloc("gpm", [128, KC], F32)
    k.gpf, _ = M.alloc("gpf", [128, KC], F32)
    k.psc, _ = M.alloc("psc", [128, 8], F32)
    k.dec, _ = M.alloc("dec", [128, 32], F32)
    k.mask, _ = M.alloc("mask", [128, 256], F32)
    k.coef, _ = M.alloc("coef", [128, 64], F32)
    k.invc, _ = M.alloc("invc", [128, 64], F32)

    k.phase_toks = []

    def ld(chan, dst, src, writes, queue="sp"):
        tok = S.dma(queue, chan, lambda e: e.dma_start(out=dst, in_=src), writes=writes)
        k.phase_toks.append(tok)
        return tok
    k.ld = ld

    def st(chan, dst, src, reads, queue="sp", writes=()):
        tok = S.dma(queue, chan, lambda e: e.dma_start(out=dst, in_=src), reads=reads, writes=writes)
        k.phase_toks.append(tok)
        return tok
    k.st = st

    def end_phase():
        S.wait_tok("sp", k.phase_toks)
        k.phase_toks = []
        S.flush()
        M.commit()
    k.end_phase = end_phase

    def dump(name, sb_ap, shape, reads, dtype=F32):
        if name not in k.dbg:
            return
        d = nc.dram_tensor("dbg_" + name, list(shape), dtype, kind="ExternalOutput").ap()
        k.dbg_outs.append("dbg_" + name)
        st("dbg_" + name, d, sb_ap, reads)
    k.dump = dump

    ld("c_identf", k.ident_f[:], k.t_ident, [("ident_f",)])
    ld("c_identb", k.ident_b[:], k.t_ident, [("ident_b",)], queue="pool")
    ld("c_gpm", k.gpm[:], k.g_pre_mix, [("gpm",)])
    ld("c_gpf", k.gpf[:], k.g_pre_ffn, [("gpf",)])
    ld("c_psc", k.psc[:], k.pscale, [("psc",)])
    ld("c_dec", k.dec[:], k.t_dec, [("dec",)])
    ld("c_mask", k.mask[:], k.t_mask, [("mask",)])
    ld("c_coef", k.coef[:], k.t_coef, [("coef",)])
    ld("c_invc", k.invc[:], k.t_invc, [("invc",)])
    S.op("dve", lambda e: e.memset(k.eps_t[:], EPS), writes=[("eps",)])

    blocks = []
    def wblk(ap2d, r0, c0, nk=KC):
        blocks.append((ap2d, r0, c0, nk))
    for c0 in (OFF_Q, OFF_Q + 512, OFF_K, OFF_K + 512):
        wblk(k.w_in, 0, c0)
    for j in range(4):
        wblk(k.w_in, 0, OFF_V + 512 * j)
    for j in range(4):
        wblk(k.w_in, 0, OFF_G + 512 * j)
    for j in range(2):
        wblk(k.w_in, 0, OFF_U + 512 * j)
    for j in range(4):
        wblk(k.w_in, 0, OFF_AP + 512 * j)
        wblk(k.w_pool_out, 0, 512 * j, 8)
        wblk(k.w_in, 0, OFF_AR + 512 * j)
        wblk(k.w_ret_out, 0, 512 * j)
    k.n_pre_wo = len(blocks)
    for j in range(4):
        wblk(k.w_o, 0, 512 * j)
    for g in range(DFF // FG):
        for j in range(FG // 512):
            wblk(k.w_up, 0, g * FG + 512 * j)
        for j in range(4):
            wblk(k.w_down, g * FG, 512 * j, FG // 128)
    k.blocks = blocks
    k.wcur = 0
    k.wloaded = 0

    def wload(j, slot_tile, key, chan):
        ap2d, r0, c0, nk = k.blocks[j]
        src = ap2d[r0:r0 + nk * 128, c0:c0 + 512].rearrange("(kc p) n -> p kc n", p=128)
        S.dma("pool", chan, lambda e: e.dma_start(out=slot_tile[:, 0:nk, :], in_=src), writes=[key])

    def wnext():
        j = k.wcur
        k.wcur += 1
        while k.wloaded < min(j + NSLOT, len(k.blocks)):
            i = k.wloaded
            wload(i, k.wslot[i % NSLOT], ("w", i % NSLOT), "w%d" % (i % NSLOT))
            k.wloaded += 1
        return k.wslot[j % NSLOT], ("w", j % NSLOT), k.blocks[j]
    k.wnext = wnext

    phases = [phase0_prepare, phase1, phase0, phase3, phase4, phase2, phase5, phase6, phase7]
    for i, ph in enumerate(phases):
        if i >= stop_after:
            break
        ph(k)
    if k.phase_toks or any(S.items[e] for e in ENGS):
        end_phase()
    es.close()
    return nc, k


def norm_to_featmajor(k, name, x_src_fn, ntile, rows_fn, g_tile, gkey, hT, hT_key_fn, hT_cols_fn,
                      x_keep=None, nbuf=2):
    S, M = k.S, k.M
    xt = [M.alloc(name + "_xt%d" % i, [128, D], F32) for i in range(nbuf)]
    xn = [M.alloc(name + "_xn%d" % i, [128, D], BF16) for i in range(nbuf)]
    junk = M.alloc(name + "_junk", [128, D], BF16)
    st_ = M.alloc(name + "_stat", [128, 2 * ntile + 2], F32)
    stat = st_[0]
    def stage_A(i):
        rows = rows_fn(i)
        b = i % nbuf
        if x_keep is None:
            xa = xt[b][0][0:rows, :]
            xkey = (name + "_xt", b)
            k.ld(name + "_x%d" % b, xa, x_src_fn(i), [xkey])
        else:
            xa, xkey = x_keep(i)
        ss = stat[0:rows, 2 * i:2 * i + 1]
        rs = stat[0:rows, 2 * i + 1:2 * i + 2]
        skey = (name + "_stat", i)
        S.op("act", _mk(lambda e, o, a, s: e.activation(out=o, in_=a, func=AF.Square, accum_out=s),
                        junk[0][0:rows, :], xa, ss),
             reads=[xkey], writes=[(name + "_junk",), skey])
        S.op("act", _mk(lambda e, s, ep: e.activation(out=s, in_=s, func=AF.Sqrt, bias=ep, scale=1.0 / D),
                        ss, k.eps_t[0:rows, :]),
             reads=[skey, ("eps",)], writes=[skey])
        S.op("dve", _mk(lambda e, o, a: e.reciprocal(o, a), rs, ss), reads=[skey], writes=[(name + "_rs", i)])
        xna = xn[b][0][0:rows, :]
        xnkey = (name + "_xn", b)
        S.op("dve", _mk(lambda e, o, a, s: e.tensor_scalar(o, a, s, None, ALU.mult), xna, xa, rs),
             reads=[xkey, (name + "_rs", i)], writes=[xnkey])

    def stage_B(i):
        rows = rows_fn(i)
        b = i % nbuf
        xnkey = (name + "_xn", b)
        c0, ncol = hT_cols_fn(i)
        for g4 in range(KC // 4):
            bank = (g4 % 2) + 2 * (i % 2)
            bkey = ("ps", bank)
            pb = k.ps[bank][:, :].bitcast(BF16)
            for j in range(4):
                kc = g4 * 4 + j
                S.op("pe", _mk(lambda e, o, a, idn: e.transpose(o, a, idn),
                               pb[:, j * 128:j * 128 + rows], xn[b][0][0:rows, kc * 128:(kc + 1) * 128],
                               k.ident_b[0:rows, 0:rows]),
                     reads=[xnkey, ("ident_b",)], writes=[bkey], counted=(j == 3))
            src = pb[:, 0:512].rearrange("p (j t) -> p j t", j=4)[:, :, 0:rows]
            dst = hT[:, g4 * 4:g4 * 4 + 4, c0:c0 + ncol]
            gb = g_tile[:, g4 * 4:g4 * 4 + 4].unsqueeze(2).to_broadcast([128, 4, rows])
            S.op("dve", _mk(lambda e, o, a, g: e.tensor_tensor(o, a, g, ALU.mult), dst, src, gb),
                 reads=[bkey, gkey], writes=[hT_key_fn(i, g4)])

    stage_A(0)
    for i in range(ntile):
        if i + 1 < ntile:
            stage_A(i + 1)
        stage_B(i)
    for t_, u in xt + xn + [junk, st_]:
        M.free(u)


def phase1(k):
    S, M = k.S, k.M
    hT, k.hT_u = M.alloc("hT", [128, KC, T], BF16)
    hTh, k.hTh_u = M.alloc("hTh", [128, KC, 16], BF16)
    k.hT, k.hTh = hT, hTh

    def rows_fn(i):
        return 16 if i == 0 else 128

    def src_fn(i):
        return k.xh if i == 0 else k.x[(i - 1) * 128:i * 128, :]

    def key_fn(i, g4):
        return ("hTh", g4) if i == 0 else ("hT", i - 1, g4)

    def run(i_list):
        pass
    class _HT:
        def __getitem__(self_, idx):
            raise NotImplementedError
    norm_to_featmajor(k, "p1h", lambda i: k.xh, 1, lambda i: 16, k.gpm, ("gpm",), hTh,
                      lambda i, g4: ("hTh", g4), lambda i: (0, 16), nbuf=1)
    norm_to_featmajor(k, "p1", lambda i: k.x[i * 128:(i + 1) * 128, :], NT, lambda i: 128, k.gpm, ("gpm",), hT,
                      lambda i, g4: ("hT", i, g4), lambda i: (i * 128, 128))
    k.dump("hT", hT[:], [128, KC, T], [("hT", i, g4) for i in range(NT) for g4 in range(4)], BF16)
    k.end_phase()


def hT_keys(t0, ntok, kc):
    return [("hT", t, kc // 4) for t in range(t0 // 128, (t0 + ntok + 127) // 128)]


def phase3(k):
    S, M = k.S, k.M
    hT = k.hT
    qT, k.qT_u = M.alloc("qT", [128, NH, T], BF16)
    kt, k.kt_u = M.alloc("kt", [128, NT, NH * DK], BF16)
    vg, k.vg_u = M.alloc("vg", [128, NH, NT * DV], BF16)
    k.qT, k.kt, k.vg = qT, kt, vg
    cs_c, u_c = M.alloc("cs_c", [128, NT, 64], F32)
    cs_s, u_s = M.alloc("cs_s", [128, NT, 64], F32)
    k.ld("cs_c", cs_c[:], k.t_cos.rearrange("(t p) f -> p t f", p=128), [("cs_c",)])
    k.ld("cs_s", cs_s[:], k.t_sin.rearrange("(t p) f -> p t f", p=128), [("cs_s",)])
    tmp = [[M.alloc("rt%d_%d" % (s, i), [128, 4, 64], F32) for i in range(6)] for s in range(2)]
    qtok = [M.alloc("qtok%d" % s, [128, 512], BF16) for s in range(3)]
    Rl, u_Rl = M.alloc("Rl", [128, NH, DV], F32)
    it = 0
    pend = []
    for blk in range(4):
        wt, wkey, _ = k.wnext()
        is_q = blk < 2
        hb = blk % 2
        for tt in range(NT):
            var = 0 if tt < 8 else 1
            s = it % 2
            bank = it % 4
            it += 1
            ps = k.ps[bank]
            pkey = ("ps", bank)
            for kc in range(KC):
                S.op("pe", _mk(lambda e, o, l, r, st, sp: e.matmul(o, lhsT=l, rhs=r, start=st, stop=sp),
                               ps[:, :], hT[:, kc, tt * 128:(tt + 1) * 128], wt[:, kc, :], kc == 0, kc == KC - 1),
                     reads=[("hT", tt, kc // 4), wkey], writes=[pkey], counted=(kc == KC - 1))
            pv = ps[:, :].rearrange("p (h f two) -> p h f two", h=4, two=2)
            xe, xo = pv[:, :, :, 0], pv[:, :, :, 1]
            cb = cs_c[:, tt, :].unsqueeze(1).to_broadcast([128, 4, 64])
            sb = cs_s[:, tt, :].unsqueeze(1).to_broadcast([128, 4, 64])
            t1, t2, t3, t4, re, ro = [tmp[s][i][0] for i in range(6)]
            tk = [("rt", s, i) for i in range(6)]
            for (o, a, b_, ky) in ((t1, xe, cb, tk[0]), (t2, xo, sb, tk[1]), (t3, xe, sb, tk[2]), (t4, xo, cb, tk[3])):
                S.op("dve", _mk(lambda e, o, a, b_: e.tensor_tensor(o[:], a, b_, ALU.mult), o, a, b_),
                     reads=[pkey, ("cs_c",), ("cs_s",)], writes=[ky])
            S.op("pool", _mk(lambda e, o, a, b_: e.tensor_tensor(o[:], a[:], b_[:], ALU.subtract), re, t1, t2),
                 reads=[tk[0], tk[1]], writes=[tk[4]])
            S.op("pool", _mk(lambda e, o, a, b_: e.tensor_tensor(o[:], a[:], b_[:], ALU.add), ro, t3, t4),
                 reads=[tk[2], tk[3]], writes=[tk[5]])
            di = (0 if is_q else 16) + var * 8 + hb * 4
            db = k.dec[:, di:di + 4].unsqueeze(2).to_broadcast([128, 4, 64])
            qs = it % 3
            if is_q:
                dst2d = qtok[qs][0][:, :]
                dkey = ("qtok", qs)
            else:
                dst2d = kt[:, tt, hb * 512:(hb + 1) * 512]
                dkey = ("kt", tt, hb)
            dv_ = dst2d.rearrange("p (h f two) -> p h f two", h=4, two=2)
            S.op("pool", _mk(lambda e, o, a, b_: e.tensor_tensor(o, a[:], b_, ALU.mult), dv_[:, :, :, 0], re, db),
                 reads=[tk[4], ("dec",)], writes=[dkey])
            S.op("pool", _mk(lambda e, o, a, b_: e.tensor_tensor(o, a[:], b_, ALU.mult), dv_[:, :, :, 1], ro, db),
                 reads=[tk[5], ("dec",)], writes=[dkey])
            if is_q:
                pend.append((qs, hb, tt, dkey, 4 + (it % 2)))
            while pend and (not is_q or len(pend) > 2 or tt == NT - 1):
                s_, hb_, tt_, dkey_, tb = pend.pop(0)
                pb = k.ps[tb][:, :].bitcast(BF16)
                tkey = ("ps", tb)
                for j in range(4):
                    S.op("pe", _mk(lambda e, o, a, idn: e.transpose(o, a, idn),
                                   pb[:, j * 128:(j + 1) * 128], qtok[s_][0][:, j * 128:(j + 1) * 128], k.ident_b[:, :]),
                         reads=[dkey_, ("ident_b",)], writes=[tkey], counted=(j == 3))
                S.op("act", _mk(lambda e, o, a: e.copy(o, a),
                                qT[:, hb_ * 4:hb_ * 4 + 4, tt_ * 128:(tt_ + 1) * 128],
                                pb[:, 0:512].rearrange("p (j t) -> p j t", j=4)),
                     reads=[tkey], writes=[("qT", hb_ * 4 + j, tt_) for j in range(4)])
    for blk in range(4):
        wt, wkey, _ = k.wnext()
        for tt in range(NT):
            bank = it % 4
            it += 1
            ps = k.ps[bank]
            pkey = ("ps", bank)
            for kc in range(KC):
                S.op("pe", _mk(lambda e, o, l, r, st, sp: e.matmul(o, lhsT=l, rhs=r, start=st, stop=sp),
                               ps[:, :], hT[:, kc, tt * 128:(tt + 1) * 128], wt[:, kc, :], kc == 0, kc == KC - 1),
                     reads=[("hT", tt, kc // 4), wkey], writes=[pkey], counted=(kc == KC - 1))
            S.op("act", _mk(lambda e, o, a: e.copy(o, a),
                            vg[:, 2 * blk:2 * blk + 2, tt * DV:(tt + 1) * DV],
                            ps[:, :].rearrange("p (h e) -> p h e", h=2)),
                 reads=[pkey], writes=[("vg", 2 * blk, tt), ("vg", 2 * blk + 1, tt)])
    k.dump("qT", qT[:], [128, NH, T], [("qT", h, tt) for h in range(NH) for tt in range(NT)], BF16)
    k.dump("kt", kt[:], [128, NT, NH * DK], [("kt", tt, hb) for tt in range(NT) for hb in range(2)], BF16)
    k.dump("vg", vg[:], [128, NH, NT * DV], [("vg", h, tt) for h in range(NH) for tt in range(NT)], BF16)
    for tt in (range(8) if "Sloc" in k.dbg else ()):
        for h in range(NH):
            bank = 4 + (h // 2) % 2 + 2 * (tt % 2)
            half = h % 2
            pkey = ("ps", bank, half)
            pd = k.ps[bank][:, half * 256:(half + 1) * 256]
            S.op("pe", _mk(lambda e, o, l, r: e.matmul(o, lhsT=l, rhs=r, start=True, stop=True),
                           pd, kt[:, tt, h * DK:(h + 1) * DK], vg[:, h, tt * DV:(tt + 1) * DV]),
                 reads=[("kt", tt, h // 4), ("vg", h, tt)], writes=[pkey])
            if tt == 0:
                S.op("dve", _mk(lambda e, o, a: e.tensor_copy(o, a), Rl[:, h, :], pd),
                     reads=[pkey], writes=[("Rl", h)])
            else:
                S.op("dve", _mk(lambda e, o, a, g: e.scalar_tensor_tensor(o, o, g, a, ALU.mult, ALU.add),
                                Rl[:, h, :], pd, G128[h]),
                     reads=[pkey, ("Rl", h)], writes=[("Rl", h)])
    for h in (range(NH) if "Sloc" in k.dbg else ()):
        S.op("act", _mk(lambda e, o, g: e.mul(o, o, g), Rl[:, h, :], G128[h]),
             reads=[("Rl", h)], writes=[("Rl", h)])
    k.dump("Sloc", Rl[:], [128, NH, DV], [("Rl", h) for h in range(NH)])
    for lst in tmp:
        for t_, u in lst:
            M.free(u)
    for t_, u in qtok:
        M.free(u)
    M.free(u_c); M.free(u_s); M.free(u_Rl)
    k.end_phase()


def _tables(c):
    inv = (1.0 / (10000.0 ** np.linspace(0.0, 1.0, DK // 2, dtype=np.float32))).astype(np.float32)
    pos = np.concatenate([c * TP + np.arange(TP), PAST + np.arange(DEC_T), PAST + np.arange(DEC_T)])
    ang = pos.astype(np.float32)[:, None] * inv[None, :]
    t_cos = np.cos(ang).astype(np.float32)
    t_sin = np.sin(ang).astype(np.float32)
    lg = np.array(LOG_GAMMA, dtype=np.float64)
    p = np.arange(128)
    dec = np.zeros((128, 2, 2, NH), dtype=np.float64)
    for var, L in enumerate((128, 64)):
        e = (p % L + 1).astype(np.float64)[:, None]
        dec[:, 0, var, :] = np.exp(e * lg[None, :])
        dec[:, 1, var, :] = np.exp(-e * lg[None, :]) * (DK ** -0.5)
    t_dec = dec.reshape(128, 32).astype(np.float32)
    j = np.arange(128)[:, None]
    i = np.arange(128)[None, :]
    maskP = (i >= j).astype(np.float32)
    maskS = ((i >= j) & ((i // 64) == (j // 64))).astype(np.float32)
    t_mask = np.concatenate([maskP, maskS], axis=1)
    coef = np.zeros((NCORES, NH), dtype=np.float64)
    for r in range(NCORES):
        if r < c:
            coef[r] = np.exp(1024.0 * (c - 1 - r) * lg) / np.exp(128.0 * lg)
    t_coef = np.broadcast_to(coef.reshape(1, 64), (128, 64)).astype(np.float32)
    invc = np.zeros((4, 16), dtype=np.float64)
    for g, w in enumerate((2, 4, 8, 16)):
        invc[g] = 1.0 / np.minimum(c * TP + np.arange(16) + 1, w)
    t_invc = np.broadcast_to(invc.reshape(1, 64), (128, 64)).astype(np.float32)
    ppos = (np.arange(NPRE * 128) - (NCORES - 1 - c) * TP).astype(np.float32)
    pang = ppos[:, None] * inv[None, :]
    t_cosp = np.cos(pang).astype(np.float32)
    t_sinp = np.sin(pang).astype(np.float32)
    return dict(t_cosp=t_cosp, t_sinp=t_sinp, t_cos=t_cos, t_sin=t_sin, t_dec=t_dec, t_mask=t_mask,
                t_coef=np.ascontiguousarray(t_coef), t_invc=np.ascontiguousarray(t_invc),
                t_ident=np.eye(128, dtype=np.float32))


def prep_inputs(x_prompt, x_sample, cache_pool, state_retention, g_pre_mix, w_in, w_pool, pool_scale,
                w_pool_out, w_ret_out, w_o, g_post_mix, g_pre_ffn, w_up, w_down, g_post_ffn):
    f = lambda a: np.ascontiguousarray(np.asarray(a, dtype=np.float32))
    shared = dict(
        w_in=f(w_in[0]), w_pool=f(w_pool[0]), w_pool_out=f(w_pool_out[0]), w_ret_out=f(w_ret_out[0]),
        w_o=f(w_o[0]), w_up=f(w_up[0]), w_down=f(w_down[0]),
        g_pre_mix=f(np.asarray(g_pre_mix[0]).reshape(KC, 128).T),
        g_pre_ffn=f(np.asarray(g_pre_ffn[0]).reshape(KC, 128).T),
        pscale=f(np.asarray(pool_scale[0]).reshape(8, 128).T),
        g_post_mix=f(np.broadcast_to(np.asarray(g_post_mix[0])[None, :], (128, D))),
        g_post_ffn=f(np.broadcast_to(np.asarray(g_post_ffn[0])[None, :], (128, D))),
    )
    xp = np.asarray(x_prompt)[0]
    xs = np.asarray(x_sample)
    maps = []
    for c in range(NCORES):
        m = dict(shared)
        m["x"] = f(np.concatenate([xp[c * TP:(c + 1) * TP], xs[2 * c], xs[2 * c + 1]], axis=0))
        m["xh"] = f(xp[c * TP - 16:c * TP]) if c > 0 else np.zeros((16, D), np.float32)
        m["cpool"] = f(np.asarray(cache_pool)[0, 2 * c:2 * c + 2])
        m["sret"] = f(np.asarray(state_retention)[0, 2 * c:2 * c + 2])
        xpv = np.zeros((NPRE * 128, D), np.float32)
        if c > 0:
            xpv[(NCORES - 1 - c) * TP:] = xp[:c * TP]
        m["xprev"] = xpv
        m.update(_tables(c))
        maps.append(m)
    return maps


_PROG = {}


def kernel(**inputs):
    if "nc" not in _PROG:
        _PROG["nc"], _PROG["k"] = build_program()
    nc = _PROG["nc"]
    maps = prep_inputs(**inputs)
    res = run_bass_kernel_spmd(nc, maps, core_ids=list(range(NCORES)))
    R = res.results
    y = [np.asarray(r["y"]) for r in R]
    y_prompt = np.concatenate([a[:TP] for a in y], axis=0)[None]
    y_sample = np.concatenate([a[TP:].reshape(2, DEC_T, D) for a in y], axis=0)
    pool_prompt = np.asarray(R[NCORES - 1]["pool_p"])[None, None]
    ret_prompt = np.asarray(R[NCORES - 1]["ret_p"])[None, None]
    pool_sample = np.concatenate([np.asarray(r["pool_s"]) for r in R], axis=0)[None]
    ret_sample = np.concatenate([np.asarray(r["ret_s"]) for r in R], axis=0)[None]
    return tuple(np.ascontiguousarray(a.astype(np.float32)) for a in
                 (y_prompt, y_sample, pool_prompt, ret_prompt, pool_sample, ret_sample))


def _mm(S, out, lhsT, rhs, start, stop, reads, writes, counted=True):
    S.op("pe", _mk(lambda e, o, l, r, st, sp: e.matmul(o, lhsT=l, rhs=r, start=st, stop=sp),
                   out, lhsT, rhs, start, stop), reads=reads, writes=writes, counted=counted)


def _tr(S, out, in_, ident, reads, writes, counted=True):
    S.op("pe", _mk(lambda e, o, a, idn: e.transpose(o, a, idn), out, in_, ident),
         reads=reads, writes=writes, counted=counted)


def phase0_prepare(k):
    S, M = k.S, k.M
    Rp, k.Rp_u = M.alloc("Rp", [128, NH, DV], F32)
    k.Rp = Rp
    wk = [k.wslot[0], k.wslot[1]]
    wkk = [("w", 0), ("w", 1)]
    wv = [M.alloc("wv%d" % j, [128, KC, 512], BF16) for j in range(4)]
    k.p0w = (wk, wkk, wv)
    xt = [M.alloc("p0x0", [128, D], F32)]
    cs = [M.alloc("p0cs%d" % i, [128, 2, 64], F32) for i in range(2)]
    k.p0pre = (xt, cs)
    k.ld("p0x0", xt[0][0][:], k.xprev[0:128, :], [("p0x", 0)])
    for pt in range(2):
        k.ld("p0cs%d" % pt, cs[pt][0][:, 0, :], k.t_cosp[pt * 128:(pt + 1) * 128, :], [("p0cs", pt)])
        k.ld("p0cs%d" % pt, cs[pt][0][:, 1, :], k.t_sinp[pt * 128:(pt + 1) * 128, :], [("p0cs", pt)])
    for kind, j in (("k", 1), ("v", 2), ("v", 3), ("k", 0), ("v", 1), ("v", 0)):
        if kind == "k":
            src = k.w_in[:, OFF_K + 512 * j:OFF_K + 512 * (j + 1)].rearrange("(kc p) n -> p kc n", p=128)
            S.dma("pool", "w%d" % j, _mk(lambda e, o, s: e.dma_start(out=o, in_=s), wk[j][:], src), writes=[wkk[j]])
        else:
            src = k.w_in[:, OFF_V + 512 * j:OFF_V + 512 * (j + 1)].rearrange("(kc p) n -> p kc n", p=128)
            tok = S.dma("pool", "wv%d" % j, _mk(lambda e, o, s: e.dma_start(out=o, in_=s), wv[j][0][:], src),
                        writes=[("wv", j)])
            k.p0_toks = getattr(k, "p0_toks", []) + [tok]


def phase0(k):
    S, M = k.S, k.M
    Rp = k.Rp
    wk, wkk, wv = k.p0w
    k.phase_toks.extend(k.p0_toks)
    xt, cs = k.p0pre
    xt = xt + [M.alloc("p0x1", [128, D], F32)]
    xn = [M.alloc("p0xn%d" % i, [128, D], BF16) for i in range(2)]
    hTt = [M.alloc("p0h%d" % i, [128, KC, 128], BF16) for i in range(2)]
    tmp = [[M.alloc("p0rt%d_%d" % (s, i), [128, 4, 64], F32) for i in range(6)] for s in range(2)]
    ktl = [M.alloc("p0k%d" % i, [128, NH * DK], BF16) for i in range(2)]
    vtl = [M.alloc("p0v%d" % i, [128, NH * DV], BF16) for i in range(2)]
    stat_, stat_u = M.alloc("p0stat", [128, 2 * NPRE], F32)
    state = {"pj": 0, "tj": 0}

    def hmin_of(pt):
        s = 7 - pt // 8
        return {1: 0, 2: 1, 3: 2, 4: 3, 5: 3, 6: 3, 7: 4}[s]
    started = set()

    def stage_A(pt):
        b = pt % 2
        xa = xt[b][0]
        xkey = ("p0x", b)
        ckey = ("p0cs", b)
        if pt >= 1:
            k.ld("p0x%d" % b, xa[:], k.xprev[pt * 128:(pt + 1) * 128, :], [xkey])
        if pt >= 2:
            k.ld("p0cs%d" % b, cs[b][0][:, 0, :], k.t_cosp[pt * 128:(pt + 1) * 128, :], [ckey])
            k.ld("p0cs%d" % b, cs[b][0][:, 1, :], k.t_sinp[pt * 128:(pt + 1) * 128, :], [ckey])
        ss = stat_[:, 2 * pt:2 * pt + 1]
        rs = stat_[:, 2 * pt + 1:2 * pt + 2]
        xna = xn[b][0]
        xnkey = ("p0xn", b)
        S.op("act", _mk(lambda e, o, a, s: e.activation(out=o, in_=a, func=AF.Square, accum_out=s), xna[:], xa[:], ss),
             reads=[xkey], writes=[xnkey, ("p0ss", pt)])
        S.op("act", _mk(lambda e, s, ep: e.activation(out=s, in_=s, func=AF.Sqrt, bias=ep, scale=1.0 / D), ss, k.eps_t[:, :]),
             reads=[("p0ss", pt), ("eps",)], writes=[("p0ss", pt)])
        S.op("dve", _mk(lambda e, o, a: e.reciprocal(o, a), rs, ss), reads=[("p0ss", pt)], writes=[("p0rs", pt)])
        S.op("dve", _mk(lambda e, o, a, s: e.tensor_scalar(o, a, s, None, ALU.mult), xna[:], xa[:], rs),
             reads=[xkey, ("p0rs", pt)], writes=[xnkey])

    def stage_B(pt):
        b = pt % 2
        xna = xn[b][0]
        xnkey = ("p0xn", b)
        hk = ("p0h", b)
        for g4 in range(4):
            bank = (0, 1, 5)[state["tj"] % 3]
            state["tj"] += 1
            bkey = ("ps", bank)
            pb = k.ps[bank][:, :].bitcast(BF16)
            for j in range(4):
                kc = g4 * 4 + j
                _tr(S, pb[:, j * 128:(j + 1) * 128], xna[:, kc * 128:(kc + 1) * 128], k.ident_b[:, :],
                    [xnkey, ("ident_b",)], [bkey], counted=(j == 3))
            srcp = pb[:, 0:512].rearrange("p (j t) -> p j t", j=4)
            gb = k.gpm[:, g4 * 4:g4 * 4 + 4].unsqueeze(2).to_broadcast([128, 4, 128])
            S.op("dve", _mk(lambda e, o, a, g: e.tensor_tensor(o, a, g, ALU.mult), hTt[b][0][:, g4 * 4:g4 * 4 + 4, :], srcp, gb),
                 reads=[bkey, ("gpm",)], writes=[(hk, g4)])

    def stage_C(pt):
        b = pt % 2
        hk = ("p0h", b)
        ckey = ("p0cs", b)
        kkey = ("p0k", b)
        vkey = ("p0v", b)
        hmin = hmin_of(pt)
        for blk in range(2):
            h0 = max(hmin - 4 * blk, 0)
            if h0 >= 4:
                continue
            nh = 4 - h0
            ncol = nh * 128
            bank = 2 + state["pj"] % 3
            state["pj"] += 1
            ps = k.ps[bank]
            pkey = ("ps", bank)
            for kc in range(KC):
                _mm(S, ps[:, 0:ncol], hTt[b][0][:, kc, :], wk[blk][:, kc, h0 * 128:512], kc == 0, kc == KC - 1,
                    [(hk, kc // 4), wkk[blk]], [pkey], counted=(kc == KC - 1))
            s = state["pj"] % 2
            pv = ps[:, 0:ncol].rearrange("p (h f two) -> p h f two", h=nh, two=2)
            xe, xo = pv[:, :, :, 0], pv[:, :, :, 1]
            cb = cs[b][0][:, 0, :].unsqueeze(1).to_broadcast([128, nh, 64])
            sb = cs[b][0][:, 1, :].unsqueeze(1).to_broadcast([128, nh, 64])
            t1, t2, t3, t4, re, ro = [tmp[s][i][0][:, 0:nh, :] for i in range(6)]
            tk = [("p0rt", s, i) for i in range(6)]
            for (o, a, b_, ky) in ((t1, xe, cb, tk[0]), (t2, xo, sb, tk[1]), (t3, xe, sb, tk[2]), (t4, xo, cb, tk[3])):
                S.op("dve", _mk(lambda e, o, a, b_: e.tensor_tensor(o, a, b_, ALU.mult), o, a, b_),
                     reads=[pkey, ckey], writes=[ky])
            S.op("pool", _mk(lambda e, o, a, b_: e.tensor_tensor(o, a, b_, ALU.subtract), re, t1, t2),
                 reads=[tk[0], tk[1]], writes=[tk[4]])
            S.op("pool", _mk(lambda e, o, a, b_: e.tensor_tensor(o, a, b_, ALU.add), ro, t3, t4),
                 reads=[tk[2], tk[3]], writes=[tk[5]])
            di = 16 + blk * 4 + h0
            db = k.dec[:, di:di + nh].unsqueeze(2).to_broadcast([128, nh, 64])
            dv_ = ktl[b][0][:, blk * 512 + h0 * 128:(blk + 1) * 512].rearrange("p (h f two) -> p h f two", h=nh, two=2)
            S.op("pool", _mk(lambda e, o, a, b_: e.tensor_tensor(o, a, b_, ALU.mult), dv_[:, :, :, 0], re, db),
                 reads=[tk[4], ("dec",)], writes=[(kkey, blk)])
            S.op("pool", _mk(lambda e, o, a, b_: e.tensor_tensor(o, a, b_, ALU.mult), dv_[:, :, :, 1], ro, db),
                 reads=[tk[5], ("dec",)], writes=[(kkey, blk)])

    def stage_Cv(pt):
        b = pt % 2
        hk = ("p0h", b)
        vkey = ("p0v", b)
        hmin = hmin_of(pt)
        for blk in range(4):
            h0 = max(hmin - 2 * blk, 0)
            if h0 >= 2:
                continue
            c0 = h0 * 256
            ncol = 512 - c0
            bank = 2 + state["pj"] % 3
            state["pj"] += 1
            ps = k.ps[bank]
            pkey = ("ps", bank)
            for kc in range(KC):
                _mm(S, ps[:, 0:ncol], hTt[b][0][:, kc, :], wv[blk][0][:, kc, c0:512], kc == 0, kc == KC - 1,
                    [(hk, kc // 4), ("wv", blk)], [pkey], counted=(kc == KC - 1))
            S.op("act", _mk(lambda e, o, a: e.copy(o, a), vtl[b][0][:, blk * 512 + c0:(blk + 1) * 512], ps[:, 0:ncol]),
                 reads=[pkey], writes=[(vkey, blk)])

    def stage_D(pt):
        b = pt % 2
        kkey = ("p0k", b)
        vkey = ("p0v", b)
        for h in range(hmin_of(pt), NH):
            bank = 6 + (h // 2) % 2
            half = h % 2
            pkey = ("ps", bank, half)
            pd = k.ps[bank][:, half * 256:(half + 1) * 256]
            _mm(S, pd, ktl[b][0][:, h * DK:(h + 1) * DK], vtl[b][0][:, h * DV:(h + 1) * DV], True, True,
                [(kkey, h // 4), (vkey, h // 2)], [pkey])
            if h not in started:
                started.add(h)
                S.op("dve", _mk(lambda e, o, a: e.tensor_copy(o, a), Rp[:, h, :], pd), reads=[pkey], writes=[("Rp", h)])
            else:
                S.op("dve", _mk(lambda e, o, a, g: e.scalar_tensor_tensor(o, o, g, a, ALU.mult, ALU.add), Rp[:, h, :], pd, G128[h]),
                     reads=[pkey, ("Rp", h)], writes=[("Rp", h)])

    stage_A(0)
    for pt in range(NPRE):
        stage_B(pt)
        stage_C(pt)
        if pt > 0:
            stage_D(pt - 1)
        if pt + 1 < NPRE:
            stage_A(pt + 1)
        stage_Cv(pt)
    stage_D(NPRE - 1)
    k.dump("Rp", Rp[:], [128, NH, DV], [("Rp", h) for h in range(NH)])
    for lst in (wv, xt, xn, hTt, cs, ktl, vtl):
        for t_, u in lst:
            M.free(u)
    for lst in tmp:
        for t_, u in lst:
            M.free(u)
    M.free(stat_u)
    k.end_phase()


def phase4(k):
    S, M = k.S, k.M
    hT, qT, kt, vg, Rp = k.hT, k.qT, k.kt, k.vg, k.Rp
    sg, sg_u = M.alloc("sg", [128, NT, 512], BF16)
    ktT = [M.alloc("ktT%d" % i, [128, T], BF16) for i in range(2)]
    go = [M.alloc("go%d" % i, [128, NT, DV], BF16) for i in range(2)]
    Sbf, Sbf_u = M.alloc("Sbf", [128, 2, DV], BF16)
    S0, S0_u = M.alloc("S0", [128, 2, 2, DV], F32)
    S0b, S0b_u = M.alloc("S0b", [128, 2, 2, DV], BF16)
    sT = [M.alloc("sT%d" % i, [128, 128], BF16) for i in range(4)]
    qA = [M.alloc("qA%d" % i, [128, 128], BF16) for i in range(2)]
    qB = [M.alloc("qB%d" % i, [128, 128], BF16) for i in range(2)]
    junk, junk_u = M.alloc("p4junk", [128, DV], BF16)
    st_, st_u = M.alloc("p4stat", [128, 2 * NT * NH], F32)
    So = [M.alloc("So%d" % i, [128, DV], F32) for i in range(4)]
    for i in range(2):
        S.op("pool", _mk(lambda e, o: e.memset(o, 0.0), qA[i][0][:, :]), writes=[("qA", i)])
        S.op("pool", _mk(lambda e, o: e.memset(o, 0.0), qB[i][0][:, :]), writes=[("qB", i)])
    nsT = 0
    nSo = 0
    nps = 0
    sgb = [(sg, sg_u), M.alloc("sg2", [128, NT, 512], BF16)]

    def gr_proj(hp_):
        wt, wkey, _ = k.wnext()
        sgt = sgb[hp_ % 2][0]
        for tt in range(NT):
            bank = tt % 2
            pkey = ("ps", bank)
            for kc in range(KC):
                _mm(S, k.ps[bank][:, :], hT[:, kc, tt * 128:(tt + 1) * 128], wt[:, kc, :], kc == 0, kc == KC - 1,
                    [("hT", tt, kc // 4), wkey], [pkey], counted=(kc == KC - 1))
            S.op("act", _mk(lambda e, o, a: e.activation(out=o, in_=a, func=AF.Silu), sgt[:, tt, :], k.ps[bank][:, :]),
                 reads=[pkey], writes=[("sg", hp_ % 2, tt)])

    gr_proj(0)
    for hp in range(4):
        sg = sgb[hp % 2][0]
        for hl in range(2):
            h = 2 * hp + hl
            for (t0, n) in ((0, 4), (4, 4), (8, 1)):
                bank = 2 + nps % 2
                nps += 1
                bkey = ("ps", bank)
                pb = k.ps[bank][:, :].bitcast(BF16)
                for j in range(n):
                    _tr(S, pb[:, j * 128:(j + 1) * 128], kt[:, t0 + j, h * DK:(h + 1) * DK], k.ident_b[:, :],
                        [("kt", t0 + j, h // 4), ("ident_b",)], [bkey], counted=(j == n - 1))
                S.op("act", _mk(lambda e, o, a: e.copy(o, a), ktT[hl][0][:, t0 * 128:(t0 + n) * 128], pb[:, 0:n * 128]),
                     reads=[bkey], writes=[("ktT", hl)])
            S.op("act", _mk(lambda e, o, a, g: e.mul(o, a, g), Sbf[:, hl, :], Rp[:, h, :], G128[h]),
                 reads=[("Rp", h)], writes=[("Sbf", hl)])
            for s_ in range(2):
                k.ld("S0_%d_%d" % (s_, hl), S0[:, s_, hl, :], k.sret[s_, h], [("S0", s_, hl)])
                S.op("act", _mk(lambda e, o, a: e.copy(o, a), S0b[:, s_, hl, :], S0[:, s_, hl, :]),
                     reads=[("S0", s_, hl)], writes=[("S0b", s_, hl)])
                S.op("act", _mk(lambda e, o, g: e.mul(o, o, g), S0[:, s_, hl, :], G64[h]),
                     reads=[("S0", s_, hl), ("S0b", s_, hl)], writes=[("S0", s_, hl)])
            S.op("pool", _mk(lambda e, o, a: e.tensor_copy(o, a), qA[hl][0][:, 0:64], qT[:, h, 1024:1088]),
                 reads=[("qT", h, 8)], writes=[("qA", hl)])
            S.op("pool", _mk(lambda e, o, a: e.tensor_copy(o, a), qB[hl][0][:, 64:128], qT[:, h, 1088:1152]),
                 reads=[("qT", h, 8)], writes=[("qB", hl)])
        ctx = {}

        def st_A(tt, hl):
            h = 2 * hp + hl
            hc = slice(h * DK, (h + 1) * DK)
            tsl = slice(tt * 128, (tt + 1) * 128)
            vsl = slice(tt * DV, (tt + 1) * DV)
            c = ctx[(tt, hl)] = {}
            if tt < 8:
                c["dkey"] = ("ps", 7, hl)
                c["pd"] = k.ps[7][:, hl * 256:(hl + 1) * 256]
                _mm(S, c["pd"], kt[:, tt, hc], vg[:, h, vsl], True, True, [("kt", tt, h // 4), ("vg", h, tt)], [c["dkey"]])
            else:
                c["dkeyA"] = ("ps", 7, hl)
                c["dkeyB"] = ("ps", 3)
                c["pdA"] = k.ps[7][:, hl * 256:(hl + 1) * 256]
                c["pdB"] = k.ps[3][:, hl * 256:(hl + 1) * 256]
                _mm(S, c["pdA"], kt[0:64, tt, hc], vg[0:64, h, vsl], True, True, [("kt", tt, h // 4), ("vg", h, tt)], [c["dkeyA"]])
                _mm(S, c["pdB"], kt[64:128, tt, hc], vg[64:128, h, vsl], True, True, [("kt", tt, h // 4), ("vg", h, tt)], [c["dkeyB"]])
            q4 = (2 * tt + hl) % 4
            skey = ("ps", 4, q4)
            pS = k.ps[4][:, q4 * 128:(q4 + 1) * 128]
            _mm(S, pS, ktT[hl][0][:, tsl], qT[:, h, tsl], True, True, [("ktT", hl), ("qT", h, tt)], [skey])
            si = (2 * tt + hl) % 4
            c["si"] = si
            mk = k.mask[:, 0:128] if tt < 8 else k.mask[:, 128:256]
            S.op("dve", _mk(lambda e, o, a, m: e.tensor_tensor(o, a, m, ALU.mult), sT[si][0][:, :], pS, mk),
                 reads=[skey, ("mask",)], writes=[("sT", si)])

        def st_B(tt, hl):
            nonlocal nSo
            h = 2 * hp + hl
            tsl = slice(tt * 128, (tt + 1) * 128)
            vsl = slice(tt * DV, (tt + 1) * DV)
            c = ctx[(tt, hl)]
            si = c["si"]
            ob = 5 if hl == 0 else 6
            oh = tt % 2
            okey = ("ps", ob, oh)
            pO = k.ps[ob][:, oh * 256:(oh + 1) * 256]
            c["okey"], c["pO"] = okey, pO
            _mm(S, pO, sT[si][0][:, :], vg[:, h, vsl], True, False, [("sT", si), ("vg", h, tt)], [okey], counted=False)
            if tt < 8:
                _mm(S, pO, qT[:, h, tsl], Sbf[:, hl, :], False, True, [("qT", h, tt), ("Sbf", hl)], [okey])
            else:
                _mm(S, pO, qA[hl][0][:, :], S0b[:, 0, hl, :], False, False, [("qA", hl), ("S0b", 0, hl)], [okey], counted=False)
                _mm(S, pO, qB[hl][0][:, :], S0b[:, 1, hl, :], False, True, [("qB", hl), ("S0b", 1, hl)], [okey])
            if tt < 8:
                S.op("dve", _mk(lambda e, o, a, g: e.scalar_tensor_tensor(o, o, g, a, ALU.mult, ALU.add), Rp[:, h, :], c["pd"], G128[h]),
                     reads=[c["dkey"], ("Rp", h)], writes=[("Rp", h)])
                if tt < 7:
                    S.op("act", _mk(lambda e, o, a, g: e.mul(o, a, g), Sbf[:, hl, :], Rp[:, h, :], G128[h]),
                         reads=[("Rp", h)], writes=[("Sbf", hl)])
                else:
                    so = So[nSo % 4]
                    sok = ("So", nSo % 4)
                    nSo += 1
                    S.op("act", _mk(lambda e, o, a, g: e.mul(o, a, g), so[0][:, :], Rp[:, h, :], G128[h]),
                         reads=[("Rp", h)], writes=[sok])
                    k.st("So%d" % (int(sok[1])), k.ret_p[h], so[0][:, :], [sok])
            else:
                for s_, (pdx, dk_) in enumerate(((c["pdA"], c["dkeyA"]), (c["pdB"], c["dkeyB"]))):
                    so = So[nSo % 4]
                    sok = ("So", nSo % 4)
                    nSo += 1
                    S.op("dve", _mk(lambda e, o, a, g, b_: e.scalar_tensor_tensor(o, a, g, b_, ALU.mult, ALU.add),
                                    so[0][:, :], pdx, G64[h], S0[:, s_, hl, :]),
                         reads=[dk_, ("S0", s_, hl)], writes=[sok])
                    k.st("So%d" % (int(sok[1])), k.ret_s[s_, h], so[0][:, :], [sok])

        def st_C(tt, hl):
            h = 2 * hp + hl
            c = ctx.pop((tt, hl))
            okey, pO = c["okey"], c["pO"]
            ssa = st_[:, h * NT + tt:h * NT + tt + 1]
            S.op("act", _mk(lambda e, o, a, s: e.activation(out=o, in_=a, func=AF.Square, accum_out=s), junk[:, :], pO, ssa),
                 reads=[okey], writes=[("p4junk",), ("p4ss", tt, h)])
            S.op("dve", _mk(lambda e, o, a, g: e.tensor_tensor(o, a, g, ALU.mult),
                            go[hl][0][:, tt, :], pO, sg[:, tt, hl * DV:(hl + 1) * DV]),
                 reads=[okey, ("p4ss", tt, h), ("sg", hp % 2, tt)], writes=[("go", hl, tt)])

        for tt in range(NT):
            for hl in range(2):
                st_A(tt, hl)
            for hl in range(2):
                st_B(tt, hl)
            for hl in range(2):
                st_C(tt, hl)
        if hp + 1 < 4:
            gr_proj(hp + 1)
        for hl in range(2):
            h = 2 * hp + hl
            ssv = st_[:, h * NT:(h + 1) * NT]
            rsv = st_[:, NH * NT + h * NT:NH * NT + (h + 1) * NT]
            S.op("act", _mk(lambda e, s, ep: e.activation(out=s, in_=s, func=AF.Sqrt, bias=ep, scale=1.0 / DV), ssv, k.eps_t[:, :]),
                 reads=[("p4ss", tt, h) for tt in range(NT)] + [("eps",)], writes=[("p4ssv", h)])
            S.op("dve", _mk(lambda e, o, a: e.reciprocal(o, a), rsv, ssv), reads=[("p4ssv", h)], writes=[("p4rs", h)])
            S.op("dve", _mk(lambda e, o, r: e.tensor_tensor(o, o, r, ALU.mult), go[hl][0][:, :, :],
                            rsv.unsqueeze(2).to_broadcast([128, NT, DV])),
                 reads=[("go", hl, tt) for tt in range(NT)] + [("p4rs", h)], writes=[("go", hl, tt) for tt in range(NT)])
        for hl in range(2):
            h = 2 * hp + hl
            for c2 in range(2):
                for (t0, n) in ((0, 4), (4, 4), (8, 1)):
                    bank = 2 + nps % 2
                    nps += 1
                    bkey = ("ps", bank)
                    pb = k.ps[bank][:, :].bitcast(BF16)
                    for j in range(n):
                        _tr(S, pb[:, j * 128:(j + 1) * 128], go[hl][0][:, t0 + j, c2 * 128:(c2 + 1) * 128], k.ident_b[:, :],
                            [("go", hl, t0 + j), ("ident_b",)], [bkey], counted=(j == n - 1))
                    S.op("act", _mk(lambda e, o, a: e.copy(o, a),
                                    vg[:, h, c2 * T + t0 * 128:c2 * T + (t0 + n) * 128], pb[:, 0:n * 128]),
                         reads=[bkey], writes=[("vg", h, t) for t in range(NT)] + [("goT", h)])
    k.dump("goT", vg[:], [128, NH, NT * DV], [("goT", h) for h in range(NH)], BF16)
    for lst in (ktT, go, sT, qA, qB, So):
        for t_, u in lst:
            M.free(u)
    for u in (sg_u, sgb[1][1], Sbf_u, S0_u, S0b_u, junk_u, st_u, k.qT_u, k.kt_u, k.Rp_u):
        M.free(u)
    k.end_phase()


UL = 1200


def phase2(k):
    S, M = k.S, k.M
    hT, hTh = k.hT, k.hTh
    pyT, k.pyT_u = M.alloc("pyT", [128, 8, T], BF16)
    k.pyT = pyT
    Uc = [M.alloc("Uc%d" % i, [128, UL], F32) for i in range(2)]
    W1, W1_u = M.alloc("pW1", [128, UL], F32)
    W2, W2_u = M.alloc("pW2", [128, UL], F32)
    dT = [M.alloc("dT%d" % i, [128, 2, T], BF16) for i in range(2)]
    cl, cl_u = M.alloc("cl", [16, 2, PW], F32)
    wp, wp_u = M.alloc("wp", [128, 4, 2, 256], BF16)
    save, save_u = M.alloc("psave", [128, 8, 3, 16], F32)
    stage, stage_u = M.alloc("pstage", [16, PW], F32)
    t16, t16_u = M.alloc("pt16", [128, 16], F32)
    for s_ in range(2):
        k.ld("cl", cl[0:LAG, s_, :], k.cpool[s_], [("cl",)])
    tok = S.dma("pool", "wp", _mk(lambda e, o, s: e.dma_start(out=o, in_=s), wp[:],
                                  k.w_pool.rearrange("g (cc p) d -> p g cc d", p=128)), writes=[("wp",)])
    k.phase_toks.append(tok)
    for i in range(2):
        S.op("pool", _mk(lambda e, o: e.memset(o, 0.0), Uc[i][0][:, :]), writes=[("Uc", i)])
    groups = [(None, 0, 16, 0), (hT, 0, 512, 16), (hT, 512, 512, 528), (hT, 1024, 64, 1056), (hT, 1088, 64, 1136)]
    mains = [(16, 1024, 0), (1056, 64, 1024), (1136, 64, 1088)]
    nb = 0
    wt = wkey = None
    for uc in range(8):
        if uc % 4 == 0:
            wt, wkey, _ = k.wnext()
        g = uc // 2
        cc = uc % 2
        ub = uc % 2
        U = Uc[ub][0]
        ukey = ("Uc", ub)
        msl = slice((uc % 4) * 128, (uc % 4 + 1) * 128)
        for (src, t0, n, c0) in groups:
            bank = nb % 4
            nb += 1
            pkey = ("ps", bank)
            for kc in range(KC):
                if src is None:
                    rhs = hTh[:, kc, 0:16]
                    rk = [("hTh", kc // 4)]
                else:
                    rhs = hT[:, kc, t0:t0 + n]
                    rk = hT_keys(t0, n, kc)
                _mm(S, k.ps[bank][:, 0:n], wt[:, kc, msl], rhs, kc == 0, kc == KC - 1, rk + [wkey], [pkey], counted=(kc == KC - 1))
            S.op("act", _mk(lambda e, o, a: e.copy(o, a), U[:, c0:c0 + n], k.ps[bank][:, 0:n]), reads=[pkey], writes=[ukey])
        for s_ in range(2):
            tkey = ("ps", 4)
            _tr(S, k.ps[4][:, s_ * 16:s_ * 16 + LAG], cl[0:LAG, s_, uc * 128:(uc + 1) * 128], k.ident_f[0:LAG, 0:LAG],
                [("cl",), ("ident_f",)], [tkey])
            c1 = 1041 + 80 * s_
            S.op("act", _mk(lambda e, o, a: e.copy(o, a), U[:, c1:c1 + LAG], k.ps[4][:, s_ * 16:s_ * 16 + LAG]),
                 reads=[tkey], writes=[ukey])
        for s3, c1 in enumerate((1025, 1105, 1185)):
            S.op("pool", _mk(lambda e, o, a: e.tensor_copy(o, a), save[:, uc, s3, 0:LAG], U[:, c1:c1 + LAG]),
                 reads=[ukey], writes=[("psave", uc)])
        cur, ckey = U, ukey
        sh = 1
        bufs = [(W1, ("pW", 1)), (W2, ("pW", 2))]
        for lvl in range(g + 1):
            dst, dkey = bufs[lvl % 2]
            lo = 2 * sh - 1
            S.op("dve", _mk(lambda e, o, a, b_: e.tensor_tensor(o, a, b_, ALU.add), dst[:, lo:UL], cur[:, lo:UL], cur[:, lo - sh:UL - sh]),
                 reads=[ckey], writes=[dkey])
            cur, ckey = dst, dkey
            sh *= 2
        w = 2 ** (g + 1)
        dt_ = dT[g % 2][0]
        dkey2 = ("dT", g % 2, cc)
        for (c0, n, t0) in mains:
            S.op("dve", _mk(lambda e, o, a, sc, b_: e.scalar_tensor_tensor(o, a, sc, b_, ALU.mult, ALU.subtract),
                            dt_[:, cc, t0:t0 + n], cur[:, c0:c0 + n], 1.0 / w, U[:, c0:c0 + n]),
                 reads=[ckey, ukey], writes=[dkey2])
        S.op("dve", _mk(lambda e, o, a, b_: e.tensor_tensor(o, a, b_, ALU.mult), t16[:, :], cur[:, 16:32], k.invc[:, g * 16:(g + 1) * 16]),
             reads=[ckey, ("invc",)], writes=[("pt16",)])
        S.op("dve", _mk(lambda e, o, a, b_: e.tensor_tensor(o, a, b_, ALU.subtract), dt_[:, cc, 0:16], t16[:, :], U[:, 16:32]),
             reads=[("pt16",), ukey], writes=[dkey2])
        if cc == 1:
            for dc in range(2):
                for (t0, n) in TBLK:
                    bank = 5 + nb % 2
                    nb += 1
                    pkey = ("ps", bank)
                    for c_ in range(2):
                        _mm(S, k.ps[bank][:, 0:n], wp[:, g, c_, dc * 128:(dc + 1) * 128], dt_[:, c_, t0:t0 + n], c_ == 0, c_ == 1,
                            [("wp",), ("dT", g % 2, c_)], [pkey], counted=(c_ == 1))
                    oc = 2 * g + dc
                    S.op("act", _mk(lambda e, o, a, s: e.activation(out=o, in_=a, func=AF.Copy, scale=s),
                                    pyT[:, oc, t0:t0 + n], k.ps[bank][:, 0:n], k.psc[:, oc:oc + 1]),
                         reads=[pkey, ("psc",)], writes=[("pyT", oc)])
    for s3 in range(3):
        for half in range(2):
            bank = 7 if half == 0 else 4
            bkey = ("ps", bank)
            for j in range(4):
                uc = half * 4 + j
                _tr(S, k.ps[bank][0:LAG, j * 128:(j + 1) * 128], save[:, uc, s3, 0:LAG], k.ident_f[:, :],
                    [("psave", uc), ("ident_f",)], [bkey], counted=(j == 3))
            S.op("act", _mk(lambda e, o, a: e.copy(o, a), stage[0:LAG, half * 512:(half + 1) * 512], k.ps[bank][0:LAG, :]),
                 reads=[bkey], writes=[("pstage",)])
        dst = k.pool_p if s3 == 0 else k.pool_s[s3 - 1]
        k.st("pstage", dst, stage[0:LAG, :], [("pstage",)])
    k.dump("pyT", pyT[:], [128, 8, T], [("pyT", oc) for oc in range(8)], BF16)
    for lst in (Uc, dT):
        for t_, u in lst:
            M.free(u)
    for u in (W1_u, W2_u, cl_u, wp_u, save_u, stage_u, t16_u, k.hTh_u):
        M.free(u)
    k.end_phase()


def goT_rhs(k, kc, t0, n):
    h, c2 = kc // 2, kc % 2
    return k.vg[:, h, c2 * T + t0:c2 * T + t0 + n]


def phase5(k):
    S, M = k.S, k.M
    hT, pyT = k.hT, k.pyT
    mT, k.mT_u = M.alloc("mT", [128, KC, T], BF16)
    k.mT = mT
    sigA, sigA_u = M.alloc("sigA", [128, 4, T], BF16)
    part1, part1_u = M.alloc("part1", [128, 4, T], F32)
    tm = [M.alloc("p5t%d" % i, [128, 512], F32) for i in range(2)]
    nb = 0
    nt = 0
    for j in range(4):
        for stage in range(4):
            wt, wkey, blk = k.wnext()
            nk = blk[3]
            for oc4 in range(4):
                msl = slice(oc4 * 128, (oc4 + 1) * 128)
                for (t0, n) in TBLK:
                    bank = nb % 6
                    nb += 1
                    pkey = ("ps", bank)
                    ps = k.ps[bank][:, 0:n]
                    for kc in range(nk):
                        if stage in (0, 2):
                            rhs, rk = hT[:, kc, t0:t0 + n], hT_keys(t0, n, kc)
                        elif stage == 1:
                            rhs, rk = pyT[:, kc, t0:t0 + n], [("pyT", kc)]
                        else:
                            rhs, rk = goT_rhs(k, kc, t0, n), [("goT", kc // 2)]
                        _mm(S, ps, wt[:, kc, msl], rhs, kc == 0, kc == nk - 1, rk + [wkey], [pkey], counted=(kc == nk - 1))
                    skey = ("sigA", oc4, t0)
                    if stage in (0, 2):
                        S.op("act", _mk(lambda e, o, a: e.activation(out=o, in_=a, func=AF.Sigmoid), sigA[:, oc4, t0:t0 + n], ps),
                             reads=[pkey], writes=[skey])
                    elif stage == 1:
                        S.op("dve", _mk(lambda e, o, a, b_: e.tensor_tensor(o, a, b_, ALU.mult), part1[:, oc4, t0:t0 + n], ps, sigA[:, oc4, t0:t0 + n]),
                             reads=[pkey, skey], writes=[("part1", oc4, t0)])
                    else:
                        tb_ = tm[nt % 2]
                        tkey = ("p5t", nt % 2)
                        nt += 1
                        S.op("dve", _mk(lambda e, o, a, b_: e.tensor_tensor(o, a, b_, ALU.mult), tb_[0][:, 0:n], ps, sigA[:, oc4, t0:t0 + n]),
                             reads=[pkey, skey], writes=[tkey])
                        S.op("pool", _mk(lambda e, o, a, b_: e.tensor_tensor(o, a, b_, ALU.add), mT[:, 4 * j + oc4, t0:t0 + n], tb_[0][:, 0:n], part1[:, oc4, t0:t0 + n]),
                             reads=[tkey, ("part1", oc4, t0)], writes=[("mT", 4 * j + oc4, t0)])
    k.dump("mT", mT[:], [128, KC, T], [("mT", oc, t0) for oc in range(KC) for (t0, n) in TBLK], BF16)
    for t_, u in tm:
        M.free(u)
    for u in (sigA_u, part1_u, k.hT_u, k.pyT_u, k.vg_u):
        M.free(u)
    k.end_phase()


def phase6(k):
    S, M = k.S, k.M
    mT = k.mT
    h2T, k.h2T_u = M.alloc("h2T", [128, KC, T], BF16)
    k.h2T = h2T
    gpo, gpo_u = M.alloc("gpo", [128, D], F32)
    k.ld("gpo", gpo[:], k.g_post_mix, [("gpo",)])
    wex = [M.alloc("woex%d" % i, [128, KC, 512], BF16) for i in range(2)]
    xt = [M.alloc("p6x%d" % i, [128, D], F32) for i in range(2)]
    x1 = [M.alloc("p6x1%d" % i, [128, D], F32) for i in range(2)]
    xn = [M.alloc("p6xn%d" % i, [128, D], BF16) for i in range(2)]
    st_, st_u = M.alloc("p6stat", [128, 8 * NT], F32)
    j0 = k.wcur
    assert j0 == k.n_pre_wo and j0 <= k.wloaded <= j0 + 2, (j0, k.wloaded, k.n_pre_wo)
    already = k.wloaded - j0
    wo = []
    for i in range(4):
        if i < 2:
            sl = (j0 + i) % NSLOT
            tile_, key, chan = k.wslot[sl], ("w", sl), "w%d" % sl
        else:
            tile_, key, chan = wex[i - 2][0], ("woex", i - 2), "woex%d" % (i - 2)
        ap2d, r0, c0, nk = k.blocks[j0 + i]
        src = ap2d[r0:r0 + nk * 128, c0:c0 + 512].rearrange("(kc p) n -> p kc n", p=128)
        if i >= already:
            tok = S.dma("pool", chan, _mk(lambda e, o, s: e.dma_start(out=o, in_=s), tile_[:, :, :], src), writes=[key])
            if i >= 2:
                k.phase_toks.append(tok)
        wo.append((tile_, key))
    k.wcur = j0 + 4
    k.wloaded = j0 + 4
    junk6, junk6_u = M.alloc("p6junk", [128, 512], BF16)

    def stage_mm(tt):
        b = tt % 2
        tsl = slice(tt * 128, (tt + 1) * 128)
        xa, xkey = xt[b][0], ("p6x", b)
        k.ld("p6x%d" % b, xa[:], k.x[tsl, :], [xkey])
        x1a, x1key = x1[b][0], ("p6x1", b)
        base = 8 * tt
        for cb in range(4):
            bank = cb
            pkey = ("ps", bank)
            csl = slice(cb * 512, (cb + 1) * 512)
            for kc in range(KC):
                _mm(S, k.ps[bank][:, :], mT[:, kc, tsl], wo[cb][0][:, kc, :], kc == 0, kc == KC - 1,
                    [("mT", kc, (tt * 128) // 512 * 512), wo[cb][1]], [pkey], counted=(kc == KC - 1))
            S.op("act", _mk(lambda e, o, a, s: e.activation(out=o, in_=a, func=AF.Square, accum_out=s),
                            junk6[:, :], k.ps[bank][:, :], st_[:, base + cb:base + cb + 1]),
                 reads=[pkey], writes=[("p6junk",), ("p6ss", tt, cb)])
            S.op("dve", _mk(lambda e, o, a, g: e.tensor_tensor(o, a, g, ALU.mult), x1a[:, csl], k.ps[bank][:, :], gpo[:, csl]),
                 reads=[pkey, ("p6ss", tt, cb), ("gpo",)], writes=[(x1key, cb)])

    def stage_post(tt):
        b = tt % 2
        tsl = slice(tt * 128, (tt + 1) * 128)
        xa, xkey = xt[b][0], ("p6x", b)
        x1a, x1key = x1[b][0], ("p6x1", b)
        xna, xnkey = xn[b][0], ("p6xn", b)
        base = 8 * tt
        x1all = [(x1key, cb) for cb in range(4)]
        ss = st_[:, base + 4:base + 5]
        rs = st_[:, base + 5:base + 6]
        S.op("dve", _mk(lambda e, o, a: e.reduce_sum(o, a, AX.X), ss, st_[:, base:base + 4]),
             reads=[("p6ss", tt, cb) for cb in range(4)], writes=[("p6s", tt)])
        S.op("act", _mk(lambda e, s, ep: e.activation(out=s, in_=s, func=AF.Sqrt, bias=ep, scale=1.0 / D), ss, k.eps_t[:, :]),
             reads=[("p6s", tt), ("eps",)], writes=[("p6s", tt)])
        S.op("dve", _mk(lambda e, o, a: e.reciprocal(o, a), rs, ss), reads=[("p6s", tt)], writes=[("p6r", tt)])
        S.op("dve", _mk(lambda e, o, s, b_: e.scalar_tensor_tensor(o, o, s, b_, ALU.mult, ALU.add), x1a[:, :], rs, xa[:, :]),
             reads=x1all + [("p6r", tt), xkey], writes=x1all + [(x1key, "f")])
        k.st("p6x1_%d" % b, k.x1_d[tsl, :], x1a[:, :], x1all + [(x1key, "f")], writes=[("x1d", tt)])
        ss2 = st_[:, base + 6:base + 7]
        rs2 = st_[:, base + 7:base + 8]
        S.op("act", _mk(lambda e, o, a, s: e.activation(out=o, in_=a, func=AF.Square, accum_out=s), xna[:, :], x1a[:, :], ss2),
             reads=x1all + [(x1key, "f")], writes=[xnkey, ("p6s2", tt)])
        S.op("act", _mk(lambda e, s, ep: e.activation(out=s, in_=s, func=AF.Sqrt, bias=ep, scale=1.0 / D), ss2, k.eps_t[:, :]),
             reads=[("p6s2", tt), ("eps",)], writes=[("p6s2", tt)])
        S.op("dve", _mk(lambda e, o, a: e.reciprocal(o, a), rs2, ss2), reads=[("p6s2", tt)], writes=[("p6r2", tt)])
        S.op("dve", _mk(lambda e, o, a, s: e.tensor_scalar(o, a, s, None, ALU.mult), xna[:, :], x1a[:, :], rs2),
             reads=x1all + [(x1key, "f"), ("p6r2", tt)], writes=[xnkey])

    def stage_post_b(tt):
        b = tt % 2
        tsl = slice(tt * 128, (tt + 1) * 128)
        xna, xnkey = xn[b][0], ("p6xn", b)
        for g4 in range(4):
            bank = 4 + (g4 % 2) + 2 * b
            bkey = ("ps", bank)
            pb = k.ps[bank][:, :].bitcast(BF16)
            for j in range(4):
                kc = g4 * 4 + j
                _tr(S, pb[:, j * 128:(j + 1) * 128], xna[:, kc * 128:(kc + 1) * 128], k.ident_b[:, :],
                    [xnkey, ("ident_b",)], [bkey], counted=(j == 3))
            gb = k.gpf[:, g4 * 4:g4 * 4 + 4].unsqueeze(2).to_broadcast([128, 4, 128])
            S.op("dve", _mk(lambda e, o, a, g: e.tensor_tensor(o, a, g, ALU.mult), h2T[:, g4 * 4:g4 * 4 + 4, tsl],
                            pb[:, 0:512].rearrange("p (j t) -> p j t", j=4), gb),
                 reads=[bkey, ("gpf",)], writes=[("h2T", tt, g4)])

    stage_mm(0)
    for tt in range(1, NT):
        stage_post(tt - 1)
        stage_mm(tt)
        stage_post_b(tt - 1)
    stage_post(NT - 1)
    stage_post_b(NT - 1)
    M.free(junk6_u)
    k.dump("h2T", h2T[:], [128, KC, T], [("h2T", tt, g4) for tt in range(NT) for g4 in range(4)], BF16)
    for lst in (wex, xt, x1, xn):
        for t_, u in lst:
            M.free(u)
    for u in (gpo_u, st_u, k.mT_u):
        M.free(u)
    k.end_phase()


def h2T_keys(t0, n, kc):
    return [("h2T", t, kc // 4) for t in range(t0 // 128, (t0 + n + 127) // 128)]


def phase7(k):
    S, M = k.S, k.M
    h2T = k.h2T
    NFC = FG // 128
    f1T, f1T_u = M.alloc("f1T", [128, NFC, T], BF16)
    facc, facc_u = M.alloc("facc", [128, NT, D], F32)
    gpo, gpo_u = M.alloc("gpo2", [128, D], F32)
    k.ld("gpo2", gpo[:], k.g_post_ffn, [("gpo2",)])
    rt = [M.alloc("p7r%d" % i, [128, 512], F32) for i in range(2)]
    x1r = [M.alloc("p7x1%d" % i, [128, D], F32) for i in range(2)]
    junk, junk_u = M.alloc("p7junk", [128, D], BF16)
    st_, st_u = M.alloc("p7stat", [128, 2 * NT], F32)
    nb = 0
    nr = 0
    ngrp = DFF // FG
    for g in range(ngrp):
        for b2 in range(FG // 512):
            wt, wkey, _ = k.wnext()
            for oc4 in range(4):
                fc = b2 * 4 + oc4
                msl = slice(oc4 * 128, (oc4 + 1) * 128)
                for (t0, n) in TBLK:
                    bank = nb % 4
                    nb += 1
                    pkey = ("ps", bank)
                    ps = k.ps[bank][:, 0:n]
                    for kc in range(KC):
                        _mm(S, ps, wt[:, kc, msl], h2T[:, kc, t0:t0 + n], kc == 0, kc == KC - 1,
                            h2T_keys(t0, n, kc) + [wkey], [pkey], counted=(kc == KC - 1))
                    r_ = rt[nr % 2]
                    rkey = ("p7r", nr % 2)
                    nr += 1
                    S.op("act", _mk(lambda e, o, a: e.activation(out=o, in_=a, func=AF.Relu), r_[0][:, 0:n], ps),
                         reads=[pkey], writes=[rkey])
                    S.op("dve", _mk(lambda e, o, a, b_: e.scalar_tensor_tensor(o, a, 0.0, b_, ALU.max, ALU.mult), f1T[:, fc, t0:t0 + n], ps, r_[0][:, 0:n]),
                         reads=[pkey, rkey], writes=[("f1T", fc, t0)])
        for cb in range(4):
            wt, wkey, blk = k.wnext()
            nk = blk[3]
            csl = slice(cb * 512, (cb + 1) * 512)
            for tt in range(NT):
                tsl = slice(tt * 128, (tt + 1) * 128)
                bank = 4 + nb % 4
                nb += 1
                pkey = ("ps", bank)
                for kc in range(nk):
                    _mm(S, k.ps[bank][:, :], f1T[:, kc, tsl], wt[:, kc, :], kc == 0, kc == nk - 1,
                        [("f1T", kc, (tt * 128) // 512 * 512), wkey], [pkey], counted=(kc == nk - 1))
                fkey = ("facc", tt, cb)
                if g == 0:
                    S.op("act", _mk(lambda e, o, a: e.copy(o, a), facc[:, tt, csl], k.ps[bank][:, :]), reads=[pkey], writes=[fkey])
                else:
                    S.op("dve", _mk(lambda e, o, a: e.tensor_tensor(o, o, a, ALU.add), facc[:, tt, csl], k.ps[bank][:, :]),
                         reads=[pkey, fkey], writes=[fkey])
    def ld_x1(t):
        bb = t % 2
        k.ld("p7x1_%d" % bb, x1r[bb][0][:], k.x1_d[t * 128:(t + 1) * 128, :], [("p7x1", bb)])

    ld_x1(0)
    ld_x1(1)
    for tt in range(NT):
        b = tt % 2
        tsl = slice(tt * 128, (tt + 1) * 128)
        fk = [("facc", tt, cb) for cb in range(4)]
        xr, xrkey = x1r[b][0], ("p7x1", b)
        ss = st_[:, 2 * tt:2 * tt + 1]
        rs = st_[:, 2 * tt + 1:2 * tt + 2]
        S.op("act", _mk(lambda e, o, a, s: e.activation(out=o, in_=a, func=AF.Square, accum_out=s), junk[:, :], facc[:, tt, :], ss),
             reads=fk, writes=[("p7junk",), ("p7s", tt)])
        S.op("act", _mk(lambda e, s, ep: e.activation(out=s, in_=s, func=AF.Sqrt, bias=ep, scale=1.0 / D), ss, k.eps_t[:, :]),
             reads=[("p7s", tt), ("eps",)], writes=[("p7s", tt)])
        S.op("dve", _mk(lambda e, o, a: e.reciprocal(o, a), rs, ss), reads=[("p7s", tt)], writes=[("p7rs", tt)])
        S.op("dve", _mk(lambda e, o, s, g_: e.scalar_tensor_tensor(o, o, s, g_, ALU.mult, ALU.mult), facc[:, tt, :], rs, gpo[:, :]),
             reads=fk + [("p7rs", tt), ("gpo2",)], writes=fk)
        S.op("pool", _mk(lambda e, o, b_: e.tensor_tensor(o, o, b_, ALU.add), facc[:, tt, :], xr[:, :]),
             reads=fk + [xrkey], writes=fk + [("yt", tt)])
        if tt + 2 < NT:
            ld_x1(tt + 2)
        k.st("y%d" % b, k.y[tsl, :], facc[:, tt, :], [("yt", tt)])
    for lst in (rt, x1r):
        for t_, u in lst:
            M.free(u)
    for u in (f1T_u, facc_u, gpo_u, junk_u, st_u, k.h2T_u):
        M.free(u)
    k.end_phase()
```
